# Optimizing a Trainium2 kernel written in Bass

```python
import jax, jax.numpy as jnp
from jax import lax
import numpy as np

D_MODEL = 1024
BATCH = 16
SEQ = 4096
DEPTH = 4

N_EVEN = (DEPTH + 1) // 2
N_ODD = DEPTH // 2
MIX_WIDTH = D_MODEL
LRU_WIDTH = MIX_WIDTH // 2
LRU_HEADS = 8
LRU_HEAD_DIM = LRU_WIDTH // LRU_HEADS
CONV_WIDTH = 4
LRU_C = 8.0
HGRN_WIDTH = MIX_WIDTH // 2
HGRN_HEAD_DIM = 128
HGRN_HEADS = HGRN_WIDTH // HGRN_HEAD_DIM
HGRN_CHUNK = 64
SGU_WIDTH = MIX_WIDTH // 2
SGU_GROUPS = 4
SGU_GROUP_DIM = SGU_WIDTH // SGU_GROUPS
SGU_CHUNK = 128
POOL_WIDTH = MIX_WIDTH // 2
POOL_WINDOWS = (2, 4, 8, 16)
POOL_GROUP_DIM = POOL_WIDTH // len(POOL_WINDOWS)
D_FF = 4 * D_MODEL
EVEN_SPLITS = (LRU_WIDTH, LRU_WIDTH, HGRN_WIDTH, HGRN_WIDTH, HGRN_WIDTH, HGRN_WIDTH)
EVEN_IN = sum(EVEN_SPLITS)
ODD_IN = 2 * SGU_WIDTH + POOL_WIDTH
EPS = 1e-6

kernel_name = "hybrid_rglru_hgrn2_sgu_pool_trunk"

F32 = jnp.float32


def rms_norm(x, gain):
    xf = x.astype(F32)
    y = xf * lax.rsqrt(jnp.mean(xf * xf, axis=-1, keepdims=True) + EPS)
    return (y * gain.astype(F32)).astype(x.dtype)


def causal_depthwise_conv(u, w, b):
    s = u.shape[1]
    up = jnp.pad(u.astype(F32), ((0, 0), (CONV_WIDTH - 1, 0), (0, 0)))
    y = b.astype(F32)
    for k in range(CONV_WIDTH):
        y = y + up[:, k:k + s] * w[k].astype(F32)
    return y


def rg_lru(u, w_a, b_a, w_x, b_x, lam):
    bsz, s, w = u.shape
    uh = u.reshape(bsz, s, LRU_HEADS, LRU_HEAD_DIM)
    r = jax.nn.sigmoid(jnp.einsum('bshi,hij->bshj', uh, w_a.astype(F32)).reshape(bsz, s, w) + b_a.astype(F32))
    i = jax.nn.sigmoid(jnp.einsum('bshi,hij->bshj', uh, w_x.astype(F32)).reshape(bsz, s, w) + b_x.astype(F32))
    log_a = -LRU_C * r * jax.nn.softplus(-lam.astype(F32))
    a = jnp.exp(log_a)
    beta = jnp.sqrt(jnp.maximum(-jnp.expm1(2.0 * log_a), 0.0))
    b_term = beta * (i * u)

    def combine(left, right):
        a_l, b_l = left
        a_r, b_r = right
        return a_l * a_r, a_r * b_l + b_r

    _, h = lax.associative_scan(combine, (a, b_term), axis=1)
    return h


def hgrn2(q, f_pre, v, g, lb, norm_gain):
    bsz, s, _ = q.shape
    nh, d, c = HGRN_HEADS, HGRN_HEAD_DIM, HGRN_CHUNK
    nc = s // c
    qf = jax.nn.silu(q.astype(F32))
    z = f_pre.astype(F32)
    lb = lb.astype(F32)
    f = lb + (1.0 - lb) * jax.nn.sigmoid(z)
    log_f = jnp.log(f)
    k = (1.0 - lb) * jax.nn.sigmoid(-z)
    vf = v.astype(F32)

    def to_chunks(t):
        return t.reshape(bsz, nc, c, nh, d).transpose(1, 0, 3, 2, 4)

    causal = jnp.tril(jnp.ones((c, c), dtype=bool))[:, :, None]
    causal_f = causal.astype(F32)

    def step(state, xs):
        qc, kc, vc, lfc = xs
        bcum = jnp.cumsum(lfc, axis=2)
        o_inter = jnp.einsum('bhtd,bhde->bhte', qc * jnp.exp(bcum), state)
        diff = bcum[:, :, :, None, :] - bcum[:, :, None, :, :]
        decay = jnp.exp(jnp.where(causal, diff, 0.0)) * causal_f
        attn = jnp.einsum('bhtd,bhsd,bhtsd->bhts', qc, kc, decay)
        o = o_inter + jnp.einsum('bhts,bhse->bhte', attn, vc)
        b_last = bcum[:, :, -1:, :]
        k_dec = kc * jnp.exp(b_last - bcum)
        new_state = jnp.exp(b_last[:, :, 0, :])[..., None] * state + jnp.einsum('bhsd,bhse->bhde', k_dec, vc)
        return new_state, o

    state0 = jnp.zeros((bsz, nh, d, d), F32)
    _, o = lax.scan(step, state0, (to_chunks(qf), to_chunks(k), to_chunks(vf), to_chunks(log_f)))
    o = o.transpose(1, 0, 3, 2, 4).reshape(bsz, s, nh, d)
    o = o * lax.rsqrt(jnp.mean(o * o, axis=-1, keepdims=True) + EPS)
    return o.reshape(bsz, s, nh * d) * norm_gain.astype(F32) * jax.nn.silu(g.astype(F32))


def spatial_gating(z, sgu_w, sgu_b):
    bsz, s, _ = z.shape
    n = s // SGU_CHUNK
    u = z[..., :SGU_WIDTH].astype(F32)
    v = z[..., SGU_WIDTH:].astype(F32).reshape(bsz, n, SGU_CHUNK, SGU_GROUPS, SGU_GROUP_DIM)
    mu = jnp.mean(v, axis=-1, keepdims=True)
    var = jnp.mean(jnp.square(v - mu), axis=-1, keepdims=True)
    vn = (v - mu) * lax.rsqrt(var + EPS)
    w = sgu_w.astype(F32) * jnp.tril(jnp.ones((SGU_CHUNK, SGU_CHUNK), F32))
    sv = jnp.einsum('gts,bnsgc->bntgc', w, vn) + sgu_b.astype(F32).T[None, None, :, :, None]
    return u * sv.reshape(bsz, s, SGU_WIDTH)


def multiscale_pool(p, pool_w, pool_scale):
    s = p.shape[1]
    pf = p.astype(F32)
    pos = jnp.arange(1, s + 1)
    outs = []
    for gi, win in enumerate(POOL_WINDOWS):
        pg = pf[..., gi * POOL_GROUP_DIM:(gi + 1) * POOL_GROUP_DIM]
        cs = jnp.cumsum(pg, axis=1)
        prev = jnp.pad(cs, ((0, 0), (win, 0), (0, 0)))[:, :s]
        count = jnp.minimum(pos, win).astype(F32)[None, :, None]
        pooled = (cs - prev) / count - pg
        outs.append(jnp.einsum('bsc,cd->bsd', pooled, pool_w[gi].astype(F32)))
    return jnp.concatenate(outs, axis=-1) * pool_scale.astype(F32)


def setup_inputs(seed: int = 0) -> dict:
    key = jax.random.key(seed)
    ks = jax.random.split(key, 24)
    nrm = lambda k, shape, scale: jax.random.normal(k, shape, F32) * scale
    a_init = jax.random.uniform(ks[9], (N_EVEN, LRU_WIDTH), F32, minval=0.9, maxval=0.999)
    s_init = a_init ** (1.0 / LRU_C)
    lam = jnp.log(s_init) - jnp.log1p(-s_init)
    return {
        "x": nrm(ks[0], (BATCH, SEQ, D_MODEL), 1.0),
        "norm_mix": 1.0 + nrm(ks[1], (DEPTH, D_MODEL), 0.02),
        "norm_ffn": 1.0 + nrm(ks[2], (DEPTH, D_MODEL), 0.02),
        "w_in_even": nrm(ks[3], (N_EVEN, D_MODEL, EVEN_IN), D_MODEL ** -0.5),
        "conv_w": nrm(ks[4], (N_EVEN, CONV_WIDTH, LRU_WIDTH), CONV_WIDTH ** -0.5),
        "conv_b": nrm(ks[5], (N_EVEN, LRU_WIDTH), 0.01),
        "lru_wa": nrm(ks[6], (N_EVEN, LRU_HEADS, LRU_HEAD_DIM, LRU_HEAD_DIM), LRU_HEAD_DIM ** -0.5),
        "lru_ba": nrm(ks[7], (N_EVEN, LRU_WIDTH), 0.01),
        "lru_wx": nrm(ks[8], (N_EVEN, LRU_HEADS, LRU_HEAD_DIM, LRU_HEAD_DIM), LRU_HEAD_DIM ** -0.5),
        "lru_bx": nrm(ks[10], (N_EVEN, LRU_WIDTH), 0.01),
        "lru_lambda": lam,
        "hgrn_lb_logits": nrm(ks[11], (N_EVEN, HGRN_WIDTH), 1.0),
        "hgrn_norm": 1.0 + nrm(ks[12], (N_EVEN, HGRN_WIDTH), 0.02),
        "w_out_even": nrm(ks[13], (N_EVEN, MIX_WIDTH, D_MODEL), MIX_WIDTH ** -0.5),
        "w_in_odd": nrm(ks[14], (N_ODD, D_MODEL, ODD_IN), D_MODEL ** -0.5),
        "sgu_w": nrm(ks[15], (N_ODD, SGU_GROUPS, SGU_CHUNK, SGU_CHUNK), SGU_CHUNK ** -0.5),
        "sgu_b": 1.0 + nrm(ks[16], (N_ODD, SGU_GROUPS, SGU_CHUNK), 0.01),
        "pool_w": nrm(ks[17], (N_ODD, len(POOL_WINDOWS), POOL_GROUP_DIM, POOL_GROUP_DIM), POOL_GROUP_DIM ** -0.5),
        "pool_scale": 1.0 + nrm(ks[18], (N_ODD, POOL_WIDTH), 0.02),
        "w_out_odd": nrm(ks[19], (N_ODD, MIX_WIDTH, D_MODEL), MIX_WIDTH ** -0.5),
        "w_ffn_in": nrm(ks[20], (DEPTH, D_MODEL, D_FF), D_MODEL ** -0.5),
        "w_ffn_out": nrm(ks[21], (DEPTH, D_FF, D_MODEL), D_FF ** -0.5),
        "norm_final": 1.0 + nrm(ks[22], (D_MODEL,), 0.02),
    }


def reference(x, norm_mix, norm_ffn, w_in_even, conv_w, conv_b, lru_wa, lru_ba, lru_wx, lru_bx,
              lru_lambda, hgrn_lb_logits, hgrn_norm, w_out_even, w_in_odd, sgu_w, sgu_b, pool_w,
              pool_scale, w_out_odd, w_ffn_in, w_ffn_out, norm_final):
    dt = x.dtype
    p_lb = jax.nn.softmax(hgrn_lb_logits.astype(F32), axis=0)
    lower_bounds = jnp.cumsum(p_lb, axis=0) - p_lb[0]
    even_cuts = [int(c) for c in np.cumsum(EVEN_SPLITS)[:-1]]
    h = x
    for layer in range(DEPTH):
        j = layer // 2
        xn = rms_norm(h, norm_mix[layer])
        if layer % 2 == 0:
            proj = xn @ w_in_even[j]
            xa, ga, q, f_pre, vi, gb = jnp.split(proj, even_cuts, axis=-1)
            ua = causal_depthwise_conv(xa, conv_w[j], conv_b[j])
            ya = rg_lru(ua, lru_wa[j], lru_ba[j], lru_wx[j], lru_bx[j], lru_lambda[j]) * jax.nn.gelu(ga.astype(F32))
            yb = hgrn2(q, f_pre, vi, gb, lower_bounds[j], hgrn_norm[j])
            mix = jnp.concatenate([ya, yb], axis=-1).astype(dt) @ w_out_even[j]
        else:
            proj = xn @ w_in_odd[j]
            zc = jax.nn.gelu(proj[..., :2 * SGU_WIDTH])
            pd = proj[..., 2 * SGU_WIDTH:]
            yc = spatial_gating(zc, sgu_w[j], sgu_b[j])
            yd = multiscale_pool(pd, pool_w[j], pool_scale[j])
            mix = jnp.concatenate([yc, yd], axis=-1).astype(dt) @ w_out_odd[j]
        h = h + mix
        hn = rms_norm(h, norm_ffn[layer])
        hidden = jnp.square(jax.nn.relu(hn @ w_ffn_in[layer]))
        h = h + hidden @ w_ffn_out[layer]
    return rms_norm(h, norm_final)
```

```python
import numpy as np
import concourse.bass as bass
import concourse.mybir as mybir
from concourse.bass_utils import run_bass_kernel_spmd

F32 = mybir.dt.float32
BF16 = mybir.dt.bfloat16
U8 = mybir.dt.uint8
AF = mybir.ActivationFunctionType
ALU = mybir.AluOpType

D = 1024
KC = 8
T = 256
EPS = 1e-6
NCORES = 8


class Op:
    __slots__ = ("eng", "fn", "deps", "signal", "sem", "count", "is_dma")

    def __init__(self, eng, fn, is_dma=False):
        self.eng = eng
        self.fn = fn
        self.deps = ()
        self.signal = False
        self.sem = None
        self.count = 0
        self.is_dma = is_dma


class Prog:
    ENGS = ("pe", "act", "dve", "pool", "sp")

    def __init__(self, nc):
        self.nc = nc
        self.ops = {e: [] for e in self.ENGS}
        self.res = {}
        self.cur_sem = {}
        self.dma_sems = {}
        self.dma_cnt = {}
        self.nsem = 0
        self.new_epoch()

    def _alloc_sem(self, name):
        self.nsem += 1
        return self.nc.alloc_semaphore(f"s{self.nsem}_{name}")

    def new_epoch(self):
        for e in ("pe", "act", "dve", "pool"):
            self.cur_sem[e] = self._alloc_sem(e)

    def _deps(self, eng, reads, writes, is_dma):
        deps = set()
        for k in reads:
            r = self.res.get(k)
            if r is not None and r[0] is not None:
                deps.add(r[0])
        for k in writes:
            r = self.res.get(k)
            if r is not None:
                if r[0] is not None:
                    deps.add(r[0])
                deps.update(r[1])
        out = []
        for d in deps:
            if eng == "pe" and d.eng == "pe" and not d.is_dma and not is_dma:
                continue
            out.append(d)
        return out

    def _update(self, o, reads, writes):
        for k in reads:
            r = self.res.get(k)
            if r is None:
                r = self.res[k] = [None, []]
            r[1].append(o)
        for k in writes:
            self.res[k] = [o, []]

    def op(self, eng, fn, reads=(), writes=()):
        o = Op(eng, fn)
        o.sem = self.cur_sem.get(eng)
        o.deps = self._deps(eng, reads, writes, False)
        for d in o.deps:
            d.signal = True
        self._update(o, reads, writes)
        self.ops[eng].append(o)
        return o

    def dma(self, queue, fn, semkey, reads=(), writes=()):
        o = Op(queue, fn, is_dma=True)
        if semkey not in self.dma_sems:
            self.dma_sems[semkey] = self._alloc_sem("dma")
            self.dma_cnt[semkey] = 0
        o.sem = self.dma_sems[semkey]
        self.dma_cnt[semkey] += 16
        o.count = self.dma_cnt[semkey]
        o.signal = True
        o.deps = self._deps(queue, reads, writes, True)
        for d in o.deps:
            d.signal = True
        self._update(o, reads, writes)
        self.ops[queue].append(o)
        return o

    def fence(self, eng, reads=(), writes=()):
        return self.op(eng, None, reads, writes)

    def barrier(self):
        lasts = []
        for e in ("pe", "act", "dve", "pool"):
            for o in reversed(self.ops[e]):
                if o.fn is not None and not o.is_dma:
                    lasts.append(o)
                    break
        for e in ("pe", "act", "dve", "pool"):
            f = Op(e, None)
            f.deps = list(lasts)
            for d in lasts:
                d.signal = True
            self.ops[e].append(f)

    def emit(self):
        for e in self.ENGS:
            cnt = {}
            for o in self.ops[e]:
                if o.is_dma:
                    continue
                if o.signal:
                    assert o.fn is not None
                    cnt[o.sem] = cnt.get(o.sem, 0) + 1
                    o.count = cnt[o.sem]
        prog = self

        def run(e, h):
            waited = {}
            for o in prog.ops[e]:
                need = {}
                for d in o.deps:
                    if d.count > need.get(d.sem, 0):
                        need[d.sem] = d.count
                for s, v in need.items():
                    if waited.get(s, 0) >= v:
                        continue
                    h.wait_ge(s, v)
                    waited[s] = v
                if o.fn is None:
                    continue
                inst = o.fn(h)
                if o.signal:
                    inst.then_inc(o.sem, 16 if o.is_dma else 1)

        with self.nc.Block() as block:
            @block.tensor
            def _(h):
                run("pe", h)

            @block.scalar
            def _(h):
                run("act", h)

            @block.vector
            def _(h):
                run("dve", h)

            @block.gpsimd
            def _(h):
                run("pool", h)

            @block.sync
            def _(h):
                run("sp", h)


PV_NMIX = 0
PV_NFFN = 32
PV_NFIN = 64
PV_EVEN = 72
PV_ODD = 160
NPV = 168
E_CONVW = 0
E_CONVB = 16
E_BA = 20
E_BX = 24
E_LAM = 28
E_HN = 32
E_LBL0 = 36
E_LBL1 = 40


def _chunks(v):
    v = np.asarray(v, np.float32)
    return np.ascontiguousarray(v.reshape(-1, 128).T)


def pack_pvec(inp):
    pv = np.zeros((128, NPV), np.float32)
    for l in range(4):
        pv[:, PV_NMIX + l * 8:PV_NMIX + l * 8 + 8] = _chunks(inp["norm_mix"][l])
        pv[:, PV_NFFN + l * 8:PV_NFFN + l * 8 + 8] = _chunks(inp["norm_ffn"][l])
    pv[:, PV_NFIN:PV_NFIN + 8] = _chunks(inp["norm_final"])
    for j in range(2):
        b = PV_EVEN + j * 44
        for k in range(4):
            pv[:, b + E_CONVW + k * 4:b + E_CONVW + k * 4 + 4] = _chunks(inp["conv_w"][j, k])
        pv[:, b + E_CONVB:b + E_CONVB + 4] = _chunks(inp["conv_b"][j])
        pv[:, b + E_BA:b + E_BA + 4] = _chunks(inp["lru_ba"][j])
        pv[:, b + E_BX:b + E_BX + 4] = _chunks(inp["lru_bx"][j])
        pv[:, b + E_LAM:b + E_LAM + 4] = _chunks(inp["lru_lambda"][j])
        pv[:, b + E_HN:b + E_HN + 4] = _chunks(inp["hgrn_norm"][j])
        pv[:, b + E_LBL0:b + E_LBL0 + 4] = _chunks(inp["hgrn_lb_logits"][0])
        pv[:, b + E_LBL1:b + E_LBL1 + 4] = _chunks(inp["hgrn_lb_logits"][1])
        pv[:, PV_ODD + j * 4:PV_ODD + j * 4 + 4] = _chunks(inp["pool_scale"][j])
    return pv


def pack_gates(inp):
    g = np.zeros((2, 2, 4, 128, 128), np.float32)
    for j in range(2):
        for ax, name in enumerate(("lru_wa", "lru_wx")):
            w = np.asarray(inp[name][j], np.float32)
            for c in range(4):
                g[j, ax, c, 0:64, 0:64] = w[2 * c]
                g[j, ax, c, 64:128, 64:128] = w[2 * c + 1]
    return g


def const_table():
    c = np.zeros((128, 64), np.float32)
    for g, win in enumerate((2, 4, 8, 16)):
        for t in range(16):
            c[:, g * 16 + t] = 1.0 / min(t + 1, win)
    return c


class Builder:
    def __init__(self, ntok, seqlen, phases, final_norm):
        self.ntok = ntok
        self.seqlen = seqlen
        self.phases = phases
        self.final_norm = final_norm
        self.ntiles = ntok // T
        self.tiles_per_seq = seqlen // T
        nc = self.nc = bass.Bass("TRN2", target_bir_lowering=False)
        self.P = Prog(nc)
        self.tilecnt = 0
        self._dram()
        self._sbuf()

    def _dram(self):
        nc = self.nc
        ei = lambda n, s: nc.dram_tensor(n, s, F32, kind="ExternalInput").ap()
        self.xT = ei("xT", [D, self.ntok])
        self.outT = nc.dram_tensor("outT", [D, self.ntok], F32, kind="ExternalOutput").ap()
        self.hbuf = nc.dram_tensor("hbuf", [D, self.ntok], F32, kind="Internal").ap()
        self.w_in_even = ei("w_in_even", [2, D, 3072])
        self.w_out_even = ei("w_out_even", [2, D, D])
        self.w_in_odd = ei("w_in_odd", [2, D, 1536])
        self.w_out_odd = ei("w_out_odd", [2, D, D])
        self.w_ffn_in = ei("w_ffn_in", [4, D, 4096])
        self.w_ffn_out = ei("w_ffn_out", [4, 4096, D])
        self.wgate = ei("wgate", [2, 2, 4, 128, 128])
        self.sguwT = ei("sguwT", [2, 4, 128, 128])
        self.sgub = ei("sgub", [2, 1, 512])
        self.poolw = ei("poolw", [2, 4, 128, 128])
        self.pvec = ei("pvec", [128, NPV])
        self.cst = ei("cst", [128, 64])

    def sb(self, name, shape, dt=F32):
        return self.nc.alloc_sbuf_tensor(name, shape, dt).ap()

    def _sbuf(self):
        nc = self.nc
        sb = self.sb
        self.arena = sb("arena", [128, 65536], BF16)
        self.hx = [sb(f"hx{i}", [128, KC, T]) for i in range(3)]
        self.sq = sb("sq", [128, KC, T], BF16)
        self.xn = sb("xn", [128, KC, T], BF16)
        self.rstd = sb("rstd", [128, T])
        self.lnt = sb("lnt", [128, T])
        self.pv = sb("pv", [128, NPV])
        self.g32 = sb("g32", [128, 72])
        self.cs = sb("cs", [128, 64])
        self.ones = sb("ones", [128, 128], BF16)
        self.ident = sb("ident", [128, 128], BF16)
        self.triu = sb("triu", [128, 128])
        self.pmask = sb("pmask", [128, 128], U8)
        self.big = sb("big", [128, 8192], BF16)
        self.mb = self.arena[:, 36864:53248]
        self.y = [sb(f"y{i}", [128, KC, T], BF16) for i in range(2)]
        pt = lambda n, s, d=F32: nc.alloc_psum_tensor(n, s, d).ap()
        self.ps_stat = pt("ps_stat", [128, 2, T])
        self.ps_pj = [pt(f"ps_pj{i}", [128, 2, T]) for i in range(2)]
        self.ps_misc = pt("ps_misc", [128, 512])
        self.ps_po = [pt(f"ps_po{i}", [128, 512]) for i in range(2)]
        self.ps_v = pt("ps_v", [128, 512])
        self.ps_ao = pt("ps_ao", [128, 2, T])

    rec = None
    SCHED_WIN = 1e-9
    PE_FILL = 0.0

    @staticmethod
    def _fd(ap):
        n = 1
        for d in ap.shape[1:]:
            n *= int(d)
        return n

    def OP(self, eng, fn, R=(), W=(), cost=0.3):
        if self.rec is not None:
            self.rec.append((0, eng, fn, tuple(R), tuple(W), cost))
            return None
        return self.P.op(eng, fn, R, W)

    def DM(self, q, fn, semkey, R=(), W=(), cost=2.0):
        if self.rec is not None:
            self.rec.append((1, q, fn, semkey, tuple(R), tuple(W), cost))
            return None
        return self.P.dma(q, fn, semkey, R, W)

    def collect(self, pieces):
        self.rec = []
        for pc in pieces:
            pc()
        r = self.rec
        self.rec = None
        return r

    def play(self, it):
        if it[0] == 0:
            self.P.op(it[1], it[2], it[3], it[4])
        else:
            self.P.dma(it[1], it[2], it[3], it[4], it[5])

    def sched_reset(self):
        self.sim_eng = {}
        self.sim_w = {}
        self.sim_r = {}

    def sched_merge(self, streams, after=None):
        heads = [0] * len(streams)
        ef, wd, rd = self.sim_eng, self.sim_w, self.sim_r
        remaining = sum(len(st) for st in streams)
        lastw = []
        for st in streams:
            d = {}
            for idx, it in enumerate(st):
                for k in (it[4] if it[0] == 0 else it[5]):
                    d[k] = idx
            lastw.append(d)
        after = after or {}
        while remaining:
            best = None
            for si, st in enumerate(streams):
                if heads[si] >= len(st):
                    continue
                it = st[heads[si]]
                eng = it[1]
                R, W = (it[3], it[4]) if it[0] == 0 else (it[4], it[5])
                blocked = False
                for sj in after.get(si, ()):
                    lw = lastw[sj]
                    hj = heads[sj]
                    for k in R + W:
                        v = lw.get(k)
                        if v is not None and v >= hj:
                            blocked = True
                            break
                    if blocked:
                        break
                if blocked:
                    continue
                t = ef.get(eng, 0.0)
                for k in R:
                    v = wd.get(k)
                    if v is not None and v > t:
                        t = v
                for k in W:
                    v = wd.get(k)
                    if v is not None and v > t:
                        t = v
                    v = rd.get(k)
                    if v is not None and v > t:
                        t = v
                if best is None or t < best[0] - self.SCHED_WIN:
                    best = (t, si)
            t, si = best
            it = streams[si][heads[si]]
            heads[si] += 1
            remaining -= 1
            eng = it[1]
            if eng == "pe" and self.PE_FILL > 0:
                gap = t - ef.get("pe", 0.0)
                if gap > 0.6:
                    ident = self.ident
                    for _ in range(int(min(gap / self.PE_FILL, 40))):
                        self.P.op("pe", lambda h: h.ldweights(ident), ["ident"], ())
            R, W = (it[3], it[4]) if it[0] == 0 else (it[4], it[5])
            cost = it[-1]
            if it[0] == 1:
                ef[eng] = t + 0.06
                fin = t + cost
            else:
                fin = t + cost
                ef[eng] = fin
            for k in R:
                if rd.get(k, 0.0) < fin:
                    rd[k] = fin
            for k in W:
                wd[k] = fin + 0.25
                rd[k] = 0.0
            self.play(it)

    def MM(self, out, lhsT, rhs, st, sp, R, W, skip=False):
        c = 0.035 + self._fd(out) / 2100.0
        if skip:
            return self.OP("pe", lambda h: h.matmul(out, lhsT, rhs, start=st, stop=sp, skip_group_check=True), R, W, c)
        return self.OP("pe", lambda h: h.matmul(out, lhsT, rhs, start=st, stop=sp), R, W, c)

    def TR(self, out, in_, R, W):
        ident = self.ident
        return self.OP("pe", lambda h: h.transpose(out, in_, ident), list(R) + ["ident"], W, 0.1)

    def _c(self, eng, out):
        fd = self._fd(out)
        if eng == "act":
            return 0.22 + fd / 1100.0
        if eng == "pool":
            return 0.6 + fd / 450.0
        return 0.1 + fd / 900.0

    def ACT(self, out, in_, func, R, W, bias=None, scale=None):
        kw = {}
        if bias is not None:
            kw["bias"] = bias
            if not isinstance(bias, (int, float)):
                R = list(R) + ["biasc"]
        if scale is not None:
            kw["scale"] = scale
        return self.OP("act", lambda h: h.activation(out, in_, func, **kw), R, W, self._c("act", out))

    def TS(self, eng, out, in0, s1, s2, op0, op1, R, W):
        if s2 is None:
            return self.OP(eng, lambda h: h.tensor_single_scalar(out, in0, s1, op0), R, W, self._c(eng, out))
        return self.OP(eng, lambda h: h.tensor_scalar(out, in0, s1, s2, op0, op1), R, W, self._c(eng, out))

    def STT(self, eng, out, in0, sc, in1, op0, op1, R, W):
        return self.OP(eng, lambda h: h.scalar_tensor_tensor(out, in0, sc, in1, op0, op1), R, W, self._c(eng, out))

    def TT(self, eng, out, in0, in1, op, R, W):
        return self.OP(eng, lambda h: h.tensor_tensor(out, in0, in1, op), R, W, self._c(eng, out))

    def CP(self, eng, out, in_, R, W):
        if eng == "act":
            return self.OP("act", lambda h: h.activation(out, in_, AF.Copy), R, W, self._c(eng, out))
        return self.OP(eng, lambda h: h.tensor_copy(out, in_), R, W, self._c(eng, out))

    def MS(self, eng, out, val, W):
        return self.OP(eng, lambda h: h.memset(out, val), (), W, self._c(eng, out))

    def SCAN(self, out, d0, d1, init, R, W):
        return self.OP("dve", lambda h: h.tensor_tensor_scan(out, d0, d1, init, ALU.mult, ALU.add), R, W, self._c("dve", out))

    def DMA(self, q, out, in_, semkey, R, W):
        return self.DM(q, lambda h: h.dma_start(out=out, in_=in_), semkey, R, W)

    def setup(self):
        P = self.P
        self.DMA("sp", self.pv, self.pvec, "c0", (), ["pv"])
        self.DMA("sp", self.cs, self.cst, "c1", (), ["cs"])
        self.TS("dve", self.g32, self.pv[:, 0:72], 32.0, None, ALU.mult, None, ["pv"], ["g32"])
        self.MS("pool", self.ones, 1.0, ["ones"])
        self.MS("pool", self.ident, 1.0, ["ident"])
        ident = self.ident
        self.OP("pool", lambda h: h.affine_select(ident, ident, [[-1, 128]], ALU.is_equal, 0.0, base=0, channel_multiplier=1),
             ["ident"], ["ident"])
        triu = self.triu
        self.MS("pool", triu, 1.0, ["triu"])
        self.OP("pool", lambda h: h.affine_select(triu, triu, [[1, 128]], ALU.is_ge, 0.0, base=0, channel_multiplier=-1),
             ["triu"], ["triu"])
        self.pm32 = self.sb("pm32", [128, 128])
        self.CP("pool", self.pm32, triu, ["triu"], ["pm32"])
        self.MS("pool", self.pm32[0:64, 64:128], 0.0, ["pm32"])
        self.CP("dve", self.pmask, self.pm32, ["pm32"], ["pmask"])
        self.pmask4 = self.sb("pmask4", [128, 4, 128], U8)
        for i_ in range(4):
            self.CP("dve", self.pmask4[:, i_, :], self.pm32, ["pm32", "pmask"], ["pmask"])
        self.MS("pool", self.segm4, 1.0, ["segm"])
        self.MS("pool", self.segm4.rearrange("p (c t) -> p c t", t=64)[:, :, 0:1], 0.0, ["segm"])

    def wkeys(self, lo, hi):
        return [("wa", g) for g in range(lo // 4096, (hi - 1) // 4096 + 1)]

    def load_w(self, off, view_shape, src, nsplit, tag):
        a, b = view_shape
        dst = self.arena[:, off:off + a * b].rearrange("p (a b) -> p a b", b=b)
        step = a // nsplit
        for i in range(nsplit):
            lo = off + i * step * b
            hi = off + (i + 1) * step * b
            self.DMA("pool", dst[:, i * step:(i + 1) * step, :], src[:, i * step:(i + 1) * step, :],
                     (tag, i), (), self.wkeys(lo, hi))
        return dst

    def hxkeys(self, slot):
        return [("hx", slot, m) for m in range(KC)]

    def load_tile(self, src, j):
        slot = self.tilecnt % 3
        self.cur_slot = slot
        srcv = src.rearrange("(k p) t -> p k t", p=128)[:, :, j * T:(j + 1) * T]
        rk = [("hd", j)] if src is self.hbuf else []
        self.DMA("sp", self.hx[slot], srcv, ("hx", slot), rk, self.hxkeys(slot))
        self.tilecnt += 1
        return slot

    def store_tile(self, dst, j, slot):
        dstv = dst.rearrange("(k p) t -> p k t", p=128)[:, :, j * T:(j + 1) * T]
        wk = [("hd", j)] if dst is self.hbuf else [("od", j)]
        o = self.DMA("sp", dstv, self.hx[slot], ("st", slot), self.hxkeys(slot), wk)
        if dst is self.outT:
            self.out_keys.append(("od", j))

    def rstd_from(self, ps, scale, R):
        self.ACT(self.lnt, ps, AF.Ln, R, ["lnt"], bias=self.epsb, scale=scale)
        self.ACT(self.rstd, self.lnt, AF.Exp, ["lnt"], ["rstd"], scale=-0.5)

    def norm(self, slot, gcol, out, outkeys, bufs=None):
        hx = self.hx[slot]
        hk = self.hxkeys(slot)
        if bufs is None:
            bufs = (self.sq, "sq", self.ps_stat[:, 0, :], "ps_stat", self.lnt, "lnt", self.rstd, "rstd")
        sq, sqk, ss, ssk, lnt, lntk, rstd, rstdk = bufs
        self.ACT(sq, hx, AF.Square, hk, [sqk])
        for k in range(KC):
            self.MM(ss, self.ones, sq[:, k, :], k == 0, k == KC - 1, [sqk, "ones"], [ssk])
        self.ACT(lnt, ss, AF.Ln, [ssk], [lntk], bias=self.eps1024)
        self.ACT(rstd, lnt, AF.Exp, [lntk], [rstdk], scale=-0.5)
        for k in range(KC):
            self.STT("dve", out[:, k, :], hx[:, k, :], gcol[:, k:k + 1], rstd, ALU.mult, ALU.mult,
                     [("hx", slot, k), rstdk, "g32"], [outkeys[k]])

    def phase_ffn(self, l, src, dst, last):
        w1 = self.load_w(0, (KC, 4096), self.w_ffn_in[l].rearrange("(k p) n -> p k n", p=128), 8, "w")
        w2 = self.load_w(32768, (32, D), self.w_ffn_out[l].rearrange("(k p) n -> p k n", p=128), 8, "w2")
        hid = self.big[:, 0:32 * T].rearrange("p (m t) -> p m t", t=T)
        r32 = [self.sb_r32a, self.sb_r32b]
        gcol = self.g32[:, PV_NFFN + l * 8:PV_NFFN + l * 8 + 8]
        fg = self.g32[:, PV_NFIN:PV_NFIN + 8]
        xns = [self.xn, self.y[1]]
        xnks = [[("xn", k) for k in range(KC)], [("xn2", k) for k in range(KC)]]
        dofin = last and self.final_norm
        n = self.ntiles
        slots = {0: self.load_tile(src, 0)}
        if n > 1:
            slots[1] = self.load_tile(src, 1)
        self.norm(slots[0], gcol, xns[0], xnks[0])

        def norm_parts(slot, out, outkeys):
            hx = self.hx[slot]
            hk = self.hxkeys(slot)
            ss = self.ps_stat[:, 0, :]

            def p1():
                self.ACT(self.sq, hx, AF.Square, hk, ["sq"])

            def p2():
                for k in range(KC):
                    self.MM(ss, self.ones, self.sq[:, k, :], k == 0, k == KC - 1, ["sq", "ones"], ["ps_stat"])

            def p3():
                self.ACT(self.lnt, ss, AF.Ln, ["ps_stat"], ["lnt"], bias=self.eps1024)
                self.ACT(self.rstd, self.lnt, AF.Exp, ["lnt"], ["rstd"], scale=-0.5)

            def p4():
                for k in range(KC):
                    self.STT("dve", out[:, k, :], hx[:, k, :], gcol[:, k:k + 1], self.rstd, ALU.mult, ALU.mult,
                             [("hx", slot, k), "rstd", "g32"], [outkeys[k]])
            return p1, p2, p3, p4

        def hidden(j, lo, hi):
            xn = xns[j % 2]
            xk = xnks[j % 2]
            for m2 in range(lo, hi):
                pj = self.ps_pj[m2 % 2]
                pk = f"pj{m2 % 2}"
                for h2 in range(2):
                    m = m2 * 2 + h2
                    for k in range(KC):
                        self.MM(pj[:, h2, :], w1[:, k, m * 128:(m + 1) * 128], xn[:, k, :], k == 0, k == KC - 1,
                                [xk[k], ("wa", k)], [pk])
                rb = r32[m2 % 2]
                rk = f"r32{m2 % 2}"
                self.ACT(rb, pj, AF.Relu, [pk], [rk])
                self.TT("pool", hid[:, 2 * m2:2 * m2 + 2, :], rb, rb, ALU.mult, [rk], [("hid", m2)])

        for j in range(n):
            slot = slots[j]
            if j + 1 < n:
                p1, p2, p3, p4 = norm_parts(slots[j + 1], xns[(j + 1) % 2], xnks[(j + 1) % 2])
            else:
                p1 = p2 = p3 = p4 = (lambda: None)
            hidden(j, 0, 4)
            p1()
            hidden(j, 4, 8)
            p2()
            p3()
            hidden(j, 8, 12)
            p4()
            hidden(j, 12, 16)
            if j + 2 < n:
                slots[j + 2] = self.load_tile(src, j + 2)
            for m in range(KC):
                po = self.ps_po[m % 2][:, 0:T]
                pok = f"po{m % 2}"
                for k in range(32):
                    self.MM(po, w2[:, k, m * 128:(m + 1) * 128], hid[:, k, :], k == 0, k == 31,
                            [("hid", k // 2), ("wa", 8 + k // 4)], [pok])
                self.TT("dve", self.hx[slot][:, m, :], po, self.hx[slot][:, m, :], ALU.add,
                        [pok, ("hx", slot, m)], [("hx", slot, m)])
            if dofin:
                self.norm(slot, fg, self.hx[slot], self.hxkeys(slot))
            self.store_tile(dst, j, slot)

    def out_proj(self, wout, slot, yb=None, yp=0):
        if yb is None:
            yb = self.y[0]
        for m in range(KC):
            po = self.ps_po[m % 2][:, 0:T]
            pok = f"po{m % 2}"
            for k in range(KC):
                self.MM(po, wout[:, k, m * 128:(m + 1) * 128], yb[:, k, :], k == 0, k == KC - 1,
                        [("y", yp, k), ("wa", 6 + k // 4)], [pok])
            self.TT("dve", self.hx[slot][:, m, :], po, self.hx[slot][:, m, :], ALU.add,
                    [pok, ("hx", slot, m)], [("hx", slot, m)])

    def pipeline(self, src, dst, A, B):
        n = self.ntiles
        self.sched_reset()
        slots = {0: self.load_tile(src, 0)}
        if n > 1:
            slots[1] = self.load_tile(src, 1)
        st0 = [self.collect(pl) for pl in A(0, slots[0])]
        self.sched_merge(st0, {i: list(range(i)) for i in range(1, len(st0))})
        for j in range(n):
            if j + 2 < n:
                slots[j + 2] = self.load_tile(src, j + 2)
            streams = [self.collect(B(j, slots[j]))]
            if j + 1 < n:
                streams += [self.collect(pl) for pl in A(j + 1, slots[j + 1])]
            self.sched_merge(streams, {i: list(range(1, i)) for i in range(2, len(streams))})
            self.store_tile(dst, j, slots[j])

    def abuf(self, off, n, dt=BF16):
        if dt == F32:
            return self.arena[:, off:off + 2 * n].bitcast(F32), off + 2 * n
        return self.arena[:, off:off + n], off + n

    def phase_odd(self, l, src, dst):
        j_ = l // 2
        P = self.P
        win = self.load_w(0, (KC, 1536), self.w_in_odd[j_].rearrange("(k p) n -> p k n", p=128), 8, "w")
        wout = self.load_w(24576, (KC, D), self.w_out_odd[j_].rearrange("(k p) n -> p k n", p=128), 2, "wo")
        off = 33792
        wsg32, off = self.abuf(off, 512, F32)
        wsg32 = wsg32.rearrange("p (g t) -> p g t", t=128)
        self.DMA("sp", wsg32, self.sguwT[j_].rearrange("g s t -> s g t"), "c2", (), ["wsg32"])
        wsg = self.arena[:, 32768:32768 + 512].rearrange("p (g t) -> p g t", t=128)
        self.TT("dve", wsg, wsg32, self.triu.unsqueeze(1).to_broadcast([128, 4, 128]), ALU.mult,
                ["wsg32", "triu"], [("wa", 8)])
        wpl = self.arena[:, 32768 + 512:32768 + 1024].rearrange("p (g t) -> p g t", t=128)
        self.DMA("pool", wpl, self.poolw[j_].rearrange("g c d -> c g d"), "c3", (), [("wa", 8)])
        sbb, off = self.abuf(off, 512, F32)
        self.DMA("sp", sbb, self.sgub[j_].partition_broadcast(128), "c4", (), ["sbb"])
        gcol = self.g32[:, PV_NMIX + l * 8:PV_NMIX + l * 8 + 8]
        psc = self.pv[:, PV_ODD + j_ * 4:PV_ODD + j_ * 4 + 4]
        xnk = [("xn", k) for k in range(KC)]
        ug, vn, pp = [], [], []
        for p in range(2):
            a, off = self.abuf(off, 4 * T)
            ug.append(a.rearrange("p (c t) -> p c t", t=T))
            a, off = self.abuf(off, 1024)
            vn.append(a.rearrange("p (s c) -> p s c", c=512))
            a, off = self.abuf(off, 4 * (16 + T), F32)
            pp.append(a.rearrange("p (g t) -> p g t", t=16 + T))
        vgs = []
        for i_ in range(2):
            a, off = self.abuf(off, 512, F32)
            vgs.append(a)
        mv8 = self.sb_mv8
        pooled, off = self.abuf(off, 4 * T)
        pooled = pooled.rearrange("p (g t) -> p g t", t=T)
        ws = []
        for i in range(2):
            a, off = self.abuf(off, 4 * (T + 16), F32)
            ws.append(a.rearrange("p (g t) -> p g t", t=T + 16))
        svt, off = self.abuf(off, 512, F32)
        svt = svt.rearrange("p (g t) -> p g t", t=128)
        assert off <= 65536
        st6 = self.sb_st6
        mv = self.sb_mv
        L = T + 16
        inw = lambda k: [("wa", (k * 1536) // 4096), ("wa", (k * 1536 + 1535) // 4096)]

        def A(j, slot):
            p = j % 2
            first = (j % self.tiles_per_seq) == 0
            pcs = []

            def a_norm():
                if first:
                    self.MS("pool", pp[p][:, :, 0:16], 0.0, [("pp", p)])
                else:
                    self.CP("pool", pp[p][:, :, 0:16], pp[1 - p][:, :, T:T + 16], [("pp", 1 - p)], [("pp", p)])
                self.norm(slot, gcol, self.xn, xnk)
            pcs.append(a_norm)

            def pair(m0, evac, i):
                def f():
                    pj = self.ps_pj[i]
                    pk = f"pj{i}"
                    for h2 in range(2):
                        m = m0 + h2
                        for k in range(KC):
                            self.MM(pj[:, h2, :], win[:, k, m * 128:(m + 1) * 128], self.xn[:, k, :], k == 0, k == KC - 1,
                                    [("xn", k)] + inw(k), [pk])
                    evac(pj, pk)
                return f
            for m2 in range(2):
                pcs.append(pair(2 * m2, lambda pj, pk, m2=m2: self.ACT(ug[p][:, 2 * m2:2 * m2 + 2, :], pj, AF.Gelu_apprx_tanh,
                                                                        [pk], [("ug", p)]), m2))
            for m2 in range(2):
                pcs.append(pair(8 + 2 * m2, lambda pj, pk, m2=m2: self.CP("act", pp[p][:, 2 * m2:2 * m2 + 2, 16:16 + T], pj,
                                                                            [pk], [("pp", p)]), m2))

            def vsub(sub):
                def f():
                    for k in range(KC):
                        self.MM(self.ps_v, self.xn[:, k, sub * 128:(sub + 1) * 128], win[:, k, 512:1024], k == 0, k == KC - 1,
                                [("xn", k)] + inw(k), ["ps_v"])
                    self.ACT(vgs[sub], self.ps_v, AF.Gelu_apprx_tanh, ["ps_v"], [("vg", sub)])
                    for g in range(4):
                        vgg = vgs[sub][:, g * 128:(g + 1) * 128]
                        self.OP("dve", (lambda h, o=st6[:, g, :], i=vgg: h.bn_stats(o, i)), [("vg", sub)], [("st6", g)])
                        self.OP("dve", (lambda h, o=mv8[:, sub * 4 + g, :], i=st6[:, g, :]: h.bn_aggr(o, i)), [("st6", g)], ["mv"])
                return f
            pcs.append(vsub(0))
            pcs.append(vsub(1))

            def vfin():
                self.ACT(self.sb_l8, mv8[:, :, 1], AF.Ln, ["mv"], ["l4"], bias=self.epsb)
                self.ACT(self.sb_r8, self.sb_l8, AF.Exp, ["l4"], ["r4"], scale=-0.5)
                for sub in range(2):
                    for g in range(4):
                        self.TS("dve", vn[p][:, sub, g * 128:(g + 1) * 128], vgs[sub][:, g * 128:(g + 1) * 128],
                                mv8[:, sub * 4 + g, 0:1], self.sb_r8[:, sub * 4 + g:sub * 4 + g + 1], ALU.subtract, ALU.mult,
                                [("vg", sub), "mv", "r4"], [("vn", p)])
            pcs.append(vfin)
            return [pcs]

        def B(j, slot):
            p = j % 2
            first = (j % self.tiles_per_seq) == 0
            pcs = []

            def sgu(sub):
                def f():
                    svp = self.ps_ao.rearrange("p a t -> p (a t)").rearrange("p (g t) -> p g t", t=128)
                    for g in range(4):
                        self.MM(svp[:, g, :], vn[p][:, sub, g * 128:(g + 1) * 128], wsg[:, g, :], True, True,
                                [("vn", p), ("wa", 8)], ["ps_ao"])
                    self.TT("dve", svt, svp, sbb.rearrange("p (g t) -> p g t", t=128), ALU.add, ["ps_ao", "sbb"], ["svt"])
                    self.TT("dve", self.y[0][:, 0:4, sub * 128:(sub + 1) * 128], svt, ug[p][:, :, sub * 128:(sub + 1) * 128],
                            ALU.mult, ["svt", ("ug", p)], [("y", 0, 0), ("y", 0, 1), ("y", 0, 2), ("y", 0, 3)])
                return f
            pcs.append(sgu(0))
            pcs.append(sgu(1))

            def pool_():
                a = pp[p]
                self.TT("dve", ws[0][:, :, 1:L], a[:, :, 1:L], a[:, :, 0:L - 1], ALU.add, [("pp", p)], ["ws0"])
                self.TT("dve", ws[1][:, 1:4, 3:L], ws[0][:, 1:4, 3:L], ws[0][:, 1:4, 1:L - 2], ALU.add, ["ws0"], ["ws1"])
                self.TT("dve", ws[0][:, 2:4, 7:L], ws[1][:, 2:4, 7:L], ws[1][:, 2:4, 3:L - 4], ALU.add, ["ws1", "ws0"], ["ws0b"])
                self.TT("dve", ws[1][:, 3, 15:L], ws[0][:, 3, 15:L], ws[0][:, 3, 7:L - 8], ALU.add, ["ws0b", "ws1"], ["ws1b"])
                finals = [(ws[0], ["ws0"]), (ws[1], ["ws1"]), (ws[0], ["ws0b"]), (ws[1], ["ws1b"])]
                for g in range(4):
                    win_ = 2 << g
                    cur, ck = finals[g]
                    cur = cur[:, g, :]
                    if first:
                        self.TS("dve", svt.rearrange("p g t -> p (g t)")[:, 0:T], cur[:, 16:L], 1.0 / win_, None, ALU.mult, None,
                                ck, ["svt"])
                        self.TT("dve", svt.rearrange("p g t -> p (g t)")[:, 0:16], cur[:, 16:32], self.cs[:, g * 16:(g + 1) * 16],
                                ALU.mult, ck + ["cs", "svt"], ["svt"])
                        self.TT("dve", pooled[:, g, :], svt.rearrange("p g t -> p (g t)")[:, 0:T], a[:, g, 16:L], ALU.subtract,
                                ["svt", ("pp", p)], [("pooled", g)])
                    else:
                        self.STT("dve", pooled[:, g, :], cur[:, 16:L], 1.0 / win_, a[:, g, 16:L], ALU.mult, ALU.subtract,
                                 ck + [("pp", p)], [("pooled", g)])
            pcs.append(pool_)

            def pd_(g2):
                def f():
                    pd = self.ps_v.rearrange("p (a t) -> p a t", t=T)
                    for h2 in range(2):
                        g = 2 * g2 + h2
                        self.MM(pd[:, h2, :], wpl[:, g, :], pooled[:, g, :], True, True, [("pooled", g), ("wa", 8)], ["ps_v"])
                    for h2 in range(2):
                        g = 2 * g2 + h2
                        self.OP("act", (lambda h, o=self.y[0][:, 4 + g, :], i_=pd[:, h2, :], s_=psc[:, g:g + 1]: h.mul(o, i_, s_)),
                             ["ps_v", "pv"], [("y", 0, 4 + g)])
                return f
            pcs.append(pd_(0))
            pcs.append(pd_(1))
            pcs.append(lambda: self.out_proj(wout, slot))
            return pcs

        self.pipeline(src, dst, A, B)

    def even_consts(self, j_):
        b = PV_EVEN + j_ * 44
        pv = self.pv
        ec = self.sb_ec
        k = f"ec"
        self.ACT(ec[:, 0:4], pv[:, b + E_LAM:b + E_LAM + 4], AF.Exp, ["pv"], [k], scale=-1.0)
        self.ACT(ec[:, 4:8], ec[:, 0:4], AF.Ln, [k], [k], bias=self.oneb)
        self.TS("dve", ec[:, 8:12], ec[:, 4:8], -4.0, None, ALU.mult, None, [k], [k])
        self.TS("dve", ec[:, 12:16], ec[:, 4:8], -8.0, None, ALU.mult, None, [k], [k])
        self.TS("dve", ec[:, 16:20], pv[:, b + E_BA:b + E_BA + 4], 0.5, None, ALU.mult, None, ["pv", k], [k])
        self.TS("dve", ec[:, 20:24], pv[:, b + E_BX:b + E_BX + 4], 0.5, None, ALU.mult, None, ["pv", k], [k])
        if j_ == 0:
            self.MS("dve", ec[:, 24:28], 0.0, [k])
        else:
            self.ACT(ec[:, 28:32], pv[:, b + E_LBL0:b + E_LBL0 + 4], AF.Exp, ["pv", k], [k])
            self.ACT(ec[:, 32:36], pv[:, b + E_LBL1:b + E_LBL1 + 4], AF.Exp, ["pv", k], [k])
            self.TT("dve", ec[:, 28:32], ec[:, 28:32], ec[:, 32:36], ALU.add, [k], [k])
            self.OP("dve", lambda h: h.reciprocal(ec[:, 28:32], ec[:, 28:32]), [k], [k])
            self.TT("dve", ec[:, 24:28], ec[:, 32:36], ec[:, 28:32], ALU.mult, [k], [k])
        self.TS("dve", ec[:, 28:32], ec[:, 24:28], 0.5, 0.5, ALU.mult, ALU.add, [k], [k])
        self.TS("dve", ec[:, 32:36], ec[:, 24:28], -0.5, 0.5, ALU.mult, ALU.add, [k], [k])
        self.TS("dve", ec[:, 36:40], ec[:, 32:36], -1.0, None, ALU.mult, None, [k], [k])

    def phase_even(self, l, src, dst):
        j_ = l // 2
        P = self.P
        win = self.load_w(0, (KC, 3072), self.w_in_even[j_].rearrange("(k p) n -> p k n", p=128), 8, "w")
        wout = self.load_w(24576, (KC, D), self.w_out_even[j_].rearrange("(k p) n -> p k n", p=128), 2, "wo")
        wg = self.arena[:, 32768:32768 + 1024].rearrange("p (a c t) -> p a c t", a=2, c=4)
        self.DMA("pool", wg, self.wgate[j_].rearrange("a c i o -> i a c o"), "c3", (), [("wa", 8)])
        self.even_consts(j_)
        ec = self.sb_ec
        pvb = PV_EVEN + j_ * 44
        pv = self.pv
        gcol = self.g32[:, PV_NMIX + l * 8:PV_NMIX + l * 8 + 8]
        xnk = [("xn", k) for k in range(KC)]
        off = 33792
        c4 = lambda a: a.rearrange("p (c t) -> p c t", t=T)
        gg, qf, sg, tz, vt, xa = [], [], [], [], [], []
        for p in range(2):
            a, off = self.abuf(off, 4 * T); gg.append(c4(a))
            a, off = self.abuf(off, 4 * T); qf.append(c4(a))
            a, off = self.abuf(off, 4 * T); sg.append(c4(a))
            a, off = self.abuf(off, 4 * T, F32); tz.append(c4(a))
            a, off = self.abuf(off, 1024); vt.append(a.rearrange("p (s c) -> p s c", c=512))
            a, off = self.abuf(off, 4 * (4 + T), F32); xa.append(a.rearrange("p (c t) -> p c t", t=4 + T)[:, :, 1:4 + T])
        ua, off = self.abuf(off, 4 * T, F32); ua = c4(ua)
        tr, off = self.abuf(off, 4 * T, F32); tr = c4(tr)
        ti, off = self.abuf(off, 4 * T, F32); ti = c4(ti)
        aa, off = self.abuf(off, 4 * T, F32); aa = c4(aa)
        a2, off = self.abuf(off, 4 * T, F32); a2 = c4(a2)
        uab, off = self.abuf(off, 4 * T); uab = c4(uab)
        kk, off = self.abuf(off, 4 * T, F32); kk = c4(kk)
        e1, off = self.abuf(off, 4 * T); e1 = c4(e1)
        assert off <= 65536, off
        boff = [0]

        def bbuf(n, dt=BF16):
            if dt == F32:
                r = self.big[:, boff[0]:boff[0] + 2 * n].bitcast(F32)
                boff[0] += 2 * n
            else:
                r = self.big[:, boff[0]:boff[0] + n]
                boff[0] += n
            return r
        logf = c4(bbuf(4 * T, F32))
        bc = c4(bbuf(4 * T, F32))
        osb = c4(bbuf(4 * T, F32))
        qt = c4(bbuf(4 * T))
        kt = c4(bbuf(4 * T))
        assert boff[0] <= 8192
        S = self.sb_S
        St = self.sb_St
        Sb = self.sb_Sb
        ktT = self.sb_ktT
        Asb = self.sb_Asb
        sc = self.sb_sc
        esc = self.sb_esc
        hst = self.sb_hst
        osq = self.sb_osq
        rs2 = self.sb_rs2
        self.MS("pool", Asb, 0.0, [("Asb", 0), ("Asb", 1)])
        inw = lambda k, m: [("wa", (k * 3072 + m * 128) // 4096)]
        Skeys = [("S", h_) for h_ in range(4)]
        uak = [("ua", c) for c in range(4)]

        def A(j, slot):
            p = j % 2
            first = (j % self.tiles_per_seq) == 0
            yb = self.y[p]
            pcs = []

            def a_norm():
                if first:
                    self.MS("pool", xa[p][:, :, 0:3], 0.0, [("xa", p)])
                else:
                    self.CP("pool", xa[p][:, :, 0:3], xa[1 - p][:, :, T:T + 3], [("xa", 1 - p)], [("xa", p)])
                self.norm(slot, gcol, self.xn, xnk)
            pcs.append(a_norm)
            cnt = [0]

            def pair(m0, evac):
                def f():
                    i = cnt[0] % 2
                    cnt[0] += 1
                    pj = self.ps_pj[i]
                    pk = f"pj{i}"
                    for h2 in range(2):
                        m = m0 + h2
                        for k in range(KC):
                            self.MM(pj[:, h2, :], win[:, k, m * 128:(m + 1) * 128], self.xn[:, k, :], k == 0, k == KC - 1,
                                    [("xn", k)] + inw(k, m), [pk])
                    evac(pj, pk)
                return f

            def lru1():
                if first:
                    self.MS("dve", hst, 0.0, ["hst"])
                for c in range(4):
                    cw = lambda k_: pv[:, pvb + E_CONVW + k_ * 4 + c:pvb + E_CONVW + k_ * 4 + c + 1]
                    self.TS("dve", ua[:, c, :], xa[p][:, c, 0:T], cw(0), pv[:, pvb + E_CONVB + c:pvb + E_CONVB + c + 1],
                            ALU.mult, ALU.add, [("xa", p), "pv"], [("ua", c)])
                    for k_ in range(1, 4):
                        self.STT("dve", ua[:, c, :], xa[p][:, c, k_:k_ + T], cw(k_), ua[:, c, :], ALU.mult, ALU.add,
                                 [("xa", p), "pv", ("ua", c)], [("ua", c)])
                self.CP("act", uab, ua, uak, ["uab"])

            def lru2():
                for c in range(4):
                    pj = self.ps_v.rearrange("p (a t) -> p a t", t=T)
                    pk = "ps_v"
                    self.MM(pj[:, 0, :], wg[:, 0, c, :], uab[:, c, :], True, True, ["uab", ("wa", 8)], [pk])
                    self.MM(pj[:, 1, :], wg[:, 1, c, :], uab[:, c, :], True, True, ["uab", ("wa", 8)], [pk])
                    self.ACT(tr[:, c, :], pj[:, 0, :], AF.Tanh, [pk, "ec"], ["tr"], bias=ec[:, 16 + c:17 + c], scale=0.5)
                    self.ACT(ti[:, c, :], pj[:, 1, :], AF.Tanh, [pk, "ec"], ["ti"], bias=ec[:, 20 + c:21 + c], scale=0.5)
                for c in range(4):
                    self.ACT(aa[:, c, :], tr[:, c, :], AF.Exp, ["tr", "ec"], ["aa"], bias=ec[:, 8 + c:9 + c], scale=ec[:, 8 + c:9 + c])
                    self.ACT(a2[:, c, :], tr[:, c, :], AF.Exp, ["tr", "ec"], ["a2"], bias=ec[:, 12 + c:13 + c], scale=ec[:, 12 + c:13 + c])
                self.TS("dve", a2, a2, 1.0, -1.0, ALU.min, ALU.mult, ["a2"], ["a2"])
                self.ACT(a2, a2, AF.Ln, ["a2"], ["a2"], bias=self.oneb)
                self.ACT(a2, a2, AF.Exp, ["a2"], ["a2"], scale=0.5)
                self.STT("dve", ti, ti, 1.0, ua, ALU.add, ALU.mult, ["ti"] + uak, ["ti"])
                self.STT("dve", ti, ti, 0.5, a2, ALU.mult, ALU.mult, ["ti", "a2"], ["ti"])

            def lru3():
                for c in range(4):
                    self.SCAN(tr[:, c, :], aa[:, c, :], ti[:, c, :], hst[:, c:c + 1], ["aa", "ti", "hst", "tr"], ["tr"])
                self.CP("dve", hst, tr[:, :, T - 1], ["tr"], ["hst"])
                self.TT("dve", yb[:, 0:4, :], tr, gg[p], ALU.mult, ["tr", ("gg", p)], [("y", p, c) for c in range(4)])

            for c2 in range(2):
                pcs.append(pair(2 * c2, lambda pj, pk, c2=c2: self.CP("act", xa[p][:, 2 * c2:2 * c2 + 2, 3:3 + T], pj, [pk], [("xa", p)])))
            for c2 in range(2):
                pcs.append(pair(4 + 2 * c2, lambda pj, pk, c2=c2: self.ACT(gg[p][:, 2 * c2:2 * c2 + 2, :], pj, AF.Gelu_apprx_tanh, [pk], [("gg", p)])))
            for c2 in range(2):
                pcs.append(pair(12 + 2 * c2, lambda pj, pk, c2=c2: self.ACT(tz[p][:, 2 * c2:2 * c2 + 2, :], pj, AF.Tanh, [pk], [("tz", p)], scale=0.5)))
            for c2 in range(2):
                pcs.append(pair(8 + 2 * c2, lambda pj, pk, c2=c2: self.ACT(qf[p][:, 2 * c2:2 * c2 + 2, :], pj, AF.Silu, [pk], [("qf", p)])))
            for c2 in range(2):
                pcs.append(pair(20 + 2 * c2, lambda pj, pk, c2=c2: self.ACT(sg[p][:, 2 * c2:2 * c2 + 2, :], pj, AF.Silu, [pk], [("sg", p)])))

            def vsub(sub):
                def f():
                    i = cnt[0] % 2
                    cnt[0] += 1
                    pjv = self.ps_pj[i].rearrange("p a t -> p (a t)")
                    pk = f"pj{i}"
                    for k in range(KC):
                        lo = k * 3072 + 2048
                        self.MM(pjv, self.xn[:, k, sub * 128:(sub + 1) * 128], win[:, k, 2048:2560], k == 0, k == KC - 1,
                                [("xn", k), ("wa", lo // 4096), ("wa", (lo + 511) // 4096)], [pk])
                    self.CP("dve", vt[p][:, sub, :], pjv, [pk], [("vt", p)])
                return f
            pcs.append(vsub(0))
            pcs.append(vsub(1))
            return [pcs, [lru1, lru2, lru3]]

        def B(j, slot):
            p = j % 2
            first = (j % self.tiles_per_seq) == 0
            yb = self.y[p]
            pcs = []

            def hg1():
                if first:
                    self.MS("dve", S, 0.0, Skeys)
                for hd in range(4):
                    self.ACT(logf[:, hd, :], tz[p][:, hd, :], AF.Ln, [("tz", p), "ec"], ["logf"],
                             bias=ec[:, 28 + hd:29 + hd], scale=ec[:, 32 + hd:33 + hd])
                for hd in range(4):
                    self.TS("dve", kk[:, hd, :], tz[p][:, hd, :], ec[:, 36 + hd:37 + hd], ec[:, 32 + hd:33 + hd], ALU.mult, ALU.add,
                            [("tz", p), "ec"], ["kk"])
                flat = lambda a: a.rearrange("p c t -> p (c t)")
                self.SCAN(flat(bc), self.segm4, flat(logf), 0.0, ["segm", "logf"], ["bc"])
                bcv = flat(bc).rearrange("p (c t) -> p c t", t=64)
                self.TT("dve", flat(logf).rearrange("p (c t) -> p c t", t=64), bcv, bcv[:, :, 31:32].to_broadcast([128, 16, 64]),
                        ALU.subtract, ["bc", "logf"], ["logf"])
                self.CP("dve", sc[:, 0, :], bcv[:, :, 31], ["bc"], ["sc"])
                self.CP("dve", sc[:, 1, :], bcv[:, :, 63], ["bc", "sc"], ["sc"])
                self.TT("dve", sc[:, 2, :], sc[:, 1, :], sc[:, 0, :], ALU.subtract, ["sc"], ["sc"])
                self.ACT(e1, logf, AF.Exp, ["logf"], ["e1"])
                self.ACT(logf, logf, AF.Exp, ["logf"], ["logf"], scale=-1.0)
                self.ACT(esc, sc, AF.Exp, ["sc"], ["esc"])
                self.TT("dve", qt, qf[p], e1, ALU.mult, [("qf", p), "e1"], ["qt"])
                self.TT("dve", kt, kk, logf, ALU.mult, ["kk", "logf"], ["kt"])
            pcs.append(hg1)

            def hg2all():
                ktp = self.ps_misc.bitcast(BF16).rearrange("p (h r t) -> p h r t", h=4, r=2)
                for hd in range(4):
                    for pr in range(2):
                        self.TR(ktp[:, hd, pr, :], kt[:, hd, pr * 128:(pr + 1) * 128], ["kt"], ["ps_misc"])
                self.CP("act", ktT, ktp, ["ps_misc"], ["ktT"])
                banks = [(self.ps_ao.rearrange("p a t -> p (a t)"), "ps_ao"), (self.ps_po[1], "po1")]
                for hp in range(2):
                    bk, bkey = banks[hp]
                    aps = bk.rearrange("p (h r t) -> p h r t", h=2, r=2)
                    for hh in range(2):
                        hd = 2 * hp + hh
                        for pr in range(2):
                            self.MM(aps[:, hh, pr, :], kt[:, hd, pr * 128:(pr + 1) * 128], qt[:, hd, pr * 128:(pr + 1) * 128],
                                    True, True, ["kt", "qt"], [bkey])
                    self.OP("dve", (lambda h, o=Asb[:, 2 * hp:2 * hp + 2].rearrange("p h r t -> p (h r) t"), m_=self.pmask4,
                                    d_=aps.rearrange("p h r t -> p (h r) t"): h.copy_predicated(o, m_, d_)),
                            [bkey, "pmask", ("Asb", hp)], [("Asb", hp)], 0.7)
                obs = [bk.rearrange("p (h t) -> p h t", h=2) for bk, _ in banks]
                firstmm = [True, True]

                def omm(hp, out, lhsT, rhs, R):
                    st = firstmm[hp]
                    firstmm[hp] = False
                    self.OP("pe", (lambda h, o=out, l_=lhsT, r_=rhs, st=st: h.matmul(o, l_, r_, start=st, stop=False, skip_group_check=True)),
                            R, [banks[hp][1]], 0.035 + self._fd(out) / 2100.0)
                ups = [self.ps_misc[:, hd * 128:(hd + 1) * 128] for hd in range(4)]
                for ch in range(4):
                    pr, half = ch // 2, ch % 2
                    for hd in range(4):
                        hp, hh = hd // 2, hd % 2
                        ei = hd * 4 + ch
                        if half == 0:
                            omm(hp, obs[hp][:, hh, pr * 128:(pr + 1) * 128], vt[p][:, pr, hd * 128:(hd + 1) * 128], Asb[:, hd, pr, :],
                                [("vt", p), ("Asb", hp)])
                        if ch == 0:
                            self.TS("dve", Sb[:, hd, :], S[:, hd, :], esc[:, 0, ei:ei + 1], None, ALU.mult, None,
                                    [("S", hd), "esc"], [("Sb", hd)])
                        omm(hp, obs[hp][:, hh, ch * 64:(ch + 1) * 64], Sb[:, hd, :], qt[:, hd, ch * 64:(ch + 1) * 64], [("Sb", hd), "qt"])
                    for hd in range(4):
                        self.MM(ups[hd], ktT[half * 64:(half + 1) * 64, hd, pr, :],
                                vt[p][half * 64:(half + 1) * 64, pr, hd * 128:(hd + 1) * 128],
                                True, True, ["ktT", ("vt", p)], ["ps_misc"])
                    for hd in range(4):
                        ei = hd * 4 + ch
                        self.TS("dve", St[:, hd, :], S[:, hd, :], esc[:, 1, ei:ei + 1], None, ALU.mult, None,
                                [("S", hd), "esc"], [("St", hd)])
                        self.STT("dve", S[:, hd, :], ups[hd], esc[:, 2, ei:ei + 1], St[:, hd, :], ALU.mult, ALU.add,
                                 ["ps_misc", ("St", hd), "esc"], [("S", hd)])
                        if ch < 3:
                            self.TS("dve", Sb[:, hd, :], S[:, hd, :], esc[:, 0, ei + 1:ei + 2], None, ALU.mult, None,
                                    [("S", hd), "esc"], [("Sb", hd)])
                osp = self.ps_misc.rearrange("p (h t) -> p h t", h=2)
                for hp in range(2):
                    h0 = 2 * hp
                    bkey = banks[hp][1]
                    self.ACT(osq, obs[hp], AF.Square, [bkey], ["osq"])
                    self.CP("act", osb[:, h0:h0 + 2, :], obs[hp], [bkey], [("osb", hp)])
                    for hh in range(2):
                        self.MM(osp[:, hh, :], self.ones, osq[:, hh, :], True, True, ["osq", "ones"], ["ps_misc"])
                    self.ACT(rs2, osp, AF.Ln, ["ps_misc"], ["rs2"], bias=self.epsb, scale=1.0 / 128.0)
                    self.ACT(rs2, rs2, AF.Exp, ["rs2"], ["rs2"], scale=-0.5)
                    for hh in range(2):
                        hd = h0 + hh
                        self.STT("dve", osb[:, hd, :], osb[:, hd, :], pv[:, pvb + E_HN + hd:pvb + E_HN + hd + 1], rs2[:, hh, :],
                                 ALU.mult, ALU.mult, [("osb", hp), "rs2", "pv"], [("osb", hp)])
            pcs.append(hg2all)

            def fin():
                self.TT("dve", yb[:, 4:8, :], osb, sg[p], ALU.mult, [("osb", 0), ("osb", 1), ("sg", p)],
                        [("y", p, 4 + h_) for h_ in range(4)])
                self.out_proj(wout, slot, yb, p)
            pcs.append(fin)
            return pcs

        self.pipeline(src, dst, A, B)

    def build(self):
        sb = self.sb
        yf = self.y[0].rearrange("p k t -> p (k t)")
        self.sb_r32a = yf[:, 0:1024].bitcast(F32).rearrange("p (a t) -> p a t", t=T)
        self.sb_r32b = yf[:, 1024:2048].bitcast(F32).rearrange("p (a t) -> p a t", t=T)
        self.sb_st6 = sb("st6", [128, 4, 6])
        self.sb_mv = sb("mv", [128, 4, 2])
        self.sb_l8 = sb("l8", [128, 8])
        self.sb_r8 = sb("r8", [128, 8])
        self.sb_mv8 = sb("mv8", [128, 8, 2])
        self.sb_ec = sb("ec", [128, 40])
        self.sb_hst = sb("hst", [128, 4])
        self.sb_Sb = sb("Sb", [128, 4, 128], BF16)
        self.sb_ktT = sb("ktT", [128, 4, 2, 128], BF16)
        self.sb_Asb = sb("Asb", [128, 4, 2, 128], BF16)
        self.sb_sc = sb("sc", [128, 3, 16])
        self.sb_esc = sb("esc", [128, 3, 16])
        self.sb_S = sb("S", [128, 4, 128])
        self.sb_St = sb("St", [128, 4, 128])
        self.sb_osq = sb("osq", [128, 2, T], BF16)
        self.sb_rs2 = sb("rs2", [128, 2, T])
        self.segm4 = sb("segm4", [128, 4 * T])
        self.sb_bias = sb("biasc", [128, 4])
        self.ps_u = [self.ps_misc[:, 0:128], self.ps_misc[:, 128:256]]
        self.ps_ktT = self.ps_misc[:, 256:384].bitcast(BF16).rearrange("p (r t) -> p r t", t=128)
        self.out_keys = []
        self.MS("dve", self.sb_bias[:, 0:1], EPS, ["biasc"])
        self.MS("dve", self.sb_bias[:, 1:2], 1.0, ["biasc"])
        self.MS("dve", self.sb_bias[:, 2:3], 1024.0 * EPS, ["biasc"])
        self.epsb = self.sb_bias[:, 0:1]
        self.oneb = self.sb_bias[:, 1:2]
        self.eps1024 = self.sb_bias[:, 2:3]
        self.setup()
        n = len(self.phases)
        for i, (kind, l) in enumerate(self.phases):
            src = self.xT if i == 0 else self.hbuf
            dst = self.outT if i == n - 1 else self.hbuf
            if i > 0:
                self.P.barrier()
                self.P.new_epoch()
            if kind == "ffn":
                self.phase_ffn(l, src, dst, last=(i == n - 1))
            elif l % 2 == 0:
                self.phase_even(l, src, dst)
            else:
                self.phase_odd(l, src, dst)
        self.P.fence("sp", (), self.out_keys)
        with self.nc.allow_low_precision("bf16 matmul operands, fp32 accumulation"):
            self.P.emit()
        return self.nc


ALL_PHASES = [("mix", 0), ("ffn", 0), ("mix", 1), ("ffn", 1), ("mix", 2), ("ffn", 2), ("mix", 3), ("ffn", 3)]


def prep_shared(inp):
    f = lambda a: np.ascontiguousarray(np.asarray(a, np.float32))
    return {
        "w_in_even": f(inp["w_in_even"]), "w_out_even": f(inp["w_out_even"]),
        "w_in_odd": f(inp["w_in_odd"]), "w_out_odd": f(inp["w_out_odd"]),
        "w_ffn_in": f(inp["w_ffn_in"]), "w_ffn_out": f(inp["w_ffn_out"]),
        "wgate": pack_gates(inp),
        "sguwT": f(np.transpose(np.asarray(inp["sgu_w"], np.float32), (0, 1, 3, 2))),
        "sgub": f(np.asarray(inp["sgu_b"], np.float32).reshape(2, 1, 512)),
        "poolw": f(inp["pool_w"]),
        "pvec": pack_pvec(inp),
        "cst": const_table(),
    }


def run_phases(xT_list, shared, phases, final_norm, seqlen):
    ntok = xT_list[0].shape[1]
    b = Builder(ntok, seqlen, phases, final_norm)
    nc = b.build()
    in_maps = []
    for xT in xT_list:
        m = dict(shared)
        m["xT"] = np.ascontiguousarray(xT, dtype=np.float32)
        in_maps.append(m)
    res = run_bass_kernel_spmd(nc, in_maps, core_ids=list(range(len(xT_list))))
    return [r["outT"] for r in res.results]


def kernel(**inputs):
    x = np.asarray(inputs["x"], np.float32)
    B, S, _ = x.shape
    per = B // NCORES
    shared = prep_shared(inputs)
    xT_list = [np.ascontiguousarray(x[c * per:(c + 1) * per].reshape(per * S, D).T) for c in range(NCORES)]
    outs = run_phases(xT_list, shared, ALL_PHASES, True, S)
    out = np.empty((B, S, D), np.float32)
    for c in range(NCORES):
        out[c * per:(c + 1) * per] = outs[c].T.reshape(per, S, D)
    return out
```

```python
import numpy as np
import concourse.bass as bass
import concourse.mybir as mybir
from concourse.bass_utils import run_bass_kernel_spmd

F32 = mybir.dt.float32
BF16 = mybir.dt.bfloat16
U8 = mybir.dt.uint8
AF = mybir.ActivationFunctionType
ALU = mybir.AluOpType

D = 1024
KC = 8
T = 256
EPS = 1e-6
NCORES = 8


class Op:
    __slots__ = ("eng", "fn", "deps", "signal", "sem", "count", "is_dma")

    def __init__(self, eng, fn, is_dma=False):
        self.eng = eng
        self.fn = fn
        self.deps = ()
        self.signal = False
        self.sem = None
        self.count = 0
        self.is_dma = is_dma


class Prog:
    ENGS = ("pe", "act", "dve", "pool", "sp")

    def __init__(self, nc):
        self.nc = nc
        self.ops = {e: [] for e in self.ENGS}
        self.res = {}
        self.cur_sem = {}
        self.dma_sems = {}
        self.dma_cnt = {}
        self.nsem = 0
        self.new_epoch()

    def _alloc_sem(self, name):
        self.nsem += 1
        return self.nc.alloc_semaphore(f"s{self.nsem}_{name}")

    def new_epoch(self):
        for e in ("pe", "act", "dve", "pool"):
            self.cur_sem[e] = self._alloc_sem(e)

    def _deps(self, eng, reads, writes, is_dma):
        deps = set()
        for k in reads:
            r = self.res.get(k)
            if r is not None and r[0] is not None:
                deps.add(r[0])
        for k in writes:
            r = self.res.get(k)
            if r is not None:
                if r[0] is not None:
                    deps.add(r[0])
                deps.update(r[1])
        out = []
        for d in deps:
            if eng == "pe" and d.eng == "pe" and not d.is_dma and not is_dma:
                continue
            out.append(d)
        return out

    def _update(self, o, reads, writes):
        for k in reads:
            r = self.res.get(k)
            if r is None:
                r = self.res[k] = [None, []]
            r[1].append(o)
        for k in writes:
            self.res[k] = [o, []]

    def op(self, eng, fn, reads=(), writes=()):
        o = Op(eng, fn)
        o.sem = self.cur_sem.get(eng)
        o.deps = self._deps(eng, reads, writes, False)
        for d in o.deps:
            d.signal = True
        self._update(o, reads, writes)
        self.ops[eng].append(o)
        return o

    def dma(self, queue, fn, semkey, reads=(), writes=()):
        o = Op(queue, fn, is_dma=True)
        if semkey not in self.dma_sems:
            self.dma_sems[semkey] = self._alloc_sem("dma")
            self.dma_cnt[semkey] = 0
        o.sem = self.dma_sems[semkey]
        self.dma_cnt[semkey] += 16
        o.count = self.dma_cnt[semkey]
        o.signal = True
        o.deps = self._deps(queue, reads, writes, True)
        for d in o.deps:
            d.signal = True
        self._update(o, reads, writes)
        self.ops[queue].append(o)
        return o

    def fence(self, eng, reads=(), writes=()):
        return self.op(eng, None, reads, writes)

    def barrier(self):
        lasts = []
        for e in ("pe", "act", "dve", "pool"):
            for o in reversed(self.ops[e]):
                if o.fn is not None and not o.is_dma:
                    lasts.append(o)
                    break
        for e in ("pe", "act", "dve", "pool"):
            f = Op(e, None)
            f.deps = list(lasts)
            for d in lasts:
                d.signal = True
            self.ops[e].append(f)

    def emit(self):
        for e in self.ENGS:
            cnt = {}
            for o in self.ops[e]:
                if o.is_dma:
                    continue
                if o.signal:
                    assert o.fn is not None
                    cnt[o.sem] = cnt.get(o.sem, 0) + 1
                    o.count = cnt[o.sem]
        prog = self

        def run(e, h):
            waited = {}
            for o in prog.ops[e]:
                need = {}
                for d in o.deps:
                    if d.count > need.get(d.sem, 0):
                        need[d.sem] = d.count
                for s, v in need.items():
                    if waited.get(s, 0) >= v:
                        continue
                    h.wait_ge(s, v)
                    waited[s] = v
                if o.fn is None:
                    continue
                inst = o.fn(h)
                if o.signal:
                    inst.then_inc(o.sem, 16 if o.is_dma else 1)

        with self.nc.Block() as block:
            @block.tensor
            def _(h):
                run("pe", h)

            @block.scalar
            def _(h):
                run("act", h)

            @block.vector
            def _(h):
                run("dve", h)

            @block.gpsimd
            def _(h):
                run("pool", h)

            @block.sync
            def _(h):
                run("sp", h)


PV_NMIX = 0
PV_NFFN = 32
PV_NFIN = 64
PV_EVEN = 72
PV_ODD = 160
NPV = 168
E_CONVW = 0
E_CONVB = 16
E_BA = 20
E_BX = 24
E_LAM = 28
E_HN = 32
E_LBL0 = 36
E_LBL1 = 40


def _chunks(v):
    v = np.asarray(v, np.float32)
    return np.ascontiguousarray(v.reshape(-1, 128).T)


def pack_pvec(inp):
    pv = np.zeros((128, NPV), np.float32)
    for l in range(4):
        pv[:, PV_NMIX + l * 8:PV_NMIX + l * 8 + 8] = _chunks(inp["norm_mix"][l])
        pv[:, PV_NFFN + l * 8:PV_NFFN + l * 8 + 8] = _chunks(inp["norm_ffn"][l])
    pv[:, PV_NFIN:PV_NFIN + 8] = _chunks(inp["norm_final"])
    for j in range(2):
        b = PV_EVEN + j * 44
        for k in range(4):
            pv[:, b + E_CONVW + k * 4:b + E_CONVW + k * 4 + 4] = _chunks(inp["conv_w"][j, k])
        pv[:, b + E_CONVB:b + E_CONVB + 4] = _chunks(inp["conv_b"][j])
        pv[:, b + E_BA:b + E_BA + 4] = _chunks(inp["lru_ba"][j])
        pv[:, b + E_BX:b + E_BX + 4] = _chunks(inp["lru_bx"][j])
        pv[:, b + E_LAM:b + E_LAM + 4] = _chunks(inp["lru_lambda"][j])
        pv[:, b + E_HN:b + E_HN + 4] = _chunks(inp["hgrn_norm"][j])
        pv[:, b + E_LBL0:b + E_LBL0 + 4] = _chunks(inp["hgrn_lb_logits"][0])
        pv[:, b + E_LBL1:b + E_LBL1 + 4] = _chunks(inp["hgrn_lb_logits"][1])
        pv[:, PV_ODD + j * 4:PV_ODD + j * 4 + 4] = _chunks(inp["pool_scale"][j])
    return pv


def pack_gates(inp):
    g = np.zeros((2, 2, 4, 128, 128), np.float32)
    for j in range(2):
        for ax, name in enumerate(("lru_wa", "lru_wx")):
            w = np.asarray(inp[name][j], np.float32)
            for c in range(4):
                g[j, ax, c, 0:64, 0:64] = w[2 * c]
                g[j, ax, c, 64:128, 64:128] = w[2 * c + 1]
    return g


def const_table():
    c = np.zeros((128, 64), np.float32)
    for g, win in enumerate((2, 4, 8, 16)):
        for t in range(16):
            c[:, g * 16 + t] = 1.0 / min(t + 1, win)
    return c


class Builder:
    def __init__(self, ntok, seqlen, phases, final_norm):
        self.ntok = ntok
        self.seqlen = seqlen
        self.phases = phases
        self.final_norm = final_norm
        self.ntiles = ntok // T
        self.tiles_per_seq = seqlen // T
        nc = self.nc = bass.Bass("TRN2", target_bir_lowering=False)
        self.P = Prog(nc)
        self.tilecnt = 0
        self._dram()
        self._sbuf()

    def _dram(self):
        nc = self.nc
        ei = lambda n, s: nc.dram_tensor(n, s, F32, kind="ExternalInput").ap()
        self.xT = ei("xT", [D, self.ntok])
        self.outT = nc.dram_tensor("outT", [D, self.ntok], F32, kind="ExternalOutput").ap()
        self.hbuf = nc.dram_tensor("hbuf", [D, self.ntok], F32, kind="Internal").ap()
        self.w_in_even = ei("w_in_even", [2, D, 3072])
        self.w_out_even = ei("w_out_even", [2, D, D])
        self.w_in_odd = ei("w_in_odd", [2, D, 1536])
        self.w_out_odd = ei("w_out_odd", [2, D, D])
        self.w_ffn_in = ei("w_ffn_in", [4, D, 4096])
        self.w_ffn_out = ei("w_ffn_out", [4, 4096, D])
        self.wgate = ei("wgate", [2, 2, 4, 128, 128])
        self.sguwT = ei("sguwT", [2, 4, 128, 128])
        self.sgub = ei("sgub", [2, 1, 512])
        self.poolw = ei("poolw", [2, 4, 128, 128])
        self.pvec = ei("pvec", [128, NPV])
        self.cst = ei("cst", [128, 64])

    def sb(self, name, shape, dt=F32):
        return self.nc.alloc_sbuf_tensor(name, shape, dt).ap()

    def _sbuf(self):
        nc = self.nc
        sb = self.sb
        self.arena = sb("arena", [128, 65536], BF16)
        self.hx = [sb(f"hx{i}", [128, KC, T]) for i in range(3)]
        self.sq = sb("sq", [128, KC, T], BF16)
        self.xn = sb("xn", [128, KC, T], BF16)
        self.rstd = sb("rstd", [128, T])
        self.lnt = sb("lnt", [128, T])
        self.pv = sb("pv", [128, NPV])
        self.g32 = sb("g32", [128, 72])
        self.cs = sb("cs", [128, 64])
        self.ones = sb("ones", [128, 128], BF16)
        self.ident = sb("ident", [128, 128], BF16)
        self.triu = sb("triu", [128, 128])
        self.pmask = sb("pmask", [128, 128], U8)
        self.big = sb("big", [128, 8192], BF16)
        self.mb = self.arena[:, 36864:53248]
        self.y = [sb(f"y{i}", [128, KC, T], BF16) for i in range(2)]
        pt = lambda n, s, d=F32: nc.alloc_psum_tensor(n, s, d).ap()
        self.ps_stat = pt("ps_stat", [128, 2, T])
        self.ps_pj = [pt(f"ps_pj{i}", [128, 2, T]) for i in range(2)]
        self.ps_misc = pt("ps_misc", [128, 512])
        self.ps_po = [pt(f"ps_po{i}", [128, 512]) for i in range(2)]
        self.ps_v = pt("ps_v", [128, 512])
        self.ps_ao = pt("ps_ao", [128, 2, T])

    rec = None
    SCHED_WIN = 1e-9

    @staticmethod
    def _fd(ap):
        n = 1
        for d in ap.shape[1:]:
            n *= int(d)
        return n

    def OP(self, eng, fn, R=(), W=(), cost=0.3):
        if self.rec is not None:
            self.rec.append((0, eng, fn, tuple(R), tuple(W), cost))
            return None
        return self.P.op(eng, fn, R, W)

    def DM(self, q, fn, semkey, R=(), W=(), cost=2.0):
        if self.rec is not None:
            self.rec.append((1, q, fn, semkey, tuple(R), tuple(W), cost))
            return None
        return self.P.dma(q, fn, semkey, R, W)

    def collect(self, pieces):
        self.rec = []
        for pc in pieces:
            pc()
        r = self.rec
        self.rec = None
        return r

    def play(self, it):
        if it[0] == 0:
            self.P.op(it[1], it[2], it[3], it[4])
        else:
            self.P.dma(it[1], it[2], it[3], it[4], it[5])

    def sched_reset(self):
        self.sim_eng = {}
        self.sim_w = {}
        self.sim_r = {}

    def sched_merge(self, streams, after=None):
        heads = [0] * len(streams)
        ef, wd, rd = self.sim_eng, self.sim_w, self.sim_r
        remaining = sum(len(st) for st in streams)
        lastw = []
        for st in streams:
            d = {}
            for idx, it in enumerate(st):
                for k in (it[4] if it[0] == 0 else it[5]):
                    d[k] = idx
            lastw.append(d)
        after = after or {}
        while remaining:
            best = None
            for si, st in enumerate(streams):
                if heads[si] >= len(st):
                    continue
                it = st[heads[si]]
                eng = it[1]
                R, W = (it[3], it[4]) if it[0] == 0 else (it[4], it[5])
                blocked = False
                for sj in after.get(si, ()):
                    lw = lastw[sj]
                    hj = heads[sj]
                    for k in R + W:
                        v = lw.get(k)
                        if v is not None and v >= hj:
                            blocked = True
                            break
                    if blocked:
                        break
                if blocked:
                    continue
                t = ef.get(eng, 0.0)
                for k in R:
                    v = wd.get(k)
                    if v is not None and v > t:
                        t = v
                for k in W:
                    v = wd.get(k)
                    if v is not None and v > t:
                        t = v
                    v = rd.get(k)
                    if v is not None and v > t:
                        t = v
                if best is None or t < best[0] - self.SCHED_WIN:
                    best = (t, si)
            t, si = best
            it = streams[si][heads[si]]
            heads[si] += 1
            remaining -= 1
            eng = it[1]
            R, W = (it[3], it[4]) if it[0] == 0 else (it[4], it[5])
            cost = it[-1]
            if it[0] == 1:
                ef[eng] = t + 0.06
                fin = t + cost
            else:
                fin = t + cost
                ef[eng] = fin
            for k in R:
                if rd.get(k, 0.0) < fin:
                    rd[k] = fin
            for k in W:
                wd[k] = fin + 0.25
                rd[k] = 0.0
            self.play(it)

    def MM(self, out, lhsT, rhs, st, sp, R, W, skip=False):
        c = 0.035 + self._fd(out) / 2100.0
        if skip:
            return self.OP("pe", lambda h: h.matmul(out, lhsT, rhs, start=st, stop=sp, skip_group_check=True), R, W, c)
        return self.OP("pe", lambda h: h.matmul(out, lhsT, rhs, start=st, stop=sp), R, W, c)

    def TR(self, out, in_, R, W):
        ident = self.ident
        return self.OP("pe", lambda h: h.transpose(out, in_, ident), list(R) + ["ident"], W, 0.1)

    def _c(self, eng, out):
        fd = self._fd(out)
        if eng == "act":
            return 0.22 + fd / 1100.0
        if eng == "pool":
            return 0.6 + fd / 450.0
        return 0.1 + fd / 900.0

    def ACT(self, out, in_, func, R, W, bias=None, scale=None):
        kw = {}
        if bias is not None:
            kw["bias"] = bias
            if not isinstance(bias, (int, float)):
                R = list(R) + ["biasc"]
        if scale is not None:
            kw["scale"] = scale
        return self.OP("act", lambda h: h.activation(out, in_, func, **kw), R, W, self._c("act", out))

    def TS(self, eng, out, in0, s1, s2, op0, op1, R, W):
        if s2 is None:
            return self.OP(eng, lambda h: h.tensor_single_scalar(out, in0, s1, op0), R, W, self._c(eng, out))
        return self.OP(eng, lambda h: h.tensor_scalar(out, in0, s1, s2, op0, op1), R, W, self._c(eng, out))

    def STT(self, eng, out, in0, sc, in1, op0, op1, R, W):
        return self.OP(eng, lambda h: h.scalar_tensor_tensor(out, in0, sc, in1, op0, op1), R, W, self._c(eng, out))

    def TT(self, eng, out, in0, in1, op, R, W):
        return self.OP(eng, lambda h: h.tensor_tensor(out, in0, in1, op), R, W, self._c(eng, out))

    def CP(self, eng, out, in_, R, W):
        if eng == "act":
            return self.OP("act", lambda h: h.activation(out, in_, AF.Copy), R, W, self._c(eng, out))
        return self.OP(eng, lambda h: h.tensor_copy(out, in_), R, W, self._c(eng, out))

    def MS(self, eng, out, val, W):
        return self.OP(eng, lambda h: h.memset(out, val), (), W, self._c(eng, out))

    def SCAN(self, out, d0, d1, init, R, W):
        return self.OP("dve", lambda h: h.tensor_tensor_scan(out, d0, d1, init, ALU.mult, ALU.add), R, W, self._c("dve", out))

    def DMA(self, q, out, in_, semkey, R, W):
        return self.DM(q, lambda h: h.dma_start(out=out, in_=in_), semkey, R, W)

    def setup(self):
        P = self.P
        self.DMA("sp", self.pv, self.pvec, "c0", (), ["pv"])
        self.DMA("sp", self.cs, self.cst, "c1", (), ["cs"])
        self.TS("dve", self.g32, self.pv[:, 0:72], 32.0, None, ALU.mult, None, ["pv"], ["g32"])
        self.MS("pool", self.ones, 1.0, ["ones"])
        self.MS("pool", self.ident, 1.0, ["ident"])
        ident = self.ident
        self.OP("pool", lambda h: h.affine_select(ident, ident, [[-1, 128]], ALU.is_equal, 0.0, base=0, channel_multiplier=1),
             ["ident"], ["ident"])
        triu = self.triu
        self.MS("pool", triu, 1.0, ["triu"])
        self.OP("pool", lambda h: h.affine_select(triu, triu, [[1, 128]], ALU.is_ge, 0.0, base=0, channel_multiplier=-1),
             ["triu"], ["triu"])
        self.pm32 = self.sb("pm32", [128, 128])
        self.CP("pool", self.pm32, triu, ["triu"], ["pm32"])
        self.MS("pool", self.pm32[0:64, 64:128], 0.0, ["pm32"])
        self.CP("dve", self.pmask, self.pm32, ["pm32"], ["pmask"])
        self.pmask4 = self.sb("pmask4", [128, 4, 128], U8)
        for i_ in range(4):
            self.CP("dve", self.pmask4[:, i_, :], self.pm32, ["pm32", "pmask"], ["pmask"])
        self.MS("pool", self.segm4, 1.0, ["segm"])
        self.MS("pool", self.segm4.rearrange("p (c t) -> p c t", t=64)[:, :, 0:1], 0.0, ["segm"])

    def wkeys(self, lo, hi):
        return [("wa", g) for g in range(lo // 4096, (hi - 1) // 4096 + 1)]

    def load_w(self, off, view_shape, src, nsplit, tag, idx=None):
        a, b = view_shape
        dst = self.arena[:, off:off + a * b].rearrange("p (a b) -> p a b", b=b)
        step = a // nsplit
        for i in (range(nsplit) if idx is None else idx):
            lo = off + i * step * b
            hi = off + (i + 1) * step * b
            self.DMA("pool", dst[:, i * step:(i + 1) * step, :], src[:, i * step:(i + 1) * step, :],
                     (tag, i), (), self.wkeys(lo, hi))
        return dst

    def hxkeys(self, slot):
        return [("hx", slot, m) for m in range(KC)]

    def load_tile(self, src, j):
        slot = self.tilecnt % 3
        self.cur_slot = slot
        srcv = src.rearrange("(k p) t -> p k t", p=128)[:, :, j * T:(j + 1) * T]
        rk = [("hd", j)] if src is self.hbuf else []
        self.DMA("sp", self.hx[slot], srcv, ("hx", slot), rk, self.hxkeys(slot))
        self.tilecnt += 1
        return slot

    def store_tile(self, dst, j, slot):
        dstv = dst.rearrange("(k p) t -> p k t", p=128)[:, :, j * T:(j + 1) * T]
        wk = [("hd", j)] if dst is self.hbuf else [("od", j)]
        o = self.DMA("sp", dstv, self.hx[slot], ("st", slot), self.hxkeys(slot), wk)
        if dst is self.outT:
            self.out_keys.append(("od", j))

    def rstd_from(self, ps, scale, R):
        self.ACT(self.lnt, ps, AF.Ln, R, ["lnt"], bias=self.epsb, scale=scale)
        self.ACT(self.rstd, self.lnt, AF.Exp, ["lnt"], ["rstd"], scale=-0.5)

    def norm(self, slot, gcol, out, outkeys, bufs=None):
        hx = self.hx[slot]
        hk = self.hxkeys(slot)
        if bufs is None:
            bufs = (self.sq, "sq", self.ps_stat[:, 0, :], "ps_stat", self.lnt, "lnt", self.rstd, "rstd")
        sq, sqk, ss, ssk, lnt, lntk, rstd, rstdk = bufs
        self.ACT(sq, hx, AF.Square, hk, [sqk])
        for k in range(KC):
            self.MM(ss, self.ones, sq[:, k, :], k == 0, k == KC - 1, [sqk, "ones"], [ssk])
        self.ACT(lnt, ss, AF.Ln, [ssk], [lntk], bias=self.eps1024)
        self.ACT(rstd, lnt, AF.Exp, [lntk], [rstdk], scale=-0.5)
        for k in range(KC):
            self.STT("dve", out[:, k, :], hx[:, k, :], gcol[:, k:k + 1], rstd, ALU.mult, ALU.mult,
                     [("hx", slot, k), rstdk, "g32"], [outkeys[k]])

    def phase_ffn(self, l, src, dst, last):
        pre = self.w1_prefetched.pop(l, ())
        w1 = self.load_w(0, (KC, 4096), self.w_ffn_in[l].rearrange("(k p) n -> p k n", p=128), 8, "w",
                         idx=[i for i in range(8) if i not in pre])
        w2 = self.load_w(32768, (32, D), self.w_ffn_out[l].rearrange("(k p) n -> p k n", p=128), 8, "w2")
        hid = self.big[:, 0:32 * T].rearrange("p (m t) -> p m t", t=T)
        r32 = [self.sb_r32a, self.sb_r32b]
        gcol = self.g32[:, PV_NFFN + l * 8:PV_NFFN + l * 8 + 8]
        fg = self.g32[:, PV_NFIN:PV_NFIN + 8]
        xns = [self.xn, self.y[1]]
        xnks = [[("xn", k) for k in range(KC)], [("xn2", k) for k in range(KC)]]
        dofin = last and self.final_norm
        n = self.ntiles
        slots = {0: self.load_tile(src, 0)}
        if n > 1:
            slots[1] = self.load_tile(src, 1)
        self.norm(slots[0], gcol, xns[0], xnks[0])

        def norm_parts(slot, out, outkeys):
            hx = self.hx[slot]
            hk = self.hxkeys(slot)
            ss = self.ps_stat[:, 0, :]

            def p1():
                self.ACT(self.sq, hx, AF.Square, hk, ["sq"])

            def p2():
                for k in range(KC):
                    self.MM(ss, self.ones, self.sq[:, k, :], k == 0, k == KC - 1, ["sq", "ones"], ["ps_stat"])

            def p3():
                self.ACT(self.lnt, ss, AF.Ln, ["ps_stat"], ["lnt"], bias=self.eps1024)
                self.ACT(self.rstd, self.lnt, AF.Exp, ["lnt"], ["rstd"], scale=-0.5)

            def p4():
                for k in range(KC):
                    self.STT("dve", out[:, k, :], hx[:, k, :], gcol[:, k:k + 1], self.rstd, ALU.mult, ALU.mult,
                             [("hx", slot, k), "rstd", "g32"], [outkeys[k]])
            return p1, p2, p3, p4

        def hidden(j, lo, hi):
            xn = xns[j % 2]
            xk = xnks[j % 2]
            for m2 in range(lo, hi):
                pj = self.ps_pj[m2 % 2]
                pk = f"pj{m2 % 2}"
                for h2 in range(2):
                    m = m2 * 2 + h2
                    for k in range(KC):
                        self.MM(pj[:, h2, :], w1[:, k, m * 128:(m + 1) * 128], xn[:, k, :], k == 0, k == KC - 1,
                                [xk[k], ("wa", k)], [pk])
                rb = r32[m2 % 2]
                rk = f"r32{m2 % 2}"
                self.ACT(rb, pj, AF.Relu, [pk], [rk])
                self.TT("pool", hid[:, 2 * m2:2 * m2 + 2, :], rb, rb, ALU.mult, [rk], [("hid", m2)])

        for j in range(n):
            slot = slots[j]
            if j + 1 < n:
                p1, p2, p3, p4 = norm_parts(slots[j + 1], xns[(j + 1) % 2], xnks[(j + 1) % 2])
            else:
                p1 = p2 = p3 = p4 = (lambda: None)
            hidden(j, 0, 4)
            p1()
            hidden(j, 4, 8)
            p2()
            p3()
            hidden(j, 8, 12)
            p4()
            hidden(j, 12, 16)
            if j + 2 < n:
                slots[j + 2] = self.load_tile(src, j + 2)
            for m in range(KC):
                po = self.ps_po[m % 2][:, 0:T]
                pok = f"po{m % 2}"
                for k in range(32):
                    self.MM(po, w2[:, k, m * 128:(m + 1) * 128], hid[:, k, :], k == 0, k == 31,
                            [("hid", k // 2), ("wa", 8 + k // 4)], [pok])
                self.TT("dve", self.hx[slot][:, m, :], po, self.hx[slot][:, m, :], ALU.add,
                        [pok, ("hx", slot, m)], [("hx", slot, m)])
            if dofin:
                self.norm(slot, fg, self.hx[slot], self.hxkeys(slot))
            self.store_tile(dst, j, slot)

    def out_proj(self, wout, slot, yb=None, yp=0):
        if yb is None:
            yb = self.y[0]
        for m in range(KC):
            po = self.ps_po[m % 2][:, 0:T]
            pok = f"po{m % 2}"
            for k in range(KC):
                self.MM(po, wout[:, k, m * 128:(m + 1) * 128], yb[:, k, :], k == 0, k == KC - 1,
                        [("y", yp, k), ("wa", 6 + k // 4)], [pok])
            self.TT("dve", self.hx[slot][:, m, :], po, self.hx[slot][:, m, :], ALU.add,
                    [pok, ("hx", slot, m)], [("hx", slot, m)])

    def pipeline(self, src, dst, A, B):
        n = self.ntiles
        self.sched_reset()
        slots = {0: self.load_tile(src, 0)}
        if n > 1:
            slots[1] = self.load_tile(src, 1)
        st0 = [self.collect(pl) for pl in A(0, slots[0])]
        self.sched_merge(st0, {i: list(range(i)) for i in range(1, len(st0))})
        for j in range(n):
            if j + 2 < n:
                slots[j + 2] = self.load_tile(src, j + 2)
            if j == n - 1 and self.next_ffn is not None:
                lf = self.next_ffn
                self.load_w(0, (KC, 4096), self.w_ffn_in[lf].rearrange("(k p) n -> p k n", p=128), 8, "w", idx=range(6))
                self.w1_prefetched[lf] = set(range(6))
            streams = [self.collect(B(j, slots[j]))]
            if j + 1 < n:
                streams += [self.collect(pl) for pl in A(j + 1, slots[j + 1])]
            self.sched_merge(streams, {i: list(range(1, i)) for i in range(2, len(streams))})
            self.store_tile(dst, j, slots[j])

    def abuf(self, off, n, dt=BF16):
        if dt == F32:
            return self.arena[:, off:off + 2 * n].bitcast(F32), off + 2 * n
        return self.arena[:, off:off + n], off + n

    def phase_odd(self, l, src, dst):
        j_ = l // 2
        P = self.P
        win = self.load_w(0, (KC, 1536), self.w_in_odd[j_].rearrange("(k p) n -> p k n", p=128), 8, "w")
        wout = self.load_w(24576, (KC, D), self.w_out_odd[j_].rearrange("(k p) n -> p k n", p=128), 2, "wo")
        off = 33792
        wsg32, off = self.abuf(off, 512, F32)
        wsg32 = wsg32.rearrange("p (g t) -> p g t", t=128)
        self.DMA("sp", wsg32, self.sguwT[j_].rearrange("g s t -> s g t"), "c2", (), ["wsg32"])
        wsg = self.arena[:, 32768:32768 + 512].rearrange("p (g t) -> p g t", t=128)
        self.TT("dve", wsg, wsg32, self.triu.unsqueeze(1).to_broadcast([128, 4, 128]), ALU.mult,
                ["wsg32", "triu"], [("wa", 8)])
        wpl = self.arena[:, 32768 + 512:32768 + 1024].rearrange("p (g t) -> p g t", t=128)
        self.DMA("pool", wpl, self.poolw[j_].rearrange("g c d -> c g d"), "c3", (), [("wa", 8)])
        sbb, off = self.abuf(off, 512, F32)
        self.DMA("sp", sbb, self.sgub[j_].partition_broadcast(128), "c4", (), ["sbb"])
        gcol = self.g32[:, PV_NMIX + l * 8:PV_NMIX + l * 8 + 8]
        psc = self.pv[:, PV_ODD + j_ * 4:PV_ODD + j_ * 4 + 4]
        xnk = [("xn", k) for k in range(KC)]
        ug, vn, pp = [], [], []
        for p in range(2):
            a, off = self.abuf(off, 4 * T)
            ug.append(a.rearrange("p (c t) -> p c t", t=T))
            a, off = self.abuf(off, 1024)
            vn.append(a.rearrange("p (s c) -> p s c", c=512))
            a, off = self.abuf(off, 4 * (16 + T), F32)
            pp.append(a.rearrange("p (g t) -> p g t", t=16 + T))
        vgs = []
        for i_ in range(2):
            a, off = self.abuf(off, 512, F32)
            vgs.append(a)
        mv8 = self.sb_mv8
        pooled, off = self.abuf(off, 4 * T)
        pooled = pooled.rearrange("p (g t) -> p g t", t=T)
        ws = []
        for i in range(2):
            a, off = self.abuf(off, 4 * (T + 16), F32)
            ws.append(a.rearrange("p (g t) -> p g t", t=T + 16))
        svt, off = self.abuf(off, 512, F32)
        svt = svt.rearrange("p (g t) -> p g t", t=128)
        assert off <= 65536
        st6 = self.sb_st6
        mv = self.sb_mv
        L = T + 16
        inw = lambda k: [("wa", (k * 1536) // 4096), ("wa", (k * 1536 + 1535) // 4096)]

        def A(j, slot):
            p = j % 2
            first = (j % self.tiles_per_seq) == 0
            pcs = []

            def a_norm():
                if first:
                    self.MS("pool", pp[p][:, :, 0:16], 0.0, [("pp", p)])
                else:
                    self.CP("pool", pp[p][:, :, 0:16], pp[1 - p][:, :, T:T + 16], [("pp", 1 - p)], [("pp", p)])
                self.norm(slot, gcol, self.xn, xnk)
            pcs.append(a_norm)

            def pair(m0, evac, i):
                def f():
                    pj = self.ps_pj[i]
                    pk = f"pj{i}"
                    for h2 in range(2):
                        m = m0 + h2
                        for k in range(KC):
                            self.MM(pj[:, h2, :], win[:, k, m * 128:(m + 1) * 128], self.xn[:, k, :], k == 0, k == KC - 1,
                                    [("xn", k)] + inw(k), [pk])
                    evac(pj, pk)
                return f
            for m2 in range(2):
                pcs.append(pair(2 * m2, lambda pj, pk, m2=m2: self.ACT(ug[p][:, 2 * m2:2 * m2 + 2, :], pj, AF.Gelu_apprx_tanh,
                                                                        [pk], [("ug", p)]), m2))
            for m2 in range(2):
                pcs.append(pair(8 + 2 * m2, lambda pj, pk, m2=m2: self.CP("act", pp[p][:, 2 * m2:2 * m2 + 2, 16:16 + T], pj,
                                                                            [pk], [("pp", p)]), m2))

            def vsub(sub):
                def f():
                    for k in range(KC):
                        self.MM(self.ps_v, self.xn[:, k, sub * 128:(sub + 1) * 128], win[:, k, 512:1024], k == 0, k == KC - 1,
                                [("xn", k)] + inw(k), ["ps_v"])
                    self.ACT(vgs[sub], self.ps_v, AF.Gelu_apprx_tanh, ["ps_v"], [("vg", sub)])
                    for g in range(4):
                        vgg = vgs[sub][:, g * 128:(g + 1) * 128]
                        self.OP("dve", (lambda h, o=st6[:, g, :], i=vgg: h.bn_stats(o, i)), [("vg", sub)], [("st6", g)])
                        self.OP("dve", (lambda h, o=mv8[:, sub * 4 + g, :], i=st6[:, g, :]: h.bn_aggr(o, i)), [("st6", g)], ["mv"])
                return f
            pcs.append(vsub(0))
            pcs.append(vsub(1))

            def vfin():
                self.ACT(self.sb_l8, mv8[:, :, 1], AF.Ln, ["mv"], ["l4"], bias=self.epsb)
                self.ACT(self.sb_r8, self.sb_l8, AF.Exp, ["l4"], ["r4"], scale=-0.5)
                for sub in range(2):
                    for g in range(4):
                        self.TS("dve", vn[p][:, sub, g * 128:(g + 1) * 128], vgs[sub][:, g * 128:(g + 1) * 128],
                                mv8[:, sub * 4 + g, 0:1], self.sb_r8[:, sub * 4 + g:sub * 4 + g + 1], ALU.subtract, ALU.mult,
                                [("vg", sub), "mv", "r4"], [("vn", p)])
            pcs.append(vfin)
            return [pcs]

        def B(j, slot):
            p = j % 2
            first = (j % self.tiles_per_seq) == 0
            pcs = []

            def sgu(sub):
                def f():
                    svp = self.ps_ao.rearrange("p a t -> p (a t)").rearrange("p (g t) -> p g t", t=128)
                    for g in range(4):
                        self.MM(svp[:, g, :], vn[p][:, sub, g * 128:(g + 1) * 128], wsg[:, g, :], True, True,
                                [("vn", p), ("wa", 8)], ["ps_ao"])
                    self.TT("dve", svt, svp, sbb.rearrange("p (g t) -> p g t", t=128), ALU.add, ["ps_ao", "sbb"], ["svt"])
                    self.TT("dve", self.y[0][:, 0:4, sub * 128:(sub + 1) * 128], svt, ug[p][:, :, sub * 128:(sub + 1) * 128],
                            ALU.mult, ["svt", ("ug", p)], [("y", 0, 0), ("y", 0, 1), ("y", 0, 2), ("y", 0, 3)])
                return f
            pcs.append(sgu(0))
            pcs.append(sgu(1))

            def pool_():
                a = pp[p]
                self.TT("dve", ws[0][:, :, 1:L], a[:, :, 1:L], a[:, :, 0:L - 1], ALU.add, [("pp", p)], ["ws0"])
                self.TT("dve", ws[1][:, 1:4, 3:L], ws[0][:, 1:4, 3:L], ws[0][:, 1:4, 1:L - 2], ALU.add, ["ws0"], ["ws1"])
                self.TT("dve", ws[0][:, 2:4, 7:L], ws[1][:, 2:4, 7:L], ws[1][:, 2:4, 3:L - 4], ALU.add, ["ws1", "ws0"], ["ws0b"])
                self.TT("dve", ws[1][:, 3, 15:L], ws[0][:, 3, 15:L], ws[0][:, 3, 7:L - 8], ALU.add, ["ws0b", "ws1"], ["ws1b"])
                finals = [(ws[0], ["ws0"]), (ws[1], ["ws1"]), (ws[0], ["ws0b"]), (ws[1], ["ws1b"])]
                for g in range(4):
                    win_ = 2 << g
                    cur, ck = finals[g]
                    cur = cur[:, g, :]
                    if first:
                        self.TS("dve", svt.rearrange("p g t -> p (g t)")[:, 0:T], cur[:, 16:L], 1.0 / win_, None, ALU.mult, None,
                                ck, ["svt"])
                        self.TT("dve", svt.rearrange("p g t -> p (g t)")[:, 0:16], cur[:, 16:32], self.cs[:, g * 16:(g + 1) * 16],
                                ALU.mult, ck + ["cs", "svt"], ["svt"])
                        self.TT("dve", pooled[:, g, :], svt.rearrange("p g t -> p (g t)")[:, 0:T], a[:, g, 16:L], ALU.subtract,
                                ["svt", ("pp", p)], [("pooled", g)])
                    else:
                        self.STT("dve", pooled[:, g, :], cur[:, 16:L], 1.0 / win_, a[:, g, 16:L], ALU.mult, ALU.subtract,
                                 ck + [("pp", p)], [("pooled", g)])
            pcs.append(pool_)

            def pd_(g2):
                def f():
                    pd = self.ps_v.rearrange("p (a t) -> p a t", t=T)
                    for h2 in range(2):
                        g = 2 * g2 + h2
                        self.MM(pd[:, h2, :], wpl[:, g, :], pooled[:, g, :], True, True, [("pooled", g), ("wa", 8)], ["ps_v"])
                    for h2 in range(2):
                        g = 2 * g2 + h2
                        self.OP("act", (lambda h, o=self.y[0][:, 4 + g, :], i_=pd[:, h2, :], s_=psc[:, g:g + 1]: h.mul(o, i_, s_)),
                             ["ps_v", "pv"], [("y", 0, 4 + g)])
                return f
            pcs.append(pd_(0))
            pcs.append(pd_(1))
            pcs.append(lambda: self.out_proj(wout, slot))
            return pcs

        self.pipeline(src, dst, A, B)

    def even_consts(self, j_):
        b = PV_EVEN + j_ * 44
        pv = self.pv
        ec = self.sb_ec
        k = f"ec"
        self.ACT(ec[:, 0:4], pv[:, b + E_LAM:b + E_LAM + 4], AF.Exp, ["pv"], [k], scale=-1.0)
        self.ACT(ec[:, 4:8], ec[:, 0:4], AF.Ln, [k], [k], bias=self.oneb)
        self.TS("dve", ec[:, 8:12], ec[:, 4:8], -4.0, None, ALU.mult, None, [k], [k])
        self.TS("dve", ec[:, 12:16], ec[:, 4:8], -8.0, None, ALU.mult, None, [k], [k])
        self.TS("dve", ec[:, 16:20], pv[:, b + E_BA:b + E_BA + 4], 0.5, None, ALU.mult, None, ["pv", k], [k])
        self.TS("dve", ec[:, 20:24], pv[:, b + E_BX:b + E_BX + 4], 0.5, None, ALU.mult, None, ["pv", k], [k])
        if j_ == 0:
            self.MS("dve", ec[:, 24:28], 0.0, [k])
        else:
            self.ACT(ec[:, 28:32], pv[:, b + E_LBL0:b + E_LBL0 + 4], AF.Exp, ["pv", k], [k])
            self.ACT(ec[:, 32:36], pv[:, b + E_LBL1:b + E_LBL1 + 4], AF.Exp, ["pv", k], [k])
            self.TT("dve", ec[:, 28:32], ec[:, 28:32], ec[:, 32:36], ALU.add, [k], [k])
            self.OP("dve", lambda h: h.reciprocal(ec[:, 28:32], ec[:, 28:32]), [k], [k])
            self.TT("dve", ec[:, 24:28], ec[:, 32:36], ec[:, 28:32], ALU.mult, [k], [k])
        self.TS("dve", ec[:, 28:32], ec[:, 24:28], 0.5, 0.5, ALU.mult, ALU.add, [k], [k])
        self.TS("dve", ec[:, 32:36], ec[:, 24:28], -0.5, 0.5, ALU.mult, ALU.add, [k], [k])
        self.TS("dve", ec[:, 36:40], ec[:, 32:36], -1.0, None, ALU.mult, None, [k], [k])

    def phase_even(self, l, src, dst):
        j_ = l // 2
        P = self.P
        win = self.load_w(0, (KC, 3072), self.w_in_even[j_].rearrange("(k p) n -> p k n", p=128), 8, "w")
        wout = self.load_w(24576, (KC, D), self.w_out_even[j_].rearrange("(k p) n -> p k n", p=128), 2, "wo")
        wg = self.arena[:, 32768:32768 + 1024].rearrange("p (a c t) -> p a c t", a=2, c=4)
        self.DMA("pool", wg, self.wgate[j_].rearrange("a c i o -> i a c o"), "c3", (), [("wa", 8)])
        self.even_consts(j_)
        ec = self.sb_ec
        pvb = PV_EVEN + j_ * 44
        pv = self.pv
        gcol = self.g32[:, PV_NMIX + l * 8:PV_NMIX + l * 8 + 8]
        xnk = [("xn", k) for k in range(KC)]
        off = 33792
        c4 = lambda a: a.rearrange("p (c t) -> p c t", t=T)
        gg, qf, sg, tz, vt, xa = [], [], [], [], [], []
        for p in range(2):
            a, off = self.abuf(off, 4 * T); gg.append(c4(a))
            a, off = self.abuf(off, 4 * T); qf.append(c4(a))
            a, off = self.abuf(off, 4 * T); sg.append(c4(a))
            a, off = self.abuf(off, 4 * T, F32); tz.append(c4(a))
            a, off = self.abuf(off, 1024); vt.append(a.rearrange("p (s c) -> p s c", c=512))
            a, off = self.abuf(off, 4 * (4 + T), F32); xa.append(a.rearrange("p (c t) -> p c t", t=4 + T)[:, :, 1:4 + T])
        ua, off = self.abuf(off, 4 * T, F32); ua = c4(ua)
        tr, off = self.abuf(off, 4 * T, F32); tr = c4(tr)
        ti, off = self.abuf(off, 4 * T, F32); ti = c4(ti)
        aa, off = self.abuf(off, 4 * T, F32); aa = c4(aa)
        a2, off = self.abuf(off, 4 * T, F32); a2 = c4(a2)
        uab, off = self.abuf(off, 4 * T); uab = c4(uab)
        kk, off = self.abuf(off, 4 * T, F32); kk = c4(kk)
        e1, off = self.abuf(off, 4 * T); e1 = c4(e1)
        assert off <= 65536, off
        boff = [0]

        def bbuf(n, dt=BF16):
            if dt == F32:
                r = self.big[:, boff[0]:boff[0] + 2 * n].bitcast(F32)
                boff[0] += 2 * n
            else:
                r = self.big[:, boff[0]:boff[0] + n]
                boff[0] += n
            return r
        logf = c4(bbuf(4 * T, F32))
        bc = c4(bbuf(4 * T, F32))
        osb = c4(bbuf(4 * T, F32))
        qt = c4(bbuf(4 * T))
        kt = c4(bbuf(4 * T))
        assert boff[0] <= 8192
        S = self.sb_S
        St = self.sb_St
        Sb = self.sb_Sb
        ktT = self.sb_ktT
        Asb = self.sb_Asb
        sc = self.sb_sc
        esc = self.sb_esc
        hst = self.sb_hst
        osq = self.sb_osq
        rs2 = self.sb_rs2
        self.MS("pool", Asb, 0.0, ["Asb"])
        inw = lambda k, m: [("wa", (k * 3072 + m * 128) // 4096)]
        Skeys = [("S", h_) for h_ in range(4)]
        uak = [("ua", c) for c in range(4)]

        def A(j, slot):
            p = j % 2
            first = (j % self.tiles_per_seq) == 0
            yb = self.y[p]
            pcs = []

            def a_norm():
                if first:
                    self.MS("pool", xa[p][:, :, 0:3], 0.0, [("xa", p)])
                else:
                    self.CP("pool", xa[p][:, :, 0:3], xa[1 - p][:, :, T:T + 3], [("xa", 1 - p)], [("xa", p)])
                self.norm(slot, gcol, self.xn, xnk)
            pcs.append(a_norm)
            cnt = [0]

            def pair(m0, evac):
                def f():
                    i = cnt[0] % 2
                    cnt[0] += 1
                    pj = self.ps_pj[i]
                    pk = f"pj{i}"
                    for h2 in range(2):
                        m = m0 + h2
                        for k in range(KC):
                            self.MM(pj[:, h2, :], win[:, k, m * 128:(m + 1) * 128], self.xn[:, k, :], k == 0, k == KC - 1,
                                    [("xn", k)] + inw(k, m), [pk])
                    evac(pj, pk)
                return f

            def lru1():
                if first:
                    self.MS("dve", hst, 0.0, ["hst"])
                for c in range(4):
                    cw = lambda k_: pv[:, pvb + E_CONVW + k_ * 4 + c:pvb + E_CONVW + k_ * 4 + c + 1]
                    self.TS("dve", ua[:, c, :], xa[p][:, c, 0:T], cw(0), pv[:, pvb + E_CONVB + c:pvb + E_CONVB + c + 1],
                            ALU.mult, ALU.add, [("xa", p), "pv"], [("ua", c)])
                    for k_ in range(1, 4):
                        self.STT("dve", ua[:, c, :], xa[p][:, c, k_:k_ + T], cw(k_), ua[:, c, :], ALU.mult, ALU.add,
                                 [("xa", p), "pv", ("ua", c)], [("ua", c)])
                self.CP("act", uab, ua, uak, ["uab"])

            def lru2():
                for c in range(4):
                    pj = self.ps_v.rearrange("p (a t) -> p a t", t=T)
                    pk = "ps_v"
                    self.MM(pj[:, 0, :], wg[:, 0, c, :], uab[:, c, :], True, True, ["uab", ("wa", 8)], [pk])
                    self.MM(pj[:, 1, :], wg[:, 1, c, :], uab[:, c, :], True, True, ["uab", ("wa", 8)], [pk])
                    self.ACT(tr[:, c, :], pj[:, 0, :], AF.Tanh, [pk, "ec"], ["tr"], bias=ec[:, 16 + c:17 + c], scale=0.5)
                    self.ACT(ti[:, c, :], pj[:, 1, :], AF.Tanh, [pk, "ec"], ["ti"], bias=ec[:, 20 + c:21 + c], scale=0.5)
                for c in range(4):
                    self.ACT(aa[:, c, :], tr[:, c, :], AF.Exp, ["tr", "ec"], ["aa"], bias=ec[:, 8 + c:9 + c], scale=ec[:, 8 + c:9 + c])
                    self.ACT(a2[:, c, :], tr[:, c, :], AF.Exp, ["tr", "ec"], ["a2"], bias=ec[:, 12 + c:13 + c], scale=ec[:, 12 + c:13 + c])
                self.TS("dve", a2, a2, 1.0, -1.0, ALU.min, ALU.mult, ["a2"], ["a2"])
                self.ACT(a2, a2, AF.Ln, ["a2"], ["a2"], bias=self.oneb)
                self.ACT(a2, a2, AF.Exp, ["a2"], ["a2"], scale=0.5)
                self.STT("dve", ti, ti, 1.0, ua, ALU.add, ALU.mult, ["ti"] + uak, ["ti"])
                self.STT("dve", ti, ti, 0.5, a2, ALU.mult, ALU.mult, ["ti", "a2"], ["ti"])

            def lru3():
                for c in range(4):
                    self.SCAN(tr[:, c, :], aa[:, c, :], ti[:, c, :], hst[:, c:c + 1], ["aa", "ti", "hst", "tr"], ["tr"])
                self.CP("dve", hst, tr[:, :, T - 1], ["tr"], ["hst"])
                self.TT("dve", yb[:, 0:4, :], tr, gg[p], ALU.mult, ["tr", ("gg", p)], [("y", p, c) for c in range(4)])

            for c2 in range(2):
                pcs.append(pair(2 * c2, lambda pj, pk, c2=c2: self.CP("act", xa[p][:, 2 * c2:2 * c2 + 2, 3:3 + T], pj, [pk], [("xa", p)])))
            for c2 in range(2):
                pcs.append(pair(4 + 2 * c2, lambda pj, pk, c2=c2: self.ACT(gg[p][:, 2 * c2:2 * c2 + 2, :], pj, AF.Gelu_apprx_tanh, [pk], [("gg", p)])))
            for c2 in range(2):
                pcs.append(pair(12 + 2 * c2, lambda pj, pk, c2=c2: self.ACT(tz[p][:, 2 * c2:2 * c2 + 2, :], pj, AF.Tanh, [pk], [("tz", p)], scale=0.5)))
            for c2 in range(2):
                pcs.append(pair(8 + 2 * c2, lambda pj, pk, c2=c2: self.ACT(qf[p][:, 2 * c2:2 * c2 + 2, :], pj, AF.Silu, [pk], [("qf", p)])))
            for c2 in range(2):
                pcs.append(pair(20 + 2 * c2, lambda pj, pk, c2=c2: self.ACT(sg[p][:, 2 * c2:2 * c2 + 2, :], pj, AF.Silu, [pk], [("sg", p)])))

            def vsub(sub):
                def f():
                    i = cnt[0] % 2
                    cnt[0] += 1
                    pjv = self.ps_pj[i].rearrange("p a t -> p (a t)")
                    pk = f"pj{i}"
                    for k in range(KC):
                        lo = k * 3072 + 2048
                        self.MM(pjv, self.xn[:, k, sub * 128:(sub + 1) * 128], win[:, k, 2048:2560], k == 0, k == KC - 1,
                                [("xn", k), ("wa", lo // 4096), ("wa", (lo + 511) // 4096)], [pk])
                    self.CP("dve", vt[p][:, sub, :], pjv, [pk], [("vt", p)])
                return f
            pcs.append(vsub(0))
            pcs.append(vsub(1))
            return [pcs, [lru1, lru2, lru3]]

        def B(j, slot):
            p = j % 2
            first = (j % self.tiles_per_seq) == 0
            yb = self.y[p]
            pcs = []

            def hg1():
                if first:
                    self.MS("dve", S, 0.0, Skeys)
                for hd in range(4):
                    self.ACT(logf[:, hd, :], tz[p][:, hd, :], AF.Ln, [("tz", p), "ec"], ["logf"],
                             bias=ec[:, 28 + hd:29 + hd], scale=ec[:, 32 + hd:33 + hd])
                for hd in range(4):
                    self.TS("dve", kk[:, hd, :], tz[p][:, hd, :], ec[:, 36 + hd:37 + hd], ec[:, 32 + hd:33 + hd], ALU.mult, ALU.add,
                            [("tz", p), "ec"], ["kk"])
                flat = lambda a: a.rearrange("p c t -> p (c t)")
                self.SCAN(flat(bc), self.segm4, flat(logf), 0.0, ["segm", "logf"], ["bc"])
                bcv = flat(bc).rearrange("p (c t) -> p c t", t=64)
                self.TT("dve", flat(logf).rearrange("p (c t) -> p c t", t=64), bcv, bcv[:, :, 31:32].to_broadcast([128, 16, 64]),
                        ALU.subtract, ["bc", "logf"], ["logf"])
                self.CP("dve", sc[:, 0, :], bcv[:, :, 31], ["bc"], ["sc"])
                self.CP("dve", sc[:, 1, :], bcv[:, :, 63], ["bc", "sc"], ["sc"])
                self.TT("dve", sc[:, 2, :], sc[:, 1, :], sc[:, 0, :], ALU.subtract, ["sc"], ["sc"])
                self.ACT(e1, logf, AF.Exp, ["logf"], ["e1"])
                self.ACT(logf, logf, AF.Exp, ["logf"], ["logf"], scale=-1.0)
                self.ACT(esc, sc, AF.Exp, ["sc"], ["esc"])
                self.TT("dve", qt, qf[p], e1, ALU.mult, [("qf", p), "e1"], ["qt"])
                self.TT("dve", kt, kk, logf, ALU.mult, ["kk", "logf"], ["kt"])
            pcs.append(hg1)

            def hg2(hp):
                def f():
                    h0 = 2 * hp
                    ktp = self.ps_misc[:, 256:512].bitcast(BF16).rearrange("p (h r t) -> p h r t", h=2, r=2)
                    for hh in range(2):
                        for pr in range(2):
                            self.TR(ktp[:, hh, pr, :], kt[:, h0 + hh, pr * 128:(pr + 1) * 128], ["kt"], ["ps_misc"])
                    self.CP("act", ktT, ktp, ["ps_misc"], ["ktT"])
                    aps = self.ps_ao.rearrange("p a t -> p (a t)").rearrange("p (h r t) -> p h r t", h=2, r=2)
                    for hh in range(2):
                        for pr in range(2):
                            self.MM(aps[:, hh, pr, :], kt[:, h0 + hh, pr * 128:(pr + 1) * 128], qt[:, h0 + hh, pr * 128:(pr + 1) * 128],
                                    True, True, ["kt", "qt"], ["ps_ao"])
                    pm = self.pmask4
                    self.OP("dve", (lambda h, o=Asb.rearrange("p h r t -> p (h r) t"), m_=pm, d_=aps.rearrange("p h r t -> p (h r) t"):
                                 h.copy_predicated(o, m_, d_)), ["ps_ao", "pmask", "Asb"], ["Asb"])
                    ob = self.ps_po[1].rearrange("p (h t) -> p h t", h=2)
                    firstmm = [True]

                    def omm(out, lhsT, rhs, R):
                        st = firstmm[0]
                        firstmm[0] = False
                        self.OP("pe", (lambda h, o=out, l_=lhsT, r_=rhs, st=st: h.matmul(o, l_, r_, start=st, stop=False, skip_group_check=True)),
                                R, ["po1"], 0.035 + self._fd(out) / 2100.0)
                    for ch in range(4):
                        pr, half = ch // 2, ch % 2
                        for hh in range(2):
                            hd = h0 + hh
                            ei = hd * 4 + ch
                            if half == 0:
                                omm(ob[:, hh, pr * 128:(pr + 1) * 128], vt[p][:, pr, hd * 128:(hd + 1) * 128], Asb[:, hh, pr, :],
                                    [("vt", p), "Asb"])
                            self.OP("act", (lambda h, o=Sb[:, hd, :], i_=S[:, hd, :], s_=esc[:, 0, ei:ei + 1]: h.mul(o, i_, s_)),
                                 [("S", hd), "esc"], [("Sb", hd)])
                            omm(ob[:, hh, ch * 64:(ch + 1) * 64], Sb[:, hd, :], qt[:, hd, ch * 64:(ch + 1) * 64], [("Sb", hd), "qt"])
                        for hh in range(2):
                            hd = h0 + hh
                            ei = hd * 4 + ch
                            up = self.ps_u[hh]
                            self.MM(up, ktT[half * 64:(half + 1) * 64, hh, pr, :],
                                    vt[p][half * 64:(half + 1) * 64, pr, hd * 128:(hd + 1) * 128],
                                    True, True, ["ktT", ("vt", p)], ["ps_misc"])
                            self.TS("dve", St[:, hd, :], S[:, hd, :], esc[:, 1, ei:ei + 1], None, ALU.mult, None,
                                    [("S", hd), "esc"], [("St", hd)])
                            self.STT("dve", S[:, hd, :], up, esc[:, 2, ei:ei + 1], St[:, hd, :], ALU.mult, ALU.add,
                                     ["ps_misc", ("St", hd), "esc"], [("S", hd)])
                    self.ACT(osq, ob, AF.Square, ["po1"], ["osq"])
                    self.CP("act", osb[:, h0:h0 + 2, :], ob, ["po1"], [("osb", hp)])
                    osp = self.ps_ao
                    for hh in range(2):
                        self.MM(osp[:, hh, :], self.ones, osq[:, hh, :], True, True, ["osq", "ones"], ["ps_ao"])
                    self.ACT(rs2, osp, AF.Ln, ["ps_ao"], ["rs2"], bias=self.epsb, scale=1.0 / 128.0)
                    self.ACT(rs2, rs2, AF.Exp, ["rs2"], ["rs2"], scale=-0.5)
                    for hh in range(2):
                        hd = h0 + hh
                        self.STT("dve", osb[:, hd, :], osb[:, hd, :], pv[:, pvb + E_HN + hd:pvb + E_HN + hd + 1], rs2[:, hh, :],
                                 ALU.mult, ALU.mult, [("osb", hp), "rs2", "pv"], [("osb", hp)])
                return f
            pcs.append(hg2(0))
            pcs.append(hg2(1))

            def fin():
                self.TT("dve", yb[:, 4:8, :], osb, sg[p], ALU.mult, [("osb", 0), ("osb", 1), ("sg", p)],
                        [("y", p, 4 + h_) for h_ in range(4)])
                self.out_proj(wout, slot, yb, p)
            pcs.append(fin)
            return pcs

        self.pipeline(src, dst, A, B)

    def build(self):
        sb = self.sb
        yf = self.y[0].rearrange("p k t -> p (k t)")
        self.sb_r32a = yf[:, 0:1024].bitcast(F32).rearrange("p (a t) -> p a t", t=T)
        self.sb_r32b = yf[:, 1024:2048].bitcast(F32).rearrange("p (a t) -> p a t", t=T)
        self.sb_st6 = sb("st6", [128, 4, 6])
        self.sb_mv = sb("mv", [128, 4, 2])
        self.sb_l8 = sb("l8", [128, 8])
        self.sb_r8 = sb("r8", [128, 8])
        self.sb_mv8 = sb("mv8", [128, 8, 2])
        self.sb_ec = sb("ec", [128, 40])
        self.sb_hst = sb("hst", [128, 4])
        self.sb_Sb = sb("Sb", [128, 4, 128], BF16)
        self.sb_ktT = sb("ktT", [128, 2, 2, 128], BF16)
        self.sb_Asb = sb("Asb", [128, 2, 2, 128], BF16)
        self.sb_sc = sb("sc", [128, 3, 16])
        self.sb_esc = sb("esc", [128, 3, 16])
        self.sb_S = sb("S", [128, 4, 128])
        self.sb_St = sb("St", [128, 4, 128])
        self.sb_osq = sb("osq", [128, 2, T], BF16)
        self.sb_rs2 = sb("rs2", [128, 2, T])
        self.segm4 = sb("segm4", [128, 4 * T])
        self.sb_bias = sb("biasc", [128, 4])
        self.ps_u = [self.ps_misc[:, 0:128], self.ps_misc[:, 128:256]]
        self.ps_ktT = self.ps_misc[:, 256:384].bitcast(BF16).rearrange("p (r t) -> p r t", t=128)
        self.out_keys = []
        self.w1_prefetched = {}
        self.next_ffn = None
        self.MS("dve", self.sb_bias[:, 0:1], EPS, ["biasc"])
        self.MS("dve", self.sb_bias[:, 1:2], 1.0, ["biasc"])
        self.MS("dve", self.sb_bias[:, 2:3], 1024.0 * EPS, ["biasc"])
        self.epsb = self.sb_bias[:, 0:1]
        self.oneb = self.sb_bias[:, 1:2]
        self.eps1024 = self.sb_bias[:, 2:3]
        self.setup()
        n = len(self.phases)
        for i, (kind, l) in enumerate(self.phases):
            src = self.xT if i == 0 else self.hbuf
            dst = self.outT if i == n - 1 else self.hbuf
            if i > 0:
                self.P.barrier()
                self.P.new_epoch()
            self.next_ffn = self.phases[i + 1][1] if (i + 1 < n and self.phases[i + 1][0] == "ffn" and kind != "ffn") else None
            if kind == "ffn":
                self.phase_ffn(l, src, dst, last=(i == n - 1))
            elif l % 2 == 0:
                self.phase_even(l, src, dst)
            else:
                self.phase_odd(l, src, dst)
        self.P.fence("sp", (), self.out_keys)
        with self.nc.allow_low_precision("bf16 matmul operands, fp32 accumulation"):
            self.P.emit()
        return self.nc


ALL_PHASES = [("mix", 0), ("ffn", 0), ("mix", 1), ("ffn", 1), ("mix", 2), ("ffn", 2), ("mix", 3), ("ffn", 3)]


def prep_shared(inp):
    f = lambda a: np.ascontiguousarray(np.asarray(a, np.float32))
    return {
        "w_in_even": f(inp["w_in_even"]), "w_out_even": f(inp["w_out_even"]),
        "w_in_odd": f(inp["w_in_odd"]), "w_out_odd": f(inp["w_out_odd"]),
        "w_ffn_in": f(inp["w_ffn_in"]), "w_ffn_out": f(inp["w_ffn_out"]),
        "wgate": pack_gates(inp),
        "sguwT": f(np.transpose(np.asarray(inp["sgu_w"], np.float32), (0, 1, 3, 2))),
        "sgub": f(np.asarray(inp["sgu_b"], np.float32).reshape(2, 1, 512)),
        "poolw": f(inp["pool_w"]),
        "pvec": pack_pvec(inp),
        "cst": const_table(),
    }


def run_phases(xT_list, shared, phases, final_norm, seqlen):
    ntok = xT_list[0].shape[1]
    b = Builder(ntok, seqlen, phases, final_norm)
    nc = b.build()
    in_maps = []
    for xT in xT_list:
        m = dict(shared)
        m["xT"] = np.ascontiguousarray(xT, dtype=np.float32)
        in_maps.append(m)
    res = run_bass_kernel_spmd(nc, in_maps, core_ids=list(range(len(xT_list))))
    return [r["outT"] for r in res.results]


def kernel(**inputs):
    x = np.asarray(inputs["x"], np.float32)
    B, S, _ = x.shape
    per = B // NCORES
    shared = prep_shared(inputs)
    xT_list = [np.ascontiguousarray(x[c * per:(c + 1) * per].reshape(per * S, D).T) for c in range(NCORES)]
    outs = run_phases(xT_list, shared, ALL_PHASES, True, S)
    out = np.empty((B, S, D), np.float32)
    for c in range(NCORES):
        out[c * per:(c + 1) * per] = outs[c].T.reshape(per, S, D)
    return out
```

```python
import numpy as np
import concourse.bass as bass
import concourse.mybir as mybir
from concourse.bass_utils import run_bass_kernel_spmd

F32 = mybir.dt.float32
BF16 = mybir.dt.bfloat16
U8 = mybir.dt.uint8
AF = mybir.ActivationFunctionType
ALU = mybir.AluOpType

D = 1024
KC = 8
T = 256
EPS = 1e-6
NCORES = 8


class Op:
    __slots__ = ("eng", "fn", "deps", "signal", "sem", "count", "is_dma")

    def __init__(self, eng, fn, is_dma=False):
        self.eng = eng
        self.fn = fn
        self.deps = ()
        self.signal = False
        self.sem = None
        self.count = 0
        self.is_dma = is_dma


class Prog:
    ENGS = ("pe", "act", "dve", "pool", "sp")

    def __init__(self, nc):
        self.nc = nc
        self.ops = {e: [] for e in self.ENGS}
        self.res = {}
        self.cur_sem = {}
        self.dma_sems = {}
        self.dma_cnt = {}
        self.nsem = 0
        self.new_epoch()

    def _alloc_sem(self, name):
        self.nsem += 1
        return self.nc.alloc_semaphore(f"s{self.nsem}_{name}")

    def new_epoch(self):
        for e in ("pe", "act", "dve", "pool"):
            self.cur_sem[e] = self._alloc_sem(e)

    def _deps(self, eng, reads, writes, is_dma):
        deps = set()
        for k in reads:
            r = self.res.get(k)
            if r is not None and r[0] is not None:
                deps.add(r[0])
        for k in writes:
            r = self.res.get(k)
            if r is not None:
                if r[0] is not None:
                    deps.add(r[0])
                deps.update(r[1])
        out = []
        for d in deps:
            if eng == "pe" and d.eng == "pe" and not d.is_dma and not is_dma:
                continue
            out.append(d)
        return out

    def _update(self, o, reads, writes):
        for k in reads:
            r = self.res.get(k)
            if r is None:
                r = self.res[k] = [None, []]
            r[1].append(o)
        for k in writes:
            self.res[k] = [o, []]

    def op(self, eng, fn, reads=(), writes=()):
        o = Op(eng, fn)
        o.sem = self.cur_sem.get(eng)
        o.deps = self._deps(eng, reads, writes, False)
        for d in o.deps:
            d.signal = True
        self._update(o, reads, writes)
        self.ops[eng].append(o)
        return o

    def dma(self, queue, fn, semkey, reads=(), writes=()):
        o = Op(queue, fn, is_dma=True)
        if semkey not in self.dma_sems:
            self.dma_sems[semkey] = self._alloc_sem("dma")
            self.dma_cnt[semkey] = 0
        o.sem = self.dma_sems[semkey]
        self.dma_cnt[semkey] += 16
        o.count = self.dma_cnt[semkey]
        o.signal = True
        o.deps = self._deps(queue, reads, writes, True)
        for d in o.deps:
            d.signal = True
        self._update(o, reads, writes)
        self.ops[queue].append(o)
        return o

    def fence(self, eng, reads=(), writes=()):
        return self.op(eng, None, reads, writes)

    def barrier(self):
        lasts = []
        for e in ("pe", "act", "dve", "pool"):
            for o in reversed(self.ops[e]):
                if o.fn is not None and not o.is_dma:
                    lasts.append(o)
                    break
        for e in ("pe", "act", "dve", "pool"):
            f = Op(e, None)
            f.deps = list(lasts)
            for d in lasts:
                d.signal = True
            self.ops[e].append(f)

    def emit(self):
        for e in self.ENGS:
            cnt = {}
            for o in self.ops[e]:
                if o.is_dma:
                    continue
                if o.signal:
                    assert o.fn is not None
                    cnt[o.sem] = cnt.get(o.sem, 0) + 1
                    o.count = cnt[o.sem]
        prog = self

        def run(e, h):
            waited = {}
            for o in prog.ops[e]:
                need = {}
                for d in o.deps:
                    if d.count > need.get(d.sem, 0):
                        need[d.sem] = d.count
                for s, v in need.items():
                    if waited.get(s, 0) >= v:
                        continue
                    h.wait_ge(s, v)
                    waited[s] = v
                if o.fn is None:
                    continue
                inst = o.fn(h)
                if o.signal:
                    inst.then_inc(o.sem, 16 if o.is_dma else 1)

        with self.nc.Block() as block:
            @block.tensor
            def _(h):
                run("pe", h)

            @block.scalar
            def _(h):
                run("act", h)

            @block.vector
            def _(h):
                run("dve", h)

            @block.gpsimd
            def _(h):
                run("pool", h)

            @block.sync
            def _(h):
                run("sp", h)


PV_NMIX = 0
PV_NFFN = 32
PV_NFIN = 64
PV_EVEN = 72
PV_ODD = 160
NPV = 168
E_CONVW = 0
E_CONVB = 16
E_BA = 20
E_BX = 24
E_LAM = 28
E_HN = 32
E_LBL0 = 36
E_LBL1 = 40


def _chunks(v):
    v = np.asarray(v, np.float32)
    return np.ascontiguousarray(v.reshape(-1, 128).T)


def pack_pvec(inp):
    pv = np.zeros((128, NPV), np.float32)
    for l in range(4):
        pv[:, PV_NMIX + l * 8:PV_NMIX + l * 8 + 8] = _chunks(inp["norm_mix"][l])
        pv[:, PV_NFFN + l * 8:PV_NFFN + l * 8 + 8] = _chunks(inp["norm_ffn"][l])
    pv[:, PV_NFIN:PV_NFIN + 8] = _chunks(inp["norm_final"])
    for j in range(2):
        b = PV_EVEN + j * 44
        for k in range(4):
            pv[:, b + E_CONVW + k * 4:b + E_CONVW + k * 4 + 4] = _chunks(inp["conv_w"][j, k])
        pv[:, b + E_CONVB:b + E_CONVB + 4] = _chunks(inp["conv_b"][j])
        pv[:, b + E_BA:b + E_BA + 4] = _chunks(inp["lru_ba"][j])
        pv[:, b + E_BX:b + E_BX + 4] = _chunks(inp["lru_bx"][j])
        pv[:, b + E_LAM:b + E_LAM + 4] = _chunks(inp["lru_lambda"][j])
        pv[:, b + E_HN:b + E_HN + 4] = _chunks(inp["hgrn_norm"][j])
        pv[:, b + E_LBL0:b + E_LBL0 + 4] = _chunks(inp["hgrn_lb_logits"][0])
        pv[:, b + E_LBL1:b + E_LBL1 + 4] = _chunks(inp["hgrn_lb_logits"][1])
        pv[:, PV_ODD + j * 4:PV_ODD + j * 4 + 4] = _chunks(inp["pool_scale"][j])
    return pv


def pack_gates(inp):
    g = np.zeros((2, 2, 4, 128, 128), np.float32)
    for j in range(2):
        for ax, name in enumerate(("lru_wa", "lru_wx")):
            w = np.asarray(inp[name][j], np.float32)
            for c in range(4):
                g[j, ax, c, 0:64, 0:64] = w[2 * c]
                g[j, ax, c, 64:128, 64:128] = w[2 * c + 1]
    return g


def const_table():
    c = np.zeros((128, 64), np.float32)
    for g, win in enumerate((2, 4, 8, 16)):
        for t in range(16):
            c[:, g * 16 + t] = 1.0 / min(t + 1, win)
    return c


class Builder:
    def __init__(self, ntok, seqlen, phases, final_norm):
        self.ntok = ntok
        self.seqlen = seqlen
        self.phases = phases
        self.final_norm = final_norm
        self.ntiles = ntok // T
        self.tiles_per_seq = seqlen // T
        nc = self.nc = bass.Bass("TRN2", target_bir_lowering=False)
        self.P = Prog(nc)
        self.tilecnt = 0
        self._dram()
        self._sbuf()

    def _dram(self):
        nc = self.nc
        ei = lambda n, s: nc.dram_tensor(n, s, F32, kind="ExternalInput").ap()
        self.xT = ei("xT", [D, self.ntok])
        self.outT = nc.dram_tensor("outT", [D, self.ntok], F32, kind="ExternalOutput").ap()
        self.hbuf = nc.dram_tensor("hbuf", [D, self.ntok], F32, kind="Internal").ap()
        self.w_in_even = ei("w_in_even", [2, D, 3072])
        self.w_out_even = ei("w_out_even", [2, D, D])
        self.w_in_odd = ei("w_in_odd", [2, D, 1536])
        self.w_out_odd = ei("w_out_odd", [2, D, D])
        self.w_ffn_in = ei("w_ffn_in", [4, D, 4096])
        self.w_ffn_out = ei("w_ffn_out", [4, 4096, D])
        self.wgate = ei("wgate", [2, 2, 4, 128, 128])
        self.sguwT = ei("sguwT", [2, 4, 128, 128])
        self.sgub = ei("sgub", [2, 1, 512])
        self.poolw = ei("poolw", [2, 4, 128, 128])
        self.pvec = ei("pvec", [128, NPV])
        self.cst = ei("cst", [128, 64])

    def sb(self, name, shape, dt=F32):
        return self.nc.alloc_sbuf_tensor(name, shape, dt).ap()

    def _sbuf(self):
        nc = self.nc
        sb = self.sb
        self.arena = sb("arena", [128, 65536], BF16)
        self.hx = [sb(f"hx{i}", [128, KC, T]) for i in range(3)]
        self.sq = sb("sq", [128, KC, T], BF16)
        self.xn = sb("xn", [128, KC, T], BF16)
        self.rstd = sb("rstd", [128, T])
        self.lnt = sb("lnt", [128, T])
        self.pv = sb("pv", [128, NPV])
        self.g32 = sb("g32", [128, 72])
        self.cs = sb("cs", [128, 64])
        self.ones = sb("ones", [128, 128], BF16)
        self.ident = sb("ident", [128, 128], BF16)
        self.triu = sb("triu", [128, 128])
        self.pmask = sb("pmask", [128, 128], U8)
        self.big = sb("big", [128, 8192], BF16)
        self.mb = self.arena[:, 36864:53248]
        self.y = [sb(f"y{i}", [128, KC, T], BF16) for i in range(2)]
        pt = lambda n, s, d=F32: nc.alloc_psum_tensor(n, s, d).ap()
        self.ps_stat = pt("ps_stat", [128, 2, T])
        self.ps_pj = [pt(f"ps_pj{i}", [128, 2, T]) for i in range(2)]
        self.ps_misc = pt("ps_misc", [128, 512])
        self.ps_po = [pt(f"ps_po{i}", [128, 512]) for i in range(2)]
        self.ps_v = pt("ps_v", [128, 512])
        self.ps_ao = pt("ps_ao", [128, 2, T])

    rec = None
    SCHED_WIN = 1e-9

    @staticmethod
    def _fd(ap):
        n = 1
        for d in ap.shape[1:]:
            n *= int(d)
        return n

    def OP(self, eng, fn, R=(), W=(), cost=0.3):
        if self.rec is not None:
            self.rec.append((0, eng, fn, tuple(R), tuple(W), cost))
            return None
        return self.P.op(eng, fn, R, W)

    def DM(self, q, fn, semkey, R=(), W=(), cost=2.0):
        if self.rec is not None:
            self.rec.append((1, q, fn, semkey, tuple(R), tuple(W), cost))
            return None
        return self.P.dma(q, fn, semkey, R, W)

    def collect(self, pieces):
        self.rec = []
        for pc in pieces:
            pc()
        r = self.rec
        self.rec = None
        return r

    def play(self, it):
        if it[0] == 0:
            self.P.op(it[1], it[2], it[3], it[4])
        else:
            self.P.dma(it[1], it[2], it[3], it[4], it[5])

    def sched_reset(self):
        self.sim_eng = {}
        self.sim_w = {}
        self.sim_r = {}

    def sched_merge(self, streams, after=None):
        heads = [0] * len(streams)
        ef, wd, rd = self.sim_eng, self.sim_w, self.sim_r
        remaining = sum(len(st) for st in streams)
        lastw = []
        for st in streams:
            d = {}
            for idx, it in enumerate(st):
                for k in (it[4] if it[0] == 0 else it[5]):
                    d[k] = idx
            lastw.append(d)
        after = after or {}
        while remaining:
            best = None
            for si, st in enumerate(streams):
                if heads[si] >= len(st):
                    continue
                it = st[heads[si]]
                eng = it[1]
                R, W = (it[3], it[4]) if it[0] == 0 else (it[4], it[5])
                blocked = False
                for sj in after.get(si, ()):
                    lw = lastw[sj]
                    hj = heads[sj]
                    for k in R + W:
                        v = lw.get(k)
                        if v is not None and v >= hj:
                            blocked = True
                            break
                    if blocked:
                        break
                if blocked:
                    continue
                t = ef.get(eng, 0.0)
                for k in R:
                    v = wd.get(k)
                    if v is not None and v > t:
                        t = v
                for k in W:
                    v = wd.get(k)
                    if v is not None and v > t:
                        t = v
                    v = rd.get(k)
                    if v is not None and v > t:
                        t = v
                if best is None or t < best[0] - self.SCHED_WIN:
                    best = (t, si)
            t, si = best
            it = streams[si][heads[si]]
            heads[si] += 1
            remaining -= 1
            eng = it[1]
            R, W = (it[3], it[4]) if it[0] == 0 else (it[4], it[5])
            cost = it[-1]
            if it[0] == 1:
                ef[eng] = t + 0.06
                fin = t + cost
            else:
                fin = t + cost
                ef[eng] = fin
            for k in R:
                if rd.get(k, 0.0) < fin:
                    rd[k] = fin
            for k in W:
                wd[k] = fin + 0.25
                rd[k] = 0.0
            self.play(it)

    def MM(self, out, lhsT, rhs, st, sp, R, W, skip=False):
        c = 0.035 + self._fd(out) / 2100.0
        if skip:
            return self.OP("pe", lambda h: h.matmul(out, lhsT, rhs, start=st, stop=sp, skip_group_check=True), R, W, c)
        return self.OP("pe", lambda h: h.matmul(out, lhsT, rhs, start=st, stop=sp), R, W, c)

    def TR(self, out, in_, R, W):
        ident = self.ident
        return self.OP("pe", lambda h: h.transpose(out, in_, ident), list(R) + ["ident"], W, 0.1)

    def _c(self, eng, out):
        fd = self._fd(out)
        if eng == "act":
            return 0.22 + fd / 1100.0
        if eng == "pool":
            return 0.6 + fd / 450.0
        return 0.1 + fd / 900.0

    def ACT(self, out, in_, func, R, W, bias=None, scale=None):
        kw = {}
        if bias is not None:
            kw["bias"] = bias
            if not isinstance(bias, (int, float)):
                R = list(R) + ["biasc"]
        if scale is not None:
            kw["scale"] = scale
        return self.OP("act", lambda h: h.activation(out, in_, func, **kw), R, W, self._c("act", out))

    def TS(self, eng, out, in0, s1, s2, op0, op1, R, W):
        if s2 is None:
            return self.OP(eng, lambda h: h.tensor_single_scalar(out, in0, s1, op0), R, W, self._c(eng, out))
        return self.OP(eng, lambda h: h.tensor_scalar(out, in0, s1, s2, op0, op1), R, W, self._c(eng, out))

    def STT(self, eng, out, in0, sc, in1, op0, op1, R, W):
        return self.OP(eng, lambda h: h.scalar_tensor_tensor(out, in0, sc, in1, op0, op1), R, W, self._c(eng, out))

    def TT(self, eng, out, in0, in1, op, R, W):
        return self.OP(eng, lambda h: h.tensor_tensor(out, in0, in1, op), R, W, self._c(eng, out))

    def CP(self, eng, out, in_, R, W):
        if eng == "act":
            return self.OP("act", lambda h: h.activation(out, in_, AF.Copy), R, W, self._c(eng, out))
        return self.OP(eng, lambda h: h.tensor_copy(out, in_), R, W, self._c(eng, out))

    def MS(self, eng, out, val, W):
        return self.OP(eng, lambda h: h.memset(out, val), (), W, self._c(eng, out))

    def SCAN(self, out, d0, d1, init, R, W):
        return self.OP("dve", lambda h: h.tensor_tensor_scan(out, d0, d1, init, ALU.mult, ALU.add), R, W, self._c("dve", out))

    def DMA(self, q, out, in_, semkey, R, W):
        return self.DM(q, lambda h: h.dma_start(out=out, in_=in_), semkey, R, W)

    def setup(self):
        P = self.P
        self.DMA("sp", self.pv, self.pvec, "c0", (), ["pv"])
        self.DMA("sp", self.cs, self.cst, "c1", (), ["cs"])
        self.TS("dve", self.g32, self.pv[:, 0:72], 32.0, None, ALU.mult, None, ["pv"], ["g32"])
        self.MS("pool", self.ones, 1.0, ["ones"])
        self.MS("pool", self.ident, 1.0, ["ident"])
        ident = self.ident
        self.OP("pool", lambda h: h.affine_select(ident, ident, [[-1, 128]], ALU.is_equal, 0.0, base=0, channel_multiplier=1),
             ["ident"], ["ident"])
        triu = self.triu
        self.MS("pool", triu, 1.0, ["triu"])
        self.OP("pool", lambda h: h.affine_select(triu, triu, [[1, 128]], ALU.is_ge, 0.0, base=0, channel_multiplier=-1),
             ["triu"], ["triu"])
        self.pm32 = self.sb("pm32", [128, 128])
        self.CP("pool", self.pm32, triu, ["triu"], ["pm32"])
        self.MS("pool", self.pm32[0:64, 64:128], 0.0, ["pm32"])
        self.CP("dve", self.pmask, self.pm32, ["pm32"], ["pmask"])
        self.pmask4 = self.sb("pmask4", [128, 4, 128], U8)
        for i_ in range(4):
            self.CP("dve", self.pmask4[:, i_, :], self.pm32, ["pm32", "pmask"], ["pmask"])
        self.MS("pool", self.segm4, 1.0, ["segm"])
        self.MS("pool", self.segm4.rearrange("p (c t) -> p c t", t=64)[:, :, 0:1], 0.0, ["segm"])

    def wkeys(self, lo, hi):
        return [("wa", g) for g in range(lo // 4096, (hi - 1) // 4096 + 1)]

    def load_w(self, off, view_shape, src, nsplit, tag, idx=None):
        a, b = view_shape
        dst = self.arena[:, off:off + a * b].rearrange("p (a b) -> p a b", b=b)
        step = a // nsplit
        for i in (range(nsplit) if idx is None else idx):
            lo = off + i * step * b
            hi = off + (i + 1) * step * b
            self.DMA("pool", dst[:, i * step:(i + 1) * step, :], src[:, i * step:(i + 1) * step, :],
                     (tag, i), (), self.wkeys(lo, hi))
        return dst

    def hxkeys(self, slot):
        return [("hx", slot, m) for m in range(KC)]

    def load_tile(self, src, j):
        slot = self.tilecnt % 3
        self.cur_slot = slot
        srcv = src.rearrange("(k p) t -> p k t", p=128)[:, :, j * T:(j + 1) * T]
        rk = [("hd", j)] if src is self.hbuf else []
        self.DMA("sp", self.hx[slot], srcv, ("hx", slot), rk, self.hxkeys(slot))
        self.tilecnt += 1
        return slot

    def store_tile(self, dst, j, slot):
        dstv = dst.rearrange("(k p) t -> p k t", p=128)[:, :, j * T:(j + 1) * T]
        wk = [("hd", j)] if dst is self.hbuf else [("od", j)]
        o = self.DMA("sp", dstv, self.hx[slot], ("st", slot), self.hxkeys(slot), wk)
        if dst is self.outT:
            self.out_keys.append(("od", j))

    def rstd_from(self, ps, scale, R):
        self.ACT(self.lnt, ps, AF.Ln, R, ["lnt"], bias=self.epsb, scale=scale)
        self.ACT(self.rstd, self.lnt, AF.Exp, ["lnt"], ["rstd"], scale=-0.5)

    def norm(self, slot, gcol, out, outkeys, bufs=None):
        hx = self.hx[slot]
        hk = self.hxkeys(slot)
        if bufs is None:
            bufs = (self.sq, "sq", self.ps_stat[:, 0, :], "ps_stat", self.lnt, "lnt", self.rstd, "rstd")
        sq, sqk, ss, ssk, lnt, lntk, rstd, rstdk = bufs
        self.ACT(sq, hx, AF.Square, hk, [sqk])
        for k in range(KC):
            self.MM(ss, self.ones, sq[:, k, :], k == 0, k == KC - 1, [sqk, "ones"], [ssk])
        self.ACT(lnt, ss, AF.Ln, [ssk], [lntk], bias=self.eps1024)
        self.ACT(rstd, lnt, AF.Exp, [lntk], [rstdk], scale=-0.5)
        for k in range(KC):
            self.STT("dve", out[:, k, :], hx[:, k, :], gcol[:, k:k + 1], rstd, ALU.mult, ALU.mult,
                     [("hx", slot, k), rstdk, "g32"], [outkeys[k]])

    def phase_ffn(self, l, src, dst, last):
        pre = self.w1_prefetched.pop(l, ())
        w1 = self.load_w(0, (KC, 4096), self.w_ffn_in[l].rearrange("(k p) n -> p k n", p=128), 8, "w",
                         idx=[i for i in range(8) if i not in pre])
        w2 = self.load_w(32768, (32, D), self.w_ffn_out[l].rearrange("(k p) n -> p k n", p=128), 8, "w2")
        hid = self.big[:, 0:32 * T].rearrange("p (m t) -> p m t", t=T)
        r32 = [self.sb_r32a, self.sb_r32b]
        gcol = self.g32[:, PV_NFFN + l * 8:PV_NFFN + l * 8 + 8]
        fg = self.g32[:, PV_NFIN:PV_NFIN + 8]
        xns = [self.xn, self.y[1]]
        xnks = [[("xn", k) for k in range(KC)], [("xn2", k) for k in range(KC)]]
        dofin = last and self.final_norm
        n = self.ntiles
        slots = {0: self.load_tile(src, 0)}
        if n > 1:
            slots[1] = self.load_tile(src, 1)
        self.norm(slots[0], gcol, xns[0], xnks[0])

        def norm_parts(slot, out, outkeys):
            hx = self.hx[slot]
            hk = self.hxkeys(slot)
            ss = self.ps_stat[:, 0, :]

            def p1():
                self.ACT(self.sq, hx, AF.Square, hk, ["sq"])

            def p2():
                for k in range(KC):
                    self.MM(ss, self.ones, self.sq[:, k, :], k == 0, k == KC - 1, ["sq", "ones"], ["ps_stat"])

            def p3():
                self.ACT(self.lnt, ss, AF.Ln, ["ps_stat"], ["lnt"], bias=self.eps1024)
                self.ACT(self.rstd, self.lnt, AF.Exp, ["lnt"], ["rstd"], scale=-0.5)

            def p4():
                for k in range(KC):
                    self.STT("dve", out[:, k, :], hx[:, k, :], gcol[:, k:k + 1], self.rstd, ALU.mult, ALU.mult,
                             [("hx", slot, k), "rstd", "g32"], [outkeys[k]])
            return p1, p2, p3, p4

        def hidden(j, lo, hi):
            xn = xns[j % 2]
            xk = xnks[j % 2]
            for m2 in range(lo, hi):
                pj = self.ps_pj[m2 % 2]
                pk = f"pj{m2 % 2}"
                for h2 in range(2):
                    m = m2 * 2 + h2
                    for k in range(KC):
                        self.MM(pj[:, h2, :], w1[:, k, m * 128:(m + 1) * 128], xn[:, k, :], k == 0, k == KC - 1,
                                [xk[k], ("wa", k)], [pk])
                rb = r32[m2 % 2]
                rk = f"r32{m2 % 2}"
                self.ACT(rb, pj, AF.Relu, [pk], [rk])
                self.TT("pool", hid[:, 2 * m2:2 * m2 + 2, :], rb, rb, ALU.mult, [rk], [("hid", m2)])

        for j in range(n):
            slot = slots[j]
            if j + 1 < n:
                p1, p2, p3, p4 = norm_parts(slots[j + 1], xns[(j + 1) % 2], xnks[(j + 1) % 2])
            else:
                p1 = p2 = p3 = p4 = (lambda: None)
            hidden(j, 0, 4)
            p1()
            hidden(j, 4, 8)
            p2()
            p3()
            hidden(j, 8, 12)
            p4()
            hidden(j, 12, 16)
            if j == n - 1 and self.next_mix is not None:
                lm = self.next_mix
                jm = lm // 2
                if lm % 2 == 0:
                    self.load_w(0, (KC, 3072), self.w_in_even[jm].rearrange("(k p) n -> p k n", p=128), 8, "w")
                    self.load_w(24576, (KC, D), self.w_out_even[jm].rearrange("(k p) n -> p k n", p=128), 2, "wo")
                else:
                    self.load_w(0, (KC, 1536), self.w_in_odd[jm].rearrange("(k p) n -> p k n", p=128), 8, "w")
                    self.load_w(24576, (KC, D), self.w_out_odd[jm].rearrange("(k p) n -> p k n", p=128), 2, "wo")
                self.mix_prefetched.add(lm)
            if j + 2 < n:
                slots[j + 2] = self.load_tile(src, j + 2)
            for m in range(KC):
                po = self.ps_po[m % 2][:, 0:T]
                pok = f"po{m % 2}"
                for k in range(32):
                    self.MM(po, w2[:, k, m * 128:(m + 1) * 128], hid[:, k, :], k == 0, k == 31,
                            [("hid", k // 2), ("wa", 8 + k // 4)], [pok])
                self.TT("dve", self.hx[slot][:, m, :], po, self.hx[slot][:, m, :], ALU.add,
                        [pok, ("hx", slot, m)], [("hx", slot, m)])
            if dofin:
                self.norm(slot, fg, self.hx[slot], self.hxkeys(slot))
            self.store_tile(dst, j, slot)

    def out_proj(self, wout, slot, yb=None, yp=0):
        if yb is None:
            yb = self.y[0]
        for m in range(KC):
            po = self.ps_po[m % 2][:, 0:T]
            pok = f"po{m % 2}"
            for k in range(KC):
                self.MM(po, wout[:, k, m * 128:(m + 1) * 128], yb[:, k, :], k == 0, k == KC - 1,
                        [("y", yp, k), ("wa", 6 + k // 4)], [pok])
            self.TT("dve", self.hx[slot][:, m, :], po, self.hx[slot][:, m, :], ALU.add,
                    [pok, ("hx", slot, m)], [("hx", slot, m)])

    def pipeline(self, src, dst, A, B):
        n = self.ntiles
        self.sched_reset()
        slots = {0: self.load_tile(src, 0)}
        if n > 1:
            slots[1] = self.load_tile(src, 1)
        st0 = [self.collect(pl) for pl in A(0, slots[0])]
        self.sched_merge(st0, {i: list(range(i)) for i in range(1, len(st0))})
        for j in range(n):
            if j + 2 < n:
                slots[j + 2] = self.load_tile(src, j + 2)
            if j == n - 1 and self.next_ffn is not None:
                lf = self.next_ffn
                self.load_w(0, (KC, 4096), self.w_ffn_in[lf].rearrange("(k p) n -> p k n", p=128), 8, "w", idx=range(6))
                self.w1_prefetched[lf] = set(range(6))
            streams = [self.collect(B(j, slots[j]))]
            if j + 1 < n:
                streams += [self.collect(pl) for pl in A(j + 1, slots[j + 1])]
            self.sched_merge(streams, {i: list(range(1, i)) for i in range(2, len(streams))})
            self.store_tile(dst, j, slots[j])

    def abuf(self, off, n, dt=BF16):
        if dt == F32:
            return self.arena[:, off:off + 2 * n].bitcast(F32), off + 2 * n
        return self.arena[:, off:off + n], off + n

    def phase_odd(self, l, src, dst):
        j_ = l // 2
        P = self.P
        pf = [] if l in self.mix_prefetched else None
        win = self.load_w(0, (KC, 1536), self.w_in_odd[j_].rearrange("(k p) n -> p k n", p=128), 8, "w", idx=pf)
        wout = self.load_w(24576, (KC, D), self.w_out_odd[j_].rearrange("(k p) n -> p k n", p=128), 2, "wo", idx=pf)
        off = 33792
        wsg32, off = self.abuf(off, 512, F32)
        wsg32 = wsg32.rearrange("p (g t) -> p g t", t=128)
        self.DMA("sp", wsg32, self.sguwT[j_].rearrange("g s t -> s g t"), "c2", (), ["wsg32"])
        wsg = self.arena[:, 32768:32768 + 512].rearrange("p (g t) -> p g t", t=128)
        self.TT("dve", wsg, wsg32, self.triu.unsqueeze(1).to_broadcast([128, 4, 128]), ALU.mult,
                ["wsg32", "triu"], [("wa", 8)])
        wpl = self.arena[:, 32768 + 512:32768 + 1024].rearrange("p (g t) -> p g t", t=128)
        self.DMA("pool", wpl, self.poolw[j_].rearrange("g c d -> c g d"), "c3", (), [("wa", 8)])
        sbb, off = self.abuf(off, 512, F32)
        self.DMA("sp", sbb, self.sgub[j_].partition_broadcast(128), "c4", (), ["sbb"])
        gcol = self.g32[:, PV_NMIX + l * 8:PV_NMIX + l * 8 + 8]
        psc = self.pv[:, PV_ODD + j_ * 4:PV_ODD + j_ * 4 + 4]
        xnk = [("xn", k) for k in range(KC)]
        ug, vn, pp = [], [], []
        for p in range(2):
            a, off = self.abuf(off, 4 * T)
            ug.append(a.rearrange("p (c t) -> p c t", t=T))
            a, off = self.abuf(off, 1024)
            vn.append(a.rearrange("p (s c) -> p s c", c=512))
            a, off = self.abuf(off, 4 * (16 + T), F32)
            pp.append(a.rearrange("p (g t) -> p g t", t=16 + T))
        vgs = []
        for i_ in range(2):
            a, off = self.abuf(off, 512, F32)
            vgs.append(a)
        mv8 = self.sb_mv8
        pooled, off = self.abuf(off, 4 * T)
        pooled = pooled.rearrange("p (g t) -> p g t", t=T)
        ws = []
        for i in range(2):
            a, off = self.abuf(off, 4 * (T + 16), F32)
            ws.append(a.rearrange("p (g t) -> p g t", t=T + 16))
        svt, off = self.abuf(off, 512, F32)
        svt = svt.rearrange("p (g t) -> p g t", t=128)
        assert off <= 65536
        st6 = self.sb_st6
        mv = self.sb_mv
        L = T + 16
        inw = lambda k: [("wa", (k * 1536) // 4096), ("wa", (k * 1536 + 1535) // 4096)]

        def A(j, slot):
            p = j % 2
            first = (j % self.tiles_per_seq) == 0
            pcs = []

            def a_norm():
                if first:
                    self.MS("pool", pp[p][:, :, 0:16], 0.0, [("pp", p)])
                else:
                    self.CP("pool", pp[p][:, :, 0:16], pp[1 - p][:, :, T:T + 16], [("pp", 1 - p)], [("pp", p)])
                self.norm(slot, gcol, self.xn, xnk)
            pcs.append(a_norm)

            def pair(m0, evac, i):
                def f():
                    pj = self.ps_pj[i]
                    pk = f"pj{i}"
                    for h2 in range(2):
                        m = m0 + h2
                        for k in range(KC):
                            self.MM(pj[:, h2, :], win[:, k, m * 128:(m + 1) * 128], self.xn[:, k, :], k == 0, k == KC - 1,
                                    [("xn", k)] + inw(k), [pk])
                    evac(pj, pk)
                return f
            for m2 in range(2):
                pcs.append(pair(2 * m2, lambda pj, pk, m2=m2: self.ACT(ug[p][:, 2 * m2:2 * m2 + 2, :], pj, AF.Gelu_apprx_tanh,
                                                                        [pk], [("ug", p)]), m2))
            for m2 in range(2):
                pcs.append(pair(8 + 2 * m2, lambda pj, pk, m2=m2: self.CP("act", pp[p][:, 2 * m2:2 * m2 + 2, 16:16 + T], pj,
                                                                            [pk], [("pp", p)]), m2))

            def vsub(sub):
                def f():
                    for k in range(KC):
                        self.MM(self.ps_v, self.xn[:, k, sub * 128:(sub + 1) * 128], win[:, k, 512:1024], k == 0, k == KC - 1,
                                [("xn", k)] + inw(k), ["ps_v"])
                    self.ACT(vgs[sub], self.ps_v, AF.Gelu_apprx_tanh, ["ps_v"], [("vg", sub)])
                    for g in range(4):
                        vgg = vgs[sub][:, g * 128:(g + 1) * 128]
                        self.OP("dve", (lambda h, o=st6[:, g, :], i=vgg: h.bn_stats(o, i)), [("vg", sub)], [("st6", g)])
                        self.OP("dve", (lambda h, o=mv8[:, sub * 4 + g, :], i=st6[:, g, :]: h.bn_aggr(o, i)), [("st6", g)], ["mv"])
                return f
            pcs.append(vsub(0))
            pcs.append(vsub(1))

            def vfin():
                self.ACT(self.sb_l8, mv8[:, :, 1], AF.Ln, ["mv"], ["l4"], bias=self.epsb)
                self.ACT(self.sb_r8, self.sb_l8, AF.Exp, ["l4"], ["r4"], scale=-0.5)
                for sub in range(2):
                    for g in range(4):
                        self.TS("dve", vn[p][:, sub, g * 128:(g + 1) * 128], vgs[sub][:, g * 128:(g + 1) * 128],
                                mv8[:, sub * 4 + g, 0:1], self.sb_r8[:, sub * 4 + g:sub * 4 + g + 1], ALU.subtract, ALU.mult,
                                [("vg", sub), "mv", "r4"], [("vn", p)])
            pcs.append(vfin)
            return [pcs]

        def B(j, slot):
            p = j % 2
            first = (j % self.tiles_per_seq) == 0
            pcs = []

            def sgu(sub):
                def f():
                    svp = self.ps_ao.rearrange("p a t -> p (a t)").rearrange("p (g t) -> p g t", t=128)
                    for g in range(4):
                        self.MM(svp[:, g, :], vn[p][:, sub, g * 128:(g + 1) * 128], wsg[:, g, :], True, True,
                                [("vn", p), ("wa", 8)], ["ps_ao"])
                    self.TT("dve", svt, svp, sbb.rearrange("p (g t) -> p g t", t=128), ALU.add, ["ps_ao", "sbb"], ["svt"])
                    self.TT("dve", self.y[0][:, 0:4, sub * 128:(sub + 1) * 128], svt, ug[p][:, :, sub * 128:(sub + 1) * 128],
                            ALU.mult, ["svt", ("ug", p)], [("y", 0, 0), ("y", 0, 1), ("y", 0, 2), ("y", 0, 3)])
                return f
            pcs.append(sgu(0))
            pcs.append(sgu(1))

            def pool_():
                a = pp[p]
                self.TT("dve", ws[0][:, :, 1:L], a[:, :, 1:L], a[:, :, 0:L - 1], ALU.add, [("pp", p)], ["ws0"])
                self.TT("dve", ws[1][:, 1:4, 3:L], ws[0][:, 1:4, 3:L], ws[0][:, 1:4, 1:L - 2], ALU.add, ["ws0"], ["ws1"])
                self.TT("dve", ws[0][:, 2:4, 7:L], ws[1][:, 2:4, 7:L], ws[1][:, 2:4, 3:L - 4], ALU.add, ["ws1", "ws0"], ["ws0b"])
                self.TT("dve", ws[1][:, 3, 15:L], ws[0][:, 3, 15:L], ws[0][:, 3, 7:L - 8], ALU.add, ["ws0b", "ws1"], ["ws1b"])
                finals = [(ws[0], ["ws0"]), (ws[1], ["ws1"]), (ws[0], ["ws0b"]), (ws[1], ["ws1b"])]
                for g in range(4):
                    win_ = 2 << g
                    cur, ck = finals[g]
                    cur = cur[:, g, :]
                    if first:
                        self.TS("dve", svt.rearrange("p g t -> p (g t)")[:, 0:T], cur[:, 16:L], 1.0 / win_, None, ALU.mult, None,
                                ck, ["svt"])
                        self.TT("dve", svt.rearrange("p g t -> p (g t)")[:, 0:16], cur[:, 16:32], self.cs[:, g * 16:(g + 1) * 16],
                                ALU.mult, ck + ["cs", "svt"], ["svt"])
                        self.TT("dve", pooled[:, g, :], svt.rearrange("p g t -> p (g t)")[:, 0:T], a[:, g, 16:L], ALU.subtract,
                                ["svt", ("pp", p)], [("pooled", g)])
                    else:
                        self.STT("dve", pooled[:, g, :], cur[:, 16:L], 1.0 / win_, a[:, g, 16:L], ALU.mult, ALU.subtract,
                                 ck + [("pp", p)], [("pooled", g)])
            pcs.append(pool_)

            def pd_(g2):
                def f():
                    pd = self.ps_v.rearrange("p (a t) -> p a t", t=T)
                    for h2 in range(2):
                        g = 2 * g2 + h2
                        self.MM(pd[:, h2, :], wpl[:, g, :], pooled[:, g, :], True, True, [("pooled", g), ("wa", 8)], ["ps_v"])
                    for h2 in range(2):
                        g = 2 * g2 + h2
                        self.OP("act", (lambda h, o=self.y[0][:, 4 + g, :], i_=pd[:, h2, :], s_=psc[:, g:g + 1]: h.mul(o, i_, s_)),
                             ["ps_v", "pv"], [("y", 0, 4 + g)])
                return f
            pcs.append(pd_(0))
            pcs.append(pd_(1))
            pcs.append(lambda: self.out_proj(wout, slot))
            return pcs

        self.pipeline(src, dst, A, B)

    def even_consts(self, j_):
        b = PV_EVEN + j_ * 44
        pv = self.pv
        ec = self.sb_ec
        k = f"ec"
        self.ACT(ec[:, 0:4], pv[:, b + E_LAM:b + E_LAM + 4], AF.Exp, ["pv"], [k], scale=-1.0)
        self.ACT(ec[:, 4:8], ec[:, 0:4], AF.Ln, [k], [k], bias=self.oneb)
        self.TS("dve", ec[:, 8:12], ec[:, 4:8], -4.0, None, ALU.mult, None, [k], [k])
        self.TS("dve", ec[:, 12:16], ec[:, 4:8], -8.0, None, ALU.mult, None, [k], [k])
        self.TS("dve", ec[:, 16:20], pv[:, b + E_BA:b + E_BA + 4], 0.5, None, ALU.mult, None, ["pv", k], [k])
        self.TS("dve", ec[:, 20:24], pv[:, b + E_BX:b + E_BX + 4], 0.5, None, ALU.mult, None, ["pv", k], [k])
        if j_ == 0:
            self.MS("dve", ec[:, 24:28], 0.0, [k])
        else:
            self.ACT(ec[:, 28:32], pv[:, b + E_LBL0:b + E_LBL0 + 4], AF.Exp, ["pv", k], [k])
            self.ACT(ec[:, 32:36], pv[:, b + E_LBL1:b + E_LBL1 + 4], AF.Exp, ["pv", k], [k])
            self.TT("dve", ec[:, 28:32], ec[:, 28:32], ec[:, 32:36], ALU.add, [k], [k])
            self.OP("dve", lambda h: h.reciprocal(ec[:, 28:32], ec[:, 28:32]), [k], [k])
            self.TT("dve", ec[:, 24:28], ec[:, 32:36], ec[:, 28:32], ALU.mult, [k], [k])
        self.TS("dve", ec[:, 28:32], ec[:, 24:28], 0.5, 0.5, ALU.mult, ALU.add, [k], [k])
        self.TS("dve", ec[:, 32:36], ec[:, 24:28], -0.5, 0.5, ALU.mult, ALU.add, [k], [k])
        self.TS("dve", ec[:, 36:40], ec[:, 32:36], -1.0, None, ALU.mult, None, [k], [k])

    def phase_even(self, l, src, dst):
        j_ = l // 2
        P = self.P
        pf = [] if l in self.mix_prefetched else None
        win = self.load_w(0, (KC, 3072), self.w_in_even[j_].rearrange("(k p) n -> p k n", p=128), 8, "w", idx=pf)
        wout = self.load_w(24576, (KC, D), self.w_out_even[j_].rearrange("(k p) n -> p k n", p=128), 2, "wo", idx=pf)
        wg = self.arena[:, 32768:32768 + 1024].rearrange("p (a c t) -> p a c t", a=2, c=4)
        self.DMA("pool", wg, self.wgate[j_].rearrange("a c i o -> i a c o"), "c3", (), [("wa", 8)])
        self.even_consts(j_)
        ec = self.sb_ec
        pvb = PV_EVEN + j_ * 44
        pv = self.pv
        gcol = self.g32[:, PV_NMIX + l * 8:PV_NMIX + l * 8 + 8]
        xnk = [("xn", k) for k in range(KC)]
        off = 33792
        c4 = lambda a: a.rearrange("p (c t) -> p c t", t=T)
        gg, qf, sg, tz, vt, xa = [], [], [], [], [], []
        for p in range(2):
            a, off = self.abuf(off, 4 * T); gg.append(c4(a))
            a, off = self.abuf(off, 4 * T); qf.append(c4(a))
            a, off = self.abuf(off, 4 * T); sg.append(c4(a))
            a, off = self.abuf(off, 4 * T, F32); tz.append(c4(a))
            a, off = self.abuf(off, 1024); vt.append(a.rearrange("p (s c) -> p s c", c=512))
            a, off = self.abuf(off, 4 * (4 + T), F32); xa.append(a.rearrange("p (c t) -> p c t", t=4 + T)[:, :, 1:4 + T])
        ua, off = self.abuf(off, 4 * T, F32); ua = c4(ua)
        tr, off = self.abuf(off, 4 * T, F32); tr = c4(tr)
        ti, off = self.abuf(off, 4 * T, F32); ti = c4(ti)
        aa, off = self.abuf(off, 4 * T, F32); aa = c4(aa)
        a2, off = self.abuf(off, 4 * T, F32); a2 = c4(a2)
        uab, off = self.abuf(off, 4 * T); uab = c4(uab)
        kk, off = self.abuf(off, 4 * T, F32); kk = c4(kk)
        e1, off = self.abuf(off, 4 * T); e1 = c4(e1)
        assert off <= 65536, off
        boff = [0]

        def bbuf(n, dt=BF16):
            if dt == F32:
                r = self.big[:, boff[0]:boff[0] + 2 * n].bitcast(F32)
                boff[0] += 2 * n
            else:
                r = self.big[:, boff[0]:boff[0] + n]
                boff[0] += n
            return r
        logf = c4(bbuf(4 * T, F32))
        bc = c4(bbuf(4 * T, F32))
        osb = c4(bbuf(4 * T, F32))
        qt = c4(bbuf(4 * T))
        kt = c4(bbuf(4 * T))
        assert boff[0] <= 8192
        S = self.sb_S
        St = self.sb_St
        Sb = self.sb_Sb
        ktT = self.sb_ktT
        Asb = self.sb_Asb
        sc = self.sb_sc
        esc = self.sb_esc
        hst = self.sb_hst
        osq = self.sb_osq
        rs2 = self.sb_rs2
        self.MS("pool", Asb, 0.0, ["Asb"])
        inw = lambda k, m: [("wa", (k * 3072 + m * 128) // 4096)]
        Skeys = [("S", h_) for h_ in range(4)]
        uak = [("ua", c) for c in range(4)]

        def A(j, slot):
            p = j % 2
            first = (j % self.tiles_per_seq) == 0
            yb = self.y[p]
            pcs = []

            def a_norm():
                if first:
                    self.MS("pool", xa[p][:, :, 0:3], 0.0, [("xa", p)])
                else:
                    self.CP("pool", xa[p][:, :, 0:3], xa[1 - p][:, :, T:T + 3], [("xa", 1 - p)], [("xa", p)])
                self.norm(slot, gcol, self.xn, xnk)
            pcs.append(a_norm)
            cnt = [0]

            def pair(m0, evac):
                def f():
                    i = cnt[0] % 2
                    cnt[0] += 1
                    pj = self.ps_pj[i]
                    pk = f"pj{i}"
                    for h2 in range(2):
                        m = m0 + h2
                        for k in range(KC):
                            self.MM(pj[:, h2, :], win[:, k, m * 128:(m + 1) * 128], self.xn[:, k, :], k == 0, k == KC - 1,
                                    [("xn", k)] + inw(k, m), [pk])
                    evac(pj, pk)
                return f

            def lru1():
                if first:
                    self.MS("dve", hst, 0.0, ["hst"])
                for c in range(4):
                    cw = lambda k_: pv[:, pvb + E_CONVW + k_ * 4 + c:pvb + E_CONVW + k_ * 4 + c + 1]
                    self.TS("dve", ua[:, c, :], xa[p][:, c, 0:T], cw(0), pv[:, pvb + E_CONVB + c:pvb + E_CONVB + c + 1],
                            ALU.mult, ALU.add, [("xa", p), "pv"], [("ua", c)])
                    for k_ in range(1, 4):
                        self.STT("dve", ua[:, c, :], xa[p][:, c, k_:k_ + T], cw(k_), ua[:, c, :], ALU.mult, ALU.add,
                                 [("xa", p), "pv", ("ua", c)], [("ua", c)])
                self.CP("act", uab, ua, uak, ["uab"])

            def lru2():
                for c in range(4):
                    pj = self.ps_v.rearrange("p (a t) -> p a t", t=T)
                    pk = "ps_v"
                    self.MM(pj[:, 0, :], wg[:, 0, c, :], uab[:, c, :], True, True, ["uab", ("wa", 8)], [pk])
                    self.MM(pj[:, 1, :], wg[:, 1, c, :], uab[:, c, :], True, True, ["uab", ("wa", 8)], [pk])
                    self.ACT(tr[:, c, :], pj[:, 0, :], AF.Tanh, [pk, "ec"], ["tr"], bias=ec[:, 16 + c:17 + c], scale=0.5)
                    self.ACT(ti[:, c, :], pj[:, 1, :], AF.Tanh, [pk, "ec"], ["ti"], bias=ec[:, 20 + c:21 + c], scale=0.5)
                for c in range(4):
                    self.ACT(aa[:, c, :], tr[:, c, :], AF.Exp, ["tr", "ec"], ["aa"], bias=ec[:, 8 + c:9 + c], scale=ec[:, 8 + c:9 + c])
                    self.ACT(a2[:, c, :], tr[:, c, :], AF.Exp, ["tr", "ec"], ["a2"], bias=ec[:, 12 + c:13 + c], scale=ec[:, 12 + c:13 + c])
                self.TS("dve", a2, a2, 1.0, -1.0, ALU.min, ALU.mult, ["a2"], ["a2"])
                self.ACT(a2, a2, AF.Ln, ["a2"], ["a2"], bias=self.oneb)
                self.ACT(a2, a2, AF.Exp, ["a2"], ["a2"], scale=0.5)
                self.STT("dve", ti, ti, 1.0, ua, ALU.add, ALU.mult, ["ti"] + uak, ["ti"])
                self.STT("dve", ti, ti, 0.5, a2, ALU.mult, ALU.mult, ["ti", "a2"], ["ti"])

            def lru3():
                for c in range(4):
                    self.SCAN(tr[:, c, :], aa[:, c, :], ti[:, c, :], hst[:, c:c + 1], ["aa", "ti", "hst", "tr"], ["tr"])
                self.CP("dve", hst, tr[:, :, T - 1], ["tr"], ["hst"])
                self.TT("dve", yb[:, 0:4, :], tr, gg[p], ALU.mult, ["tr", ("gg", p)], [("y", p, c) for c in range(4)])

            for c2 in range(2):
                pcs.append(pair(2 * c2, lambda pj, pk, c2=c2: self.CP("act", xa[p][:, 2 * c2:2 * c2 + 2, 3:3 + T], pj, [pk], [("xa", p)])))
            for c2 in range(2):
                pcs.append(pair(4 + 2 * c2, lambda pj, pk, c2=c2: self.ACT(gg[p][:, 2 * c2:2 * c2 + 2, :], pj, AF.Gelu_apprx_tanh, [pk], [("gg", p)])))
            for c2 in range(2):
                pcs.append(pair(12 + 2 * c2, lambda pj, pk, c2=c2: self.ACT(tz[p][:, 2 * c2:2 * c2 + 2, :], pj, AF.Tanh, [pk], [("tz", p)], scale=0.5)))
            for c2 in range(2):
                pcs.append(pair(8 + 2 * c2, lambda pj, pk, c2=c2: self.ACT(qf[p][:, 2 * c2:2 * c2 + 2, :], pj, AF.Silu, [pk], [("qf", p)])))
            for c2 in range(2):
                pcs.append(pair(20 + 2 * c2, lambda pj, pk, c2=c2: self.ACT(sg[p][:, 2 * c2:2 * c2 + 2, :], pj, AF.Silu, [pk], [("sg", p)])))

            def vsub(sub):
                def f():
                    i = cnt[0] % 2
                    cnt[0] += 1
                    pjv = self.ps_pj[i].rearrange("p a t -> p (a t)")
                    pk = f"pj{i}"
                    for k in range(KC):
                        lo = k * 3072 + 2048
                        self.MM(pjv, self.xn[:, k, sub * 128:(sub + 1) * 128], win[:, k, 2048:2560], k == 0, k == KC - 1,
                                [("xn", k), ("wa", lo // 4096), ("wa", (lo + 511) // 4096)], [pk])
                    self.CP("dve", vt[p][:, sub, :], pjv, [pk], [("vt", p)])
                return f
            pcs.append(vsub(0))
            pcs.append(vsub(1))
            return [pcs, [lru1, lru2, lru3]]

        def B(j, slot):
            p = j % 2
            first = (j % self.tiles_per_seq) == 0
            yb = self.y[p]
            pcs = []

            def hg1():
                if first:
                    self.MS("dve", S, 0.0, Skeys)
                for hd in range(4):
                    self.ACT(logf[:, hd, :], tz[p][:, hd, :], AF.Ln, [("tz", p), "ec"], ["logf"],
                             bias=ec[:, 28 + hd:29 + hd], scale=ec[:, 32 + hd:33 + hd])
                for hd in range(4):
                    self.TS("dve", kk[:, hd, :], tz[p][:, hd, :], ec[:, 36 + hd:37 + hd], ec[:, 32 + hd:33 + hd], ALU.mult, ALU.add,
                            [("tz", p), "ec"], ["kk"])
                flat = lambda a: a.rearrange("p c t -> p (c t)")
                self.SCAN(flat(bc), self.segm4, flat(logf), 0.0, ["segm", "logf"], ["bc"])
                bcv = flat(bc).rearrange("p (c t) -> p c t", t=64)
                self.TT("dve", flat(logf).rearrange("p (c t) -> p c t", t=64), bcv, bcv[:, :, 31:32].to_broadcast([128, 16, 64]),
                        ALU.subtract, ["bc", "logf"], ["logf"])
                self.CP("dve", sc[:, 0, :], bcv[:, :, 31], ["bc"], ["sc"])
                self.CP("dve", sc[:, 1, :], bcv[:, :, 63], ["bc", "sc"], ["sc"])
                self.TT("dve", sc[:, 2, :], sc[:, 1, :], sc[:, 0, :], ALU.subtract, ["sc"], ["sc"])
                self.ACT(e1, logf, AF.Exp, ["logf"], ["e1"])
                self.ACT(logf, logf, AF.Exp, ["logf"], ["logf"], scale=-1.0)
                self.ACT(esc, sc, AF.Exp, ["sc"], ["esc"])
                self.TT("dve", qt, qf[p], e1, ALU.mult, [("qf", p), "e1"], ["qt"])
                self.TT("dve", kt, kk, logf, ALU.mult, ["kk", "logf"], ["kt"])
            pcs.append(hg1)

            def hg2(hp):
                def f():
                    h0 = 2 * hp
                    ktp = self.ps_misc[:, 256:512].bitcast(BF16).rearrange("p (h r t) -> p h r t", h=2, r=2)
                    for hh in range(2):
                        for pr in range(2):
                            self.TR(ktp[:, hh, pr, :], kt[:, h0 + hh, pr * 128:(pr + 1) * 128], ["kt"], ["ps_misc"])
                    self.CP("act", ktT, ktp, ["ps_misc"], ["ktT"])
                    aps = self.ps_ao.rearrange("p a t -> p (a t)").rearrange("p (h r t) -> p h r t", h=2, r=2)
                    for hh in range(2):
                        for pr in range(2):
                            self.MM(aps[:, hh, pr, :], kt[:, h0 + hh, pr * 128:(pr + 1) * 128], qt[:, h0 + hh, pr * 128:(pr + 1) * 128],
                                    True, True, ["kt", "qt"], ["ps_ao"])
                    pm = self.pmask4
                    self.OP("dve", (lambda h, o=Asb.rearrange("p h r t -> p (h r) t"), m_=pm, d_=aps.rearrange("p h r t -> p (h r) t"):
                                 h.copy_predicated(o, m_, d_)), ["ps_ao", "pmask", "Asb"], ["Asb"])
                    ob = self.ps_po[1].rearrange("p (h t) -> p h t", h=2)
                    firstmm = [True]

                    def omm(out, lhsT, rhs, R):
                        st = firstmm[0]
                        firstmm[0] = False
                        self.OP("pe", (lambda h, o=out, l_=lhsT, r_=rhs, st=st: h.matmul(o, l_, r_, start=st, stop=False, skip_group_check=True)),
                                R, ["po1"], 0.035 + self._fd(out) / 2100.0)
                    for ch in range(4):
                        pr, half = ch // 2, ch % 2
                        for hh in range(2):
                            hd = h0 + hh
                            ei = hd * 4 + ch
                            if half == 0:
                                omm(ob[:, hh, pr * 128:(pr + 1) * 128], vt[p][:, pr, hd * 128:(hd + 1) * 128], Asb[:, hh, pr, :],
                                    [("vt", p), "Asb"])
                            self.OP("act", (lambda h, o=Sb[:, hd, :], i_=S[:, hd, :], s_=esc[:, 0, ei:ei + 1]: h.mul(o, i_, s_)),
                                 [("S", hd), "esc"], [("Sb", hd)])
                            omm(ob[:, hh, ch * 64:(ch + 1) * 64], Sb[:, hd, :], qt[:, hd, ch * 64:(ch + 1) * 64], [("Sb", hd), "qt"])
                        for hh in range(2):
                            hd = h0 + hh
                            ei = hd * 4 + ch
                            up = self.ps_u[hh]
                            self.MM(up, ktT[half * 64:(half + 1) * 64, hh, pr, :],
                                    vt[p][half * 64:(half + 1) * 64, pr, hd * 128:(hd + 1) * 128],
                                    True, True, ["ktT", ("vt", p)], ["ps_misc"])
                            self.TS("dve", St[:, hd, :], S[:, hd, :], esc[:, 1, ei:ei + 1], None, ALU.mult, None,
                                    [("S", hd), "esc"], [("St", hd)])
                            self.STT("dve", S[:, hd, :], up, esc[:, 2, ei:ei + 1], St[:, hd, :], ALU.mult, ALU.add,
                                     ["ps_misc", ("St", hd), "esc"], [("S", hd)])
                    self.ACT(osq, ob, AF.Square, ["po1"], ["osq"])
                    self.CP("act", osb[:, h0:h0 + 2, :], ob, ["po1"], [("osb", hp)])
                    osp = self.ps_ao
                    for hh in range(2):
                        self.MM(osp[:, hh, :], self.ones, osq[:, hh, :], True, True, ["osq", "ones"], ["ps_ao"])
                    self.ACT(rs2, osp, AF.Ln, ["ps_ao"], ["rs2"], bias=self.epsb, scale=1.0 / 128.0)
                    self.ACT(rs2, rs2, AF.Exp, ["rs2"], ["rs2"], scale=-0.5)
                    for hh in range(2):
                        hd = h0 + hh
                        self.STT("dve", osb[:, hd, :], osb[:, hd, :], pv[:, pvb + E_HN + hd:pvb + E_HN + hd + 1], rs2[:, hh, :],
                                 ALU.mult, ALU.mult, [("osb", hp), "rs2", "pv"], [("osb", hp)])
                return f
            pcs.append(hg2(0))
            pcs.append(hg2(1))

            def fin():
                self.TT("dve", yb[:, 4:8, :], osb, sg[p], ALU.mult, [("osb", 0), ("osb", 1), ("sg", p)],
                        [("y", p, 4 + h_) for h_ in range(4)])
                self.out_proj(wout, slot, yb, p)
            pcs.append(fin)
            return pcs

        self.pipeline(src, dst, A, B)

    def build(self):
        sb = self.sb
        yf = self.y[0].rearrange("p k t -> p (k t)")
        self.sb_r32a = yf[:, 0:1024].bitcast(F32).rearrange("p (a t) -> p a t", t=T)
        self.sb_r32b = yf[:, 1024:2048].bitcast(F32).rearrange("p (a t) -> p a t", t=T)
        self.sb_st6 = sb("st6", [128, 4, 6])
        self.sb_mv = sb("mv", [128, 4, 2])
        self.sb_l8 = sb("l8", [128, 8])
        self.sb_r8 = sb("r8", [128, 8])
        self.sb_mv8 = sb("mv8", [128, 8, 2])
        self.sb_ec = sb("ec", [128, 40])
        self.sb_hst = sb("hst", [128, 4])
        self.sb_Sb = sb("Sb", [128, 4, 128], BF16)
        self.sb_ktT = sb("ktT", [128, 2, 2, 128], BF16)
        self.sb_Asb = sb("Asb", [128, 2, 2, 128], BF16)
        self.sb_sc = sb("sc", [128, 3, 16])
        self.sb_esc = sb("esc", [128, 3, 16])
        self.sb_S = sb("S", [128, 4, 128])
        self.sb_St = sb("St", [128, 4, 128])
        self.sb_osq = sb("osq", [128, 2, T], BF16)
        self.sb_rs2 = sb("rs2", [128, 2, T])
        self.segm4 = sb("segm4", [128, 4 * T])
        self.sb_bias = sb("biasc", [128, 4])
        self.ps_u = [self.ps_misc[:, 0:128], self.ps_misc[:, 128:256]]
        self.ps_ktT = self.ps_misc[:, 256:384].bitcast(BF16).rearrange("p (r t) -> p r t", t=128)
        self.out_keys = []
        self.w1_prefetched = {}
        self.next_ffn = None
        self.next_mix = None
        self.mix_prefetched = set()
        self.MS("dve", self.sb_bias[:, 0:1], EPS, ["biasc"])
        self.MS("dve", self.sb_bias[:, 1:2], 1.0, ["biasc"])
        self.MS("dve", self.sb_bias[:, 2:3], 1024.0 * EPS, ["biasc"])
        self.epsb = self.sb_bias[:, 0:1]
        self.oneb = self.sb_bias[:, 1:2]
        self.eps1024 = self.sb_bias[:, 2:3]
        self.setup()
        n = len(self.phases)
        for i, (kind, l) in enumerate(self.phases):
            src = self.xT if i == 0 else self.hbuf
            dst = self.outT if i == n - 1 else self.hbuf
            if i > 0:
                self.P.barrier()
                self.P.new_epoch()
            self.next_ffn = self.phases[i + 1][1] if (i + 1 < n and self.phases[i + 1][0] == "ffn" and kind != "ffn") else None
            self.next_mix = self.phases[i + 1][1] if (i + 1 < n and self.phases[i + 1][0] == "mix" and kind == "ffn") else None
            if kind == "ffn":
                self.phase_ffn(l, src, dst, last=(i == n - 1))
            elif l % 2 == 0:
                self.phase_even(l, src, dst)
            else:
                self.phase_odd(l, src, dst)
        self.P.fence("sp", (), self.out_keys)
        with self.nc.allow_low_precision("bf16 matmul operands, fp32 accumulation"):
            self.P.emit()
        return self.nc


ALL_PHASES = [("mix", 0), ("ffn", 0), ("mix", 1), ("ffn", 1), ("mix", 2), ("ffn", 2), ("mix", 3), ("ffn", 3)]


def prep_shared(inp):
    f = lambda a: np.ascontiguousarray(np.asarray(a, np.float32))
    return {
        "w_in_even": f(inp["w_in_even"]), "w_out_even": f(inp["w_out_even"]),
        "w_in_odd": f(inp["w_in_odd"]), "w_out_odd": f(inp["w_out_odd"]),
        "w_ffn_in": f(inp["w_ffn_in"]), "w_ffn_out": f(inp["w_ffn_out"]),
        "wgate": pack_gates(inp),
        "sguwT": f(np.transpose(np.asarray(inp["sgu_w"], np.float32), (0, 1, 3, 2))),
        "sgub": f(np.asarray(inp["sgu_b"], np.float32).reshape(2, 1, 512)),
        "poolw": f(inp["pool_w"]),
        "pvec": pack_pvec(inp),
        "cst": const_table(),
    }


def run_phases(xT_list, shared, phases, final_norm, seqlen):
    ntok = xT_list[0].shape[1]
    b = Builder(ntok, seqlen, phases, final_norm)
    nc = b.build()
    in_maps = []
    for xT in xT_list:
        m = dict(shared)
        m["xT"] = np.ascontiguousarray(xT, dtype=np.float32)
        in_maps.append(m)
    res = run_bass_kernel_spmd(nc, in_maps, core_ids=list(range(len(xT_list))))
    return [r["outT"] for r in res.results]


def kernel(**inputs):
    x = np.asarray(inputs["x"], np.float32)
    B, S, _ = x.shape
    per = B // NCORES
    shared = prep_shared(inputs)
    xT_list = [np.ascontiguousarray(x[c * per:(c + 1) * per].reshape(per * S, D).T) for c in range(NCORES)]
    outs = run_phases(xT_list, shared, ALL_PHASES, True, S)
    out = np.empty((B, S, D), np.float32)
    for c in range(NCORES):
        out[c * per:(c + 1) * per] = outs[c].T.reshape(per, S, D)
    return out
```

```python
import numpy as np
import concourse.bass as bass
import concourse.mybir as mybir
from concourse.bass_utils import run_bass_kernel_spmd

F32 = mybir.dt.float32
BF16 = mybir.dt.bfloat16
U8 = mybir.dt.uint8
AF = mybir.ActivationFunctionType
ALU = mybir.AluOpType

D = 1024
KC = 8
T = 256
EPS = 1e-6
NCORES = 8


class Op:
    __slots__ = ("eng", "fn", "deps", "signal", "sem", "count", "is_dma")

    def __init__(self, eng, fn, is_dma=False):
        self.eng = eng
        self.fn = fn
        self.deps = ()
        self.signal = False
        self.sem = None
        self.count = 0
        self.is_dma = is_dma


class Prog:
    ENGS = ("pe", "act", "dve", "pool", "sp")

    def __init__(self, nc):
        self.nc = nc
        self.ops = {e: [] for e in self.ENGS}
        self.res = {}
        self.cur_sem = {}
        self.dma_sems = {}
        self.dma_cnt = {}
        self.nsem = 0
        self.new_epoch()

    def _alloc_sem(self, name):
        self.nsem += 1
        return self.nc.alloc_semaphore(f"s{self.nsem}_{name}")

    def new_epoch(self):
        for e in ("pe", "act", "dve", "pool"):
            self.cur_sem[e] = self._alloc_sem(e)

    def _deps(self, eng, reads, writes, is_dma):
        deps = set()
        for k in reads:
            r = self.res.get(k)
            if r is not None and r[0] is not None:
                deps.add(r[0])
        for k in writes:
            r = self.res.get(k)
            if r is not None:
                if r[0] is not None:
                    deps.add(r[0])
                deps.update(r[1])
        out = []
        for d in deps:
            if eng == "pe" and d.eng == "pe" and not d.is_dma and not is_dma:
                continue
            out.append(d)
        return out

    def _update(self, o, reads, writes):
        for k in reads:
            r = self.res.get(k)
            if r is None:
                r = self.res[k] = [None, []]
            r[1].append(o)
        for k in writes:
            self.res[k] = [o, []]

    def op(self, eng, fn, reads=(), writes=()):
        o = Op(eng, fn)
        o.sem = self.cur_sem.get(eng)
        o.deps = self._deps(eng, reads, writes, False)
        for d in o.deps:
            d.signal = True
        self._update(o, reads, writes)
        self.ops[eng].append(o)
        return o

    def dma(self, queue, fn, semkey, reads=(), writes=()):
        o = Op(queue, fn, is_dma=True)
        if semkey not in self.dma_sems:
            self.dma_sems[semkey] = self._alloc_sem("dma")
            self.dma_cnt[semkey] = 0
        o.sem = self.dma_sems[semkey]
        self.dma_cnt[semkey] += 16
        o.count = self.dma_cnt[semkey]
        o.signal = True
        o.deps = self._deps(queue, reads, writes, True)
        for d in o.deps:
            d.signal = True
        self._update(o, reads, writes)
        self.ops[queue].append(o)
        return o

    def fence(self, eng, reads=(), writes=()):
        return self.op(eng, None, reads, writes)

    def barrier(self):
        lasts = []
        for e in ("pe", "act", "dve", "pool"):
            for o in reversed(self.ops[e]):
                if o.fn is not None and not o.is_dma:
                    lasts.append(o)
                    break
        for e in ("pe", "act", "dve", "pool"):
            f = Op(e, None)
            f.deps = list(lasts)
            for d in lasts:
                d.signal = True
            self.ops[e].append(f)

    def emit(self):
        for e in self.ENGS:
            cnt = {}
            for o in self.ops[e]:
                if o.is_dma:
                    continue
                if o.signal:
                    assert o.fn is not None
                    cnt[o.sem] = cnt.get(o.sem, 0) + 1
                    o.count = cnt[o.sem]
        prog = self

        def run(e, h):
            waited = {}
            for o in prog.ops[e]:
                need = {}
                for d in o.deps:
                    if d.count > need.get(d.sem, 0):
                        need[d.sem] = d.count
                for s, v in need.items():
                    if waited.get(s, 0) >= v:
                        continue
                    h.wait_ge(s, v)
                    waited[s] = v
                if o.fn is None:
                    continue
                inst = o.fn(h)
                if o.signal:
                    inst.then_inc(o.sem, 16 if o.is_dma else 1)

        with self.nc.Block() as block:
            @block.tensor
            def _(h):
                run("pe", h)

            @block.scalar
            def _(h):
                run("act", h)

            @block.vector
            def _(h):
                run("dve", h)

            @block.gpsimd
            def _(h):
                run("pool", h)

            @block.sync
            def _(h):
                run("sp", h)


PV_NMIX = 0
PV_NFFN = 32
PV_NFIN = 64
PV_EVEN = 72
PV_ODD = 160
NPV = 168
E_CONVW = 0
E_CONVB = 16
E_BA = 20
E_BX = 24
E_LAM = 28
E_HN = 32
E_LBL0 = 36
E_LBL1 = 40


def _chunks(v):
    v = np.asarray(v, np.float32)
    return np.ascontiguousarray(v.reshape(-1, 128).T)


def pack_pvec(inp):
    pv = np.zeros((128, NPV), np.float32)
    for l in range(4):
        pv[:, PV_NMIX + l * 8:PV_NMIX + l * 8 + 8] = _chunks(inp["norm_mix"][l])
        pv[:, PV_NFFN + l * 8:PV_NFFN + l * 8 + 8] = _chunks(inp["norm_ffn"][l])
    pv[:, PV_NFIN:PV_NFIN + 8] = _chunks(inp["norm_final"])
    for j in range(2):
        b = PV_EVEN + j * 44
        for k in range(4):
            pv[:, b + E_CONVW + k * 4:b + E_CONVW + k * 4 + 4] = _chunks(inp["conv_w"][j, k])
        pv[:, b + E_CONVB:b + E_CONVB + 4] = _chunks(inp["conv_b"][j])
        pv[:, b + E_BA:b + E_BA + 4] = _chunks(inp["lru_ba"][j])
        pv[:, b + E_BX:b + E_BX + 4] = _chunks(inp["lru_bx"][j])
        pv[:, b + E_LAM:b + E_LAM + 4] = _chunks(inp["lru_lambda"][j])
        pv[:, b + E_HN:b + E_HN + 4] = _chunks(inp["hgrn_norm"][j])
        pv[:, b + E_LBL0:b + E_LBL0 + 4] = _chunks(inp["hgrn_lb_logits"][0])
        pv[:, b + E_LBL1:b + E_LBL1 + 4] = _chunks(inp["hgrn_lb_logits"][1])
        pv[:, PV_ODD + j * 4:PV_ODD + j * 4 + 4] = _chunks(inp["pool_scale"][j])
    return pv


def pack_gates(inp):
    g = np.zeros((2, 2, 4, 128, 128), np.float32)
    for j in range(2):
        for ax, name in enumerate(("lru_wa", "lru_wx")):
            w = np.asarray(inp[name][j], np.float32)
            for c in range(4):
                g[j, ax, c, 0:64, 0:64] = w[2 * c]
                g[j, ax, c, 64:128, 64:128] = w[2 * c + 1]
    return g


def const_table():
    c = np.zeros((128, 64), np.float32)
    for g, win in enumerate((2, 4, 8, 16)):
        for t in range(16):
            c[:, g * 16 + t] = 1.0 / min(t + 1, win)
    return c


class Builder:
    def __init__(self, ntok, seqlen, phases, final_norm):
        self.ntok = ntok
        self.seqlen = seqlen
        self.phases = phases
        self.final_norm = final_norm
        self.ntiles = ntok // T
        self.tiles_per_seq = seqlen // T
        nc = self.nc = bass.Bass("TRN2", target_bir_lowering=False)
        self.P = Prog(nc)
        self.tilecnt = 0
        self._dram()
        self._sbuf()

    def _dram(self):
        nc = self.nc
        ei = lambda n, s: nc.dram_tensor(n, s, F32, kind="ExternalInput").ap()
        self.xT = ei("xT", [D, self.ntok])
        self.outT = nc.dram_tensor("outT", [D, self.ntok], F32, kind="ExternalOutput").ap()
        self.hbuf = nc.dram_tensor("hbuf", [D, self.ntok], F32, kind="Internal").ap()
        self.w_in_even = ei("w_in_even", [2, D, 3072])
        self.w_out_even = ei("w_out_even", [2, D, D])
        self.w_in_odd = ei("w_in_odd", [2, D, 1536])
        self.w_out_odd = ei("w_out_odd", [2, D, D])
        self.w_ffn_in = ei("w_ffn_in", [4, D, 4096])
        self.w_ffn_out = ei("w_ffn_out", [4, 4096, D])
        self.wgate = ei("wgate", [2, 2, 4, 128, 128])
        self.sguwT = ei("sguwT", [2, 4, 128, 128])
        self.sgub = ei("sgub", [2, 1, 512])
        self.poolw = ei("poolw", [2, 4, 128, 128])
        self.pvec = ei("pvec", [128, NPV])
        self.cst = ei("cst", [128, 64])

    def sb(self, name, shape, dt=F32):
        return self.nc.alloc_sbuf_tensor(name, shape, dt).ap()

    def _sbuf(self):
        nc = self.nc
        sb = self.sb
        self.arena = sb("arena", [128, 65536], BF16)
        self.hx = [sb(f"hx{i}", [128, KC, T]) for i in range(3)]
        self.sq = sb("sq", [128, KC, T], BF16)
        self.xn = sb("xn", [128, KC, T], BF16)
        self.rstd = sb("rstd", [128, T])
        self.lnt = sb("lnt", [128, T])
        self.pv = sb("pv", [128, NPV])
        self.g32 = sb("g32", [128, 72])
        self.cs = sb("cs", [128, 64])
        self.ones = sb("ones", [128, 128], BF16)
        self.ident = sb("ident", [128, 128], BF16)
        self.triu = sb("triu", [128, 128])
        self.pmask = sb("pmask", [128, 128], U8)
        self.big = sb("big", [128, 8192], BF16)
        self.mb = self.arena[:, 36864:53248]
        self.y = [sb(f"y{i}", [128, KC, T], BF16) for i in range(2)]
        pt = lambda n, s, d=F32: nc.alloc_psum_tensor(n, s, d).ap()
        self.ps_stat = pt("ps_stat", [128, 2, T])
        self.ps_pj = [pt(f"ps_pj{i}", [128, 2, T]) for i in range(2)]
        self.ps_misc = pt("ps_misc", [128, 512])
        self.ps_po = [pt(f"ps_po{i}", [128, 512]) for i in range(2)]
        self.ps_v = pt("ps_v", [128, 512])
        self.ps_ao = pt("ps_ao", [128, 2, T])

    rec = None
    SCHED_WIN = 1e-9

    @staticmethod
    def _fd(ap):
        n = 1
        for d in ap.shape[1:]:
            n *= int(d)
        return n

    def OP(self, eng, fn, R=(), W=(), cost=0.3):
        if self.rec is not None:
            self.rec.append((0, eng, fn, tuple(R), tuple(W), cost))
            return None
        return self.P.op(eng, fn, R, W)

    def DM(self, q, fn, semkey, R=(), W=(), cost=2.0):
        if self.rec is not None:
            self.rec.append((1, q, fn, semkey, tuple(R), tuple(W), cost))
            return None
        return self.P.dma(q, fn, semkey, R, W)

    def collect(self, pieces):
        self.rec = []
        for pc in pieces:
            pc()
        r = self.rec
        self.rec = None
        return r

    def play(self, it):
        if it[0] == 0:
            self.P.op(it[1], it[2], it[3], it[4])
        else:
            self.P.dma(it[1], it[2], it[3], it[4], it[5])

    def sched_reset(self):
        self.sim_eng = {}
        self.sim_w = {}
        self.sim_r = {}

    def sched_merge(self, streams, after=None):
        heads = [0] * len(streams)
        ef, wd, rd = self.sim_eng, self.sim_w, self.sim_r
        remaining = sum(len(st) for st in streams)
        lastw = []
        for st in streams:
            d = {}
            for idx, it in enumerate(st):
                for k in (it[4] if it[0] == 0 else it[5]):
                    d[k] = idx
            lastw.append(d)
        after = after or {}
        while remaining:
            best = None
            for si, st in enumerate(streams):
                if heads[si] >= len(st):
                    continue
                it = st[heads[si]]
                eng = it[1]
                R, W = (it[3], it[4]) if it[0] == 0 else (it[4], it[5])
                blocked = False
                for sj in after.get(si, ()):
                    lw = lastw[sj]
                    hj = heads[sj]
                    for k in R + W:
                        v = lw.get(k)
                        if v is not None and v >= hj:
                            blocked = True
                            break
                    if blocked:
                        break
                if blocked:
                    continue
                t = ef.get(eng, 0.0)
                for k in R:
                    v = wd.get(k)
                    if v is not None and v > t:
                        t = v
                for k in W:
                    v = wd.get(k)
                    if v is not None and v > t:
                        t = v
                    v = rd.get(k)
                    if v is not None and v > t:
                        t = v
                if best is None or t < best[0] - self.SCHED_WIN:
                    best = (t, si)
            t, si = best
            it = streams[si][heads[si]]
            heads[si] += 1
            remaining -= 1
            eng = it[1]
            R, W = (it[3], it[4]) if it[0] == 0 else (it[4], it[5])
            cost = it[-1]
            if it[0] == 1:
                ef[eng] = t + 0.06
                fin = t + cost
            else:
                fin = t + cost
                ef[eng] = fin
            for k in R:
                if rd.get(k, 0.0) < fin:
                    rd[k] = fin
            for k in W:
                wd[k] = fin + 0.25
                rd[k] = 0.0
            self.play(it)

    def MM(self, out, lhsT, rhs, st, sp, R, W, skip=False):
        c = 0.035 + self._fd(out) / 2100.0
        if skip:
            return self.OP("pe", lambda h: h.matmul(out, lhsT, rhs, start=st, stop=sp, skip_group_check=True), R, W, c)
        return self.OP("pe", lambda h: h.matmul(out, lhsT, rhs, start=st, stop=sp), R, W, c)

    def TR(self, out, in_, R, W):
        ident = self.ident
        return self.OP("pe", lambda h: h.transpose(out, in_, ident), list(R) + ["ident"], W, 0.1)

    def _c(self, eng, out):
        fd = self._fd(out)
        if eng == "act":
            return 0.22 + fd / 1100.0
        if eng == "pool":
            return 0.6 + fd / 450.0
        return 0.1 + fd / 900.0

    def ACT(self, out, in_, func, R, W, bias=None, scale=None):
        kw = {}
        if bias is not None:
            kw["bias"] = bias
            if not isinstance(bias, (int, float)):
                R = list(R) + ["biasc"]
        if scale is not None:
            kw["scale"] = scale
        return self.OP("act", lambda h: h.activation(out, in_, func, **kw), R, W, self._c("act", out))

    def TS(self, eng, out, in0, s1, s2, op0, op1, R, W):
        if s2 is None:
            return self.OP(eng, lambda h: h.tensor_single_scalar(out, in0, s1, op0), R, W, self._c(eng, out))
        return self.OP(eng, lambda h: h.tensor_scalar(out, in0, s1, s2, op0, op1), R, W, self._c(eng, out))

    def STT(self, eng, out, in0, sc, in1, op0, op1, R, W):
        return self.OP(eng, lambda h: h.scalar_tensor_tensor(out, in0, sc, in1, op0, op1), R, W, self._c(eng, out))

    def TT(self, eng, out, in0, in1, op, R, W):
        return self.OP(eng, lambda h: h.tensor_tensor(out, in0, in1, op), R, W, self._c(eng, out))

    def CP(self, eng, out, in_, R, W):
        if eng == "act":
            return self.OP("act", lambda h: h.activation(out, in_, AF.Copy), R, W, self._c(eng, out))
        return self.OP(eng, lambda h: h.tensor_copy(out, in_), R, W, self._c(eng, out))

    def MS(self, eng, out, val, W):
        return self.OP(eng, lambda h: h.memset(out, val), (), W, self._c(eng, out))

    def SCAN(self, out, d0, d1, init, R, W):
        return self.OP("dve", lambda h: h.tensor_tensor_scan(out, d0, d1, init, ALU.mult, ALU.add), R, W, self._c("dve", out))

    def DMA(self, q, out, in_, semkey, R, W):
        return self.DM(q, lambda h: h.dma_start(out=out, in_=in_), semkey, R, W)

    def setup(self):
        P = self.P
        self.DMA("sp", self.pv, self.pvec, "c0", (), ["pv"])
        self.DMA("sp", self.cs, self.cst, "c1", (), ["cs"])
        self.TS("dve", self.g32, self.pv[:, 0:72], 32.0, None, ALU.mult, None, ["pv"], ["g32"])
        self.MS("pool", self.ones, 1.0, ["ones"])
        self.MS("pool", self.ident, 1.0, ["ident"])
        ident = self.ident
        self.OP("pool", lambda h: h.affine_select(ident, ident, [[-1, 128]], ALU.is_equal, 0.0, base=0, channel_multiplier=1),
             ["ident"], ["ident"])
        triu = self.triu
        self.MS("pool", triu, 1.0, ["triu"])
        self.OP("pool", lambda h: h.affine_select(triu, triu, [[1, 128]], ALU.is_ge, 0.0, base=0, channel_multiplier=-1),
             ["triu"], ["triu"])
        self.pm32 = self.sb("pm32", [128, 128])
        self.CP("pool", self.pm32, triu, ["triu"], ["pm32"])
        self.MS("pool", self.pm32[0:64, 64:128], 0.0, ["pm32"])
        self.CP("dve", self.pmask, self.pm32, ["pm32"], ["pmask"])
        self.pmask4 = self.sb("pmask4", [128, 4, 128], U8)
        for i_ in range(4):
            self.CP("dve", self.pmask4[:, i_, :], self.pm32, ["pm32", "pmask"], ["pmask"])
        self.MS("pool", self.segm4, 1.0, ["segm"])
        self.MS("pool", self.segm4.rearrange("p (c t) -> p c t", t=64)[:, :, 0:1], 0.0, ["segm"])

    def wkeys(self, lo, hi):
        return [("wa", g) for g in range(lo // 4096, (hi - 1) // 4096 + 1)]

    def load_w(self, off, view_shape, src, nsplit, tag, idx=None):
        a, b = view_shape
        dst = self.arena[:, off:off + a * b].rearrange("p (a b) -> p a b", b=b)
        step = a // nsplit
        for i in (range(nsplit) if idx is None else idx):
            lo = off + i * step * b
            hi = off + (i + 1) * step * b
            self.DMA("pool", dst[:, i * step:(i + 1) * step, :], src[:, i * step:(i + 1) * step, :],
                     (tag, i), (), self.wkeys(lo, hi))
        return dst

    def hxkeys(self, slot):
        return [("hx", slot, m) for m in range(KC)]

    def load_tile(self, src, j):
        slot = self.tilecnt % 3
        self.cur_slot = slot
        srcv = src.rearrange("(k p) t -> p k t", p=128)[:, :, j * T:(j + 1) * T]
        rk = [("hd", j)] if src is self.hbuf else []
        self.DMA("sp", self.hx[slot], srcv, ("hx", slot), rk, self.hxkeys(slot))
        self.tilecnt += 1
        return slot

    def store_tile(self, dst, j, slot):
        dstv = dst.rearrange("(k p) t -> p k t", p=128)[:, :, j * T:(j + 1) * T]
        wk = [("hd", j)] if dst is self.hbuf else [("od", j)]
        o = self.DMA("sp", dstv, self.hx[slot], ("st", slot), self.hxkeys(slot), wk)
        if dst is self.outT:
            self.out_keys.append(("od", j))

    def rstd_from(self, ps, scale, R):
        self.ACT(self.lnt, ps, AF.Ln, R, ["lnt"], bias=self.epsb, scale=scale)
        self.ACT(self.rstd, self.lnt, AF.Exp, ["lnt"], ["rstd"], scale=-0.5)

    def norm(self, slot, gcol, out, outkeys, bufs=None):
        hx = self.hx[slot]
        hk = self.hxkeys(slot)
        if bufs is None:
            bufs = (self.sq, "sq", self.ps_stat[:, 0, :], "ps_stat", self.lnt, "lnt", self.rstd, "rstd")
        sq, sqk, ss, ssk, lnt, lntk, rstd, rstdk = bufs
        self.ACT(sq, hx, AF.Square, hk, [sqk])
        for k in range(KC):
            self.MM(ss, self.ones, sq[:, k, :], k == 0, k == KC - 1, [sqk, "ones"], [ssk])
        self.ACT(lnt, ss, AF.Ln, [ssk], [lntk], bias=self.eps1024)
        self.ACT(rstd, lnt, AF.Exp, [lntk], [rstdk], scale=-0.5)
        for k in range(KC):
            self.STT("dve", out[:, k, :], hx[:, k, :], gcol[:, k:k + 1], rstd, ALU.mult, ALU.mult,
                     [("hx", slot, k), rstdk, "g32"], [outkeys[k]])

    def phase_ffn(self, l, src, dst, last):
        pre = self.w1_prefetched.pop(l, ())
        w1 = self.load_w(0, (KC, 4096), self.w_ffn_in[l].rearrange("(k p) n -> p k n", p=128), 8, "w",
                         idx=[i for i in range(8) if i not in pre])
        w2 = self.load_w(32768, (32, D), self.w_ffn_out[l].rearrange("(k p) n -> p k n", p=128), 8, "w2")
        hid = self.big[:, 0:32 * T].rearrange("p (m t) -> p m t", t=T)
        r32 = [self.sb_r32a, self.sb_r32b]
        gcol = self.g32[:, PV_NFFN + l * 8:PV_NFFN + l * 8 + 8]
        fg = self.g32[:, PV_NFIN:PV_NFIN + 8]
        xns = [self.xn, self.y[1]]
        xnks = [[("xn", k) for k in range(KC)], [("xn2", k) for k in range(KC)]]
        dofin = last and self.final_norm
        n = self.ntiles
        slots = {0: self.load_tile(src, 0)}
        if n > 1:
            slots[1] = self.load_tile(src, 1)
        self.norm(slots[0], gcol, xns[0], xnks[0])

        def norm_parts(slot, out, outkeys, gc):
            hx = self.hx[slot]
            hk = self.hxkeys(slot)
            ss = self.ps_stat[:, 0, :]

            def p1():
                self.ACT(self.sq, hx, AF.Square, hk, ["sq"])

            def p2():
                for k in range(KC):
                    self.MM(ss, self.ones, self.sq[:, k, :], k == 0, k == KC - 1, ["sq", "ones"], ["ps_stat"])

            def p3():
                self.ACT(self.lnt, ss, AF.Ln, ["ps_stat"], ["lnt"], bias=self.eps1024)
                self.ACT(self.rstd, self.lnt, AF.Exp, ["lnt"], ["rstd"], scale=-0.5)

            def p4():
                for k in range(KC):
                    self.STT("dve", out[:, k, :], hx[:, k, :], gc[:, k:k + 1], self.rstd, ALU.mult, ALU.mult,
                             [("hx", slot, k), "rstd", "g32"], [outkeys[k]])
            return p1, p2, p3, p4

        def hidden(j, lo, hi):
            xn = xns[j % 2]
            xk = xnks[j % 2]
            for m2 in range(lo, hi):
                pj = self.ps_pj[m2 % 2]
                pk = f"pj{m2 % 2}"
                for h2 in range(2):
                    m = m2 * 2 + h2
                    for k in range(KC):
                        self.MM(pj[:, h2, :], w1[:, k, m * 128:(m + 1) * 128], xn[:, k, :], k == 0, k == KC - 1,
                                [xk[k], ("wa", k)], [pk])
                rb = r32[m2 % 2]
                rk = f"r32{m2 % 2}"
                self.ACT(rb, pj, AF.Relu, [pk], [rk])
                self.TT("pool", hid[:, 2 * m2:2 * m2 + 2, :], rb, rb, ALU.mult, [rk], [("hid", m2)])

        nop = (lambda: None)

        def fin_parts(jj):
            if jj < 0 or not dofin:
                return nop, nop, nop, nop
            slot = slots[jj]
            f1, f2, f3, f4 = norm_parts(slot, self.hx[slot], self.hxkeys(slot), fg)

            def f4s():
                f4()
                self.store_tile(dst, jj, slot)
            return f1, f2, f3, f4s

        for j in range(n):
            slot = slots[j]
            if j + 1 < n:
                p1, p2, p3, p4 = norm_parts(slots[j + 1], xns[(j + 1) % 2], xnks[(j + 1) % 2], gcol)
            else:
                p1 = p2 = p3 = p4 = nop
            f1, f2, f3, f4 = fin_parts(j - 1)
            hidden(j, 0, 4)
            p1()
            hidden(j, 4, 8)
            p2()
            p3()
            f1()
            hidden(j, 8, 12)
            p4()
            f2()
            f3()
            hidden(j, 12, 16)
            f4()
            if j == n - 1 and self.next_mix is not None:
                lm = self.next_mix
                jm = lm // 2
                if lm % 2 == 0:
                    self.load_w(0, (KC, 3072), self.w_in_even[jm].rearrange("(k p) n -> p k n", p=128), 8, "w")
                    self.load_w(24576, (KC, D), self.w_out_even[jm].rearrange("(k p) n -> p k n", p=128), 2, "wo")
                else:
                    self.load_w(0, (KC, 1536), self.w_in_odd[jm].rearrange("(k p) n -> p k n", p=128), 8, "w")
                    self.load_w(24576, (KC, D), self.w_out_odd[jm].rearrange("(k p) n -> p k n", p=128), 2, "wo")
                self.mix_prefetched.add(lm)
            if j + 2 < n:
                slots[j + 2] = self.load_tile(src, j + 2)
            for m in range(KC):
                po = self.ps_po[m % 2][:, 0:T]
                pok = f"po{m % 2}"
                for k in range(32):
                    self.MM(po, w2[:, k, m * 128:(m + 1) * 128], hid[:, k, :], k == 0, k == 31,
                            [("hid", k // 2), ("wa", 8 + k // 4)], [pok])
                self.TT("dve", self.hx[slot][:, m, :], po, self.hx[slot][:, m, :], ALU.add,
                        [pok, ("hx", slot, m)], [("hx", slot, m)])
            if not dofin:
                self.store_tile(dst, j, slot)
        if dofin:
            f1, f2, f3, f4 = fin_parts(n - 1)
            f1()
            f2()
            f3()
            f4()

    def out_proj(self, wout, slot, yb=None, yp=0):
        if yb is None:
            yb = self.y[0]
        for m in range(KC):
            po = self.ps_po[m % 2][:, 0:T]
            pok = f"po{m % 2}"
            for k in range(KC):
                self.MM(po, wout[:, k, m * 128:(m + 1) * 128], yb[:, k, :], k == 0, k == KC - 1,
                        [("y", yp, k), ("wa", 6 + k // 4)], [pok])
            self.TT("dve", self.hx[slot][:, m, :], po, self.hx[slot][:, m, :], ALU.add,
                    [pok, ("hx", slot, m)], [("hx", slot, m)])

    def pipeline(self, src, dst, A, B):
        n = self.ntiles
        self.sched_reset()
        slots = {0: self.load_tile(src, 0)}
        if n > 1:
            slots[1] = self.load_tile(src, 1)
        st0 = [self.collect(pl) for pl in A(0, slots[0])]
        self.sched_merge(st0, {i: list(range(i)) for i in range(1, len(st0))})
        for j in range(n):
            if j + 2 < n:
                slots[j + 2] = self.load_tile(src, j + 2)
            if j == n - 1 and self.next_ffn is not None:
                lf = self.next_ffn
                self.load_w(0, (KC, 4096), self.w_ffn_in[lf].rearrange("(k p) n -> p k n", p=128), 8, "w", idx=range(6))
                self.w1_prefetched[lf] = set(range(6))
            streams = [self.collect(B(j, slots[j]))]
            if j + 1 < n:
                streams += [self.collect(pl) for pl in A(j + 1, slots[j + 1])]
            self.sched_merge(streams, {i: list(range(1, i)) for i in range(2, len(streams))})
            self.store_tile(dst, j, slots[j])

    def abuf(self, off, n, dt=BF16):
        if dt == F32:
            return self.arena[:, off:off + 2 * n].bitcast(F32), off + 2 * n
        return self.arena[:, off:off + n], off + n

    def phase_odd(self, l, src, dst):
        j_ = l // 2
        P = self.P
        pf = [] if l in self.mix_prefetched else None
        win = self.load_w(0, (KC, 1536), self.w_in_odd[j_].rearrange("(k p) n -> p k n", p=128), 8, "w", idx=pf)
        wout = self.load_w(24576, (KC, D), self.w_out_odd[j_].rearrange("(k p) n -> p k n", p=128), 2, "wo", idx=pf)
        off = 33792
        wsg32, off = self.abuf(off, 512, F32)
        wsg32 = wsg32.rearrange("p (g t) -> p g t", t=128)
        self.DMA("sp", wsg32, self.sguwT[j_].rearrange("g s t -> s g t"), "c2", (), ["wsg32"])
        wsg = self.arena[:, 32768:32768 + 512].rearrange("p (g t) -> p g t", t=128)
        self.TT("dve", wsg, wsg32, self.triu.unsqueeze(1).to_broadcast([128, 4, 128]), ALU.mult,
                ["wsg32", "triu"], [("wa", 8)])
        wpl = self.arena[:, 32768 + 512:32768 + 1024].rearrange("p (g t) -> p g t", t=128)
        self.DMA("pool", wpl, self.poolw[j_].rearrange("g c d -> c g d"), "c3", (), [("wa", 8)])
        sbb, off = self.abuf(off, 512, F32)
        self.DMA("sp", sbb, self.sgub[j_].partition_broadcast(128), "c4", (), ["sbb"])
        gcol = self.g32[:, PV_NMIX + l * 8:PV_NMIX + l * 8 + 8]
        psc = self.pv[:, PV_ODD + j_ * 4:PV_ODD + j_ * 4 + 4]
        xnk = [("xn", k) for k in range(KC)]
        ug, vn, pp = [], [], []
        for p in range(2):
            a, off = self.abuf(off, 4 * T)
            ug.append(a.rearrange("p (c t) -> p c t", t=T))
            a, off = self.abuf(off, 1024)
            vn.append(a.rearrange("p (s c) -> p s c", c=512))
            a, off = self.abuf(off, 4 * (16 + T), F32)
            pp.append(a.rearrange("p (g t) -> p g t", t=16 + T))
        vgs = []
        for i_ in range(2):
            a, off = self.abuf(off, 512, F32)
            vgs.append(a)
        mv8 = self.sb_mv8
        pooled, off = self.abuf(off, 4 * T)
        pooled = pooled.rearrange("p (g t) -> p g t", t=T)
        ws = []
        for i in range(2):
            a, off = self.abuf(off, 4 * (T + 16), F32)
            ws.append(a.rearrange("p (g t) -> p g t", t=T + 16))
        svt, off = self.abuf(off, 512, F32)
        svt = svt.rearrange("p (g t) -> p g t", t=128)
        assert off <= 65536
        st6 = self.sb_st6
        mv = self.sb_mv
        L = T + 16
        inw = lambda k: [("wa", (k * 1536) // 4096), ("wa", (k * 1536 + 1535) // 4096)]

        def A(j, slot):
            p = j % 2
            first = (j % self.tiles_per_seq) == 0
            pcs = []

            def a_norm():
                if first:
                    self.MS("pool", pp[p][:, :, 0:16], 0.0, [("pp", p)])
                else:
                    self.CP("pool", pp[p][:, :, 0:16], pp[1 - p][:, :, T:T + 16], [("pp", 1 - p)], [("pp", p)])
                self.norm(slot, gcol, self.xn, xnk)
            pcs.append(a_norm)

            def pair(m0, evac, i):
                def f():
                    pj = self.ps_pj[i]
                    pk = f"pj{i}"
                    for h2 in range(2):
                        m = m0 + h2
                        for k in range(KC):
                            self.MM(pj[:, h2, :], win[:, k, m * 128:(m + 1) * 128], self.xn[:, k, :], k == 0, k == KC - 1,
                                    [("xn", k)] + inw(k), [pk])
                    evac(pj, pk)
                return f
            for m2 in range(2):
                pcs.append(pair(2 * m2, lambda pj, pk, m2=m2: self.ACT(ug[p][:, 2 * m2:2 * m2 + 2, :], pj, AF.Gelu_apprx_tanh,
                                                                        [pk], [("ug", p)]), m2))
            for m2 in range(2):
                pcs.append(pair(8 + 2 * m2, lambda pj, pk, m2=m2: self.CP("act", pp[p][:, 2 * m2:2 * m2 + 2, 16:16 + T], pj,
                                                                            [pk], [("pp", p)]), m2))

            def vsub(sub):
                def f():
                    for k in range(KC):
                        self.MM(self.ps_v, self.xn[:, k, sub * 128:(sub + 1) * 128], win[:, k, 512:1024], k == 0, k == KC - 1,
                                [("xn", k)] + inw(k), ["ps_v"])
                    self.ACT(vgs[sub], self.ps_v, AF.Gelu_apprx_tanh, ["ps_v"], [("vg", sub)])
                    for g in range(4):
                        vgg = vgs[sub][:, g * 128:(g + 1) * 128]
                        self.OP("dve", (lambda h, o=st6[:, g, :], i=vgg: h.bn_stats(o, i)), [("vg", sub)], [("st6", g)])
                        self.OP("dve", (lambda h, o=mv8[:, sub * 4 + g, :], i=st6[:, g, :]: h.bn_aggr(o, i)), [("st6", g)], ["mv"])
                return f
            pcs.append(vsub(0))
            pcs.append(vsub(1))

            def vfin():
                self.ACT(self.sb_l8, mv8[:, :, 1], AF.Ln, ["mv"], ["l4"], bias=self.epsb)
                self.ACT(self.sb_r8, self.sb_l8, AF.Exp, ["l4"], ["r4"], scale=-0.5)
                for sub in range(2):
                    for g in range(4):
                        self.TS("dve", vn[p][:, sub, g * 128:(g + 1) * 128], vgs[sub][:, g * 128:(g + 1) * 128],
                                mv8[:, sub * 4 + g, 0:1], self.sb_r8[:, sub * 4 + g:sub * 4 + g + 1], ALU.subtract, ALU.mult,
                                [("vg", sub), "mv", "r4"], [("vn", p)])
            pcs.append(vfin)
            return [pcs]

        def B(j, slot):
            p = j % 2
            first = (j % self.tiles_per_seq) == 0
            pcs = []

            def sgu(sub):
                def f():
                    svp = self.ps_ao.rearrange("p a t -> p (a t)").rearrange("p (g t) -> p g t", t=128)
                    for g in range(4):
                        self.MM(svp[:, g, :], vn[p][:, sub, g * 128:(g + 1) * 128], wsg[:, g, :], True, True,
                                [("vn", p), ("wa", 8)], ["ps_ao"])
                    self.TT("dve", svt, svp, sbb.rearrange("p (g t) -> p g t", t=128), ALU.add, ["ps_ao", "sbb"], ["svt"])
                    self.TT("dve", self.y[0][:, 0:4, sub * 128:(sub + 1) * 128], svt, ug[p][:, :, sub * 128:(sub + 1) * 128],
                            ALU.mult, ["svt", ("ug", p)], [("y", 0, 0), ("y", 0, 1), ("y", 0, 2), ("y", 0, 3)])
                return f
            pcs.append(sgu(0))
            pcs.append(sgu(1))

            def pool_():
                a = pp[p]
                self.TT("dve", ws[0][:, :, 1:L], a[:, :, 1:L], a[:, :, 0:L - 1], ALU.add, [("pp", p)], ["ws0"])
                self.TT("dve", ws[1][:, 1:4, 3:L], ws[0][:, 1:4, 3:L], ws[0][:, 1:4, 1:L - 2], ALU.add, ["ws0"], ["ws1"])
                self.TT("dve", ws[0][:, 2:4, 7:L], ws[1][:, 2:4, 7:L], ws[1][:, 2:4, 3:L - 4], ALU.add, ["ws1", "ws0"], ["ws0b"])
                self.TT("dve", ws[1][:, 3, 15:L], ws[0][:, 3, 15:L], ws[0][:, 3, 7:L - 8], ALU.add, ["ws0b", "ws1"], ["ws1b"])
                finals = [(ws[0], ["ws0"]), (ws[1], ["ws1"]), (ws[0], ["ws0b"]), (ws[1], ["ws1b"])]
                for g in range(4):
                    win_ = 2 << g
                    cur, ck = finals[g]
                    cur = cur[:, g, :]
                    if first:
                        self.TS("dve", svt.rearrange("p g t -> p (g t)")[:, 0:T], cur[:, 16:L], 1.0 / win_, None, ALU.mult, None,
                                ck, ["svt"])
                        self.TT("dve", svt.rearrange("p g t -> p (g t)")[:, 0:16], cur[:, 16:32], self.cs[:, g * 16:(g + 1) * 16],
                                ALU.mult, ck + ["cs", "svt"], ["svt"])
                        self.TT("dve", pooled[:, g, :], svt.rearrange("p g t -> p (g t)")[:, 0:T], a[:, g, 16:L], ALU.subtract,
                                ["svt", ("pp", p)], [("pooled", g)])
                    else:
                        self.STT("dve", pooled[:, g, :], cur[:, 16:L], 1.0 / win_, a[:, g, 16:L], ALU.mult, ALU.subtract,
                                 ck + [("pp", p)], [("pooled", g)])
            pcs.append(pool_)

            def pd_(g2):
                def f():
                    pd = self.ps_v.rearrange("p (a t) -> p a t", t=T)
                    for h2 in range(2):
                        g = 2 * g2 + h2
                        self.MM(pd[:, h2, :], wpl[:, g, :], pooled[:, g, :], True, True, [("pooled", g), ("wa", 8)], ["ps_v"])
                    for h2 in range(2):
                        g = 2 * g2 + h2
                        self.OP("act", (lambda h, o=self.y[0][:, 4 + g, :], i_=pd[:, h2, :], s_=psc[:, g:g + 1]: h.mul(o, i_, s_)),
                             ["ps_v", "pv"], [("y", 0, 4 + g)])
                return f
            pcs.append(pd_(0))
            pcs.append(pd_(1))
            pcs.append(lambda: self.out_proj(wout, slot))
            return pcs

        self.pipeline(src, dst, A, B)

    def even_consts(self, j_):
        b = PV_EVEN + j_ * 44
        pv = self.pv
        ec = self.sb_ec
        k = f"ec"
        self.ACT(ec[:, 0:4], pv[:, b + E_LAM:b + E_LAM + 4], AF.Exp, ["pv"], [k], scale=-1.0)
        self.ACT(ec[:, 4:8], ec[:, 0:4], AF.Ln, [k], [k], bias=self.oneb)
        self.TS("dve", ec[:, 8:12], ec[:, 4:8], -4.0, None, ALU.mult, None, [k], [k])
        self.TS("dve", ec[:, 12:16], ec[:, 4:8], -8.0, None, ALU.mult, None, [k], [k])
        self.TS("dve", ec[:, 16:20], pv[:, b + E_BA:b + E_BA + 4], 0.5, None, ALU.mult, None, ["pv", k], [k])
        self.TS("dve", ec[:, 20:24], pv[:, b + E_BX:b + E_BX + 4], 0.5, None, ALU.mult, None, ["pv", k], [k])
        if j_ == 0:
            self.MS("dve", ec[:, 24:28], 0.0, [k])
        else:
            self.ACT(ec[:, 28:32], pv[:, b + E_LBL0:b + E_LBL0 + 4], AF.Exp, ["pv", k], [k])
            self.ACT(ec[:, 32:36], pv[:, b + E_LBL1:b + E_LBL1 + 4], AF.Exp, ["pv", k], [k])
            self.TT("dve", ec[:, 28:32], ec[:, 28:32], ec[:, 32:36], ALU.add, [k], [k])
            self.OP("dve", lambda h: h.reciprocal(ec[:, 28:32], ec[:, 28:32]), [k], [k])
            self.TT("dve", ec[:, 24:28], ec[:, 32:36], ec[:, 28:32], ALU.mult, [k], [k])
        self.TS("dve", ec[:, 28:32], ec[:, 24:28], 0.5, 0.5, ALU.mult, ALU.add, [k], [k])
        self.TS("dve", ec[:, 32:36], ec[:, 24:28], -0.5, 0.5, ALU.mult, ALU.add, [k], [k])
        self.TS("dve", ec[:, 36:40], ec[:, 32:36], -1.0, None, ALU.mult, None, [k], [k])

    def phase_even(self, l, src, dst):
        j_ = l // 2
        P = self.P
        pf = [] if l in self.mix_prefetched else None
        win = self.load_w(0, (KC, 3072), self.w_in_even[j_].rearrange("(k p) n -> p k n", p=128), 8, "w", idx=pf)
        wout = self.load_w(24576, (KC, D), self.w_out_even[j_].rearrange("(k p) n -> p k n", p=128), 2, "wo", idx=pf)
        wg = self.arena[:, 32768:32768 + 1024].rearrange("p (a c t) -> p a c t", a=2, c=4)
        self.DMA("pool", wg, self.wgate[j_].rearrange("a c i o -> i a c o"), "c3", (), [("wa", 8)])
        self.even_consts(j_)
        ec = self.sb_ec
        pvb = PV_EVEN + j_ * 44
        pv = self.pv
        gcol = self.g32[:, PV_NMIX + l * 8:PV_NMIX + l * 8 + 8]
        xnk = [("xn", k) for k in range(KC)]
        off = 33792
        c4 = lambda a: a.rearrange("p (c t) -> p c t", t=T)
        gg, qf, sg, tz, vt, xa = [], [], [], [], [], []
        for p in range(2):
            a, off = self.abuf(off, 4 * T); gg.append(c4(a))
            a, off = self.abuf(off, 4 * T); qf.append(c4(a))
            a, off = self.abuf(off, 4 * T); sg.append(c4(a))
            a, off = self.abuf(off, 4 * T, F32); tz.append(c4(a))
            a, off = self.abuf(off, 1024); vt.append(a.rearrange("p (s c) -> p s c", c=512))
            a, off = self.abuf(off, 4 * (4 + T), F32); xa.append(a.rearrange("p (c t) -> p c t", t=4 + T)[:, :, 1:4 + T])
        ua, off = self.abuf(off, 4 * T, F32); ua = c4(ua)
        tr, off = self.abuf(off, 4 * T, F32); tr = c4(tr)
        ti, off = self.abuf(off, 4 * T, F32); ti = c4(ti)
        aa, off = self.abuf(off, 4 * T, F32); aa = c4(aa)
        a2, off = self.abuf(off, 4 * T, F32); a2 = c4(a2)
        uab, off = self.abuf(off, 4 * T); uab = c4(uab)
        kk, off = self.abuf(off, 4 * T, F32); kk = c4(kk)
        e1, off = self.abuf(off, 4 * T); e1 = c4(e1)
        assert off <= 65536, off
        boff = [0]

        def bbuf(n, dt=BF16):
            if dt == F32:
                r = self.big[:, boff[0]:boff[0] + 2 * n].bitcast(F32)
                boff[0] += 2 * n
            else:
                r = self.big[:, boff[0]:boff[0] + n]
                boff[0] += n
            return r
        logf = c4(bbuf(4 * T, F32))
        bc = c4(bbuf(4 * T, F32))
        osb = c4(bbuf(4 * T, F32))
        qt = c4(bbuf(4 * T))
        kt = c4(bbuf(4 * T))
        assert boff[0] <= 8192
        S = self.sb_S
        St = self.sb_St
        Sb = self.sb_Sb
        ktT = self.sb_ktT
        Asb = self.sb_Asb
        sc = self.sb_sc
        esc = self.sb_esc
        hst = self.sb_hst
        osq = self.sb_osq
        rs2 = self.sb_rs2
        self.MS("pool", Asb, 0.0, ["Asb"])
        inw = lambda k, m: [("wa", (k * 3072 + m * 128) // 4096)]
        Skeys = [("S", h_) for h_ in range(4)]
        uak = [("ua", c) for c in range(4)]

        def A(j, slot):
            p = j % 2
            first = (j % self.tiles_per_seq) == 0
            yb = self.y[p]
            pcs = []

            def a_norm():
                if first:
                    self.MS("pool", xa[p][:, :, 0:3], 0.0, [("xa", p)])
                else:
                    self.CP("pool", xa[p][:, :, 0:3], xa[1 - p][:, :, T:T + 3], [("xa", 1 - p)], [("xa", p)])
                self.norm(slot, gcol, self.xn, xnk)
            pcs.append(a_norm)
            cnt = [0]

            def pair(m0, evac):
                def f():
                    i = cnt[0] % 2
                    cnt[0] += 1
                    pj = self.ps_pj[i]
                    pk = f"pj{i}"
                    for h2 in range(2):
                        m = m0 + h2
                        for k in range(KC):
                            self.MM(pj[:, h2, :], win[:, k, m * 128:(m + 1) * 128], self.xn[:, k, :], k == 0, k == KC - 1,
                                    [("xn", k)] + inw(k, m), [pk])
                    evac(pj, pk)
                return f

            def lru1():
                if first:
                    self.MS("dve", hst, 0.0, ["hst"])
                for c in range(4):
                    cw = lambda k_: pv[:, pvb + E_CONVW + k_ * 4 + c:pvb + E_CONVW + k_ * 4 + c + 1]
                    self.TS("dve", ua[:, c, :], xa[p][:, c, 0:T], cw(0), pv[:, pvb + E_CONVB + c:pvb + E_CONVB + c + 1],
                            ALU.mult, ALU.add, [("xa", p), "pv"], [("ua", c)])
                    for k_ in range(1, 4):
                        self.STT("dve", ua[:, c, :], xa[p][:, c, k_:k_ + T], cw(k_), ua[:, c, :], ALU.mult, ALU.add,
                                 [("xa", p), "pv", ("ua", c)], [("ua", c)])
                self.CP("act", uab, ua, uak, ["uab"])

            def lru2():
                for c in range(4):
                    pj = self.ps_v.rearrange("p (a t) -> p a t", t=T)
                    pk = "ps_v"
                    self.MM(pj[:, 0, :], wg[:, 0, c, :], uab[:, c, :], True, True, ["uab", ("wa", 8)], [pk])
                    self.MM(pj[:, 1, :], wg[:, 1, c, :], uab[:, c, :], True, True, ["uab", ("wa", 8)], [pk])
                    self.ACT(tr[:, c, :], pj[:, 0, :], AF.Tanh, [pk, "ec"], ["tr"], bias=ec[:, 16 + c:17 + c], scale=0.5)
                    self.ACT(ti[:, c, :], pj[:, 1, :], AF.Tanh, [pk, "ec"], ["ti"], bias=ec[:, 20 + c:21 + c], scale=0.5)
                for c in range(4):
                    self.ACT(aa[:, c, :], tr[:, c, :], AF.Exp, ["tr", "ec"], ["aa"], bias=ec[:, 8 + c:9 + c], scale=ec[:, 8 + c:9 + c])
                    self.ACT(a2[:, c, :], tr[:, c, :], AF.Exp, ["tr", "ec"], ["a2"], bias=ec[:, 12 + c:13 + c], scale=ec[:, 12 + c:13 + c])
                self.TS("dve", a2, a2, 1.0, -1.0, ALU.min, ALU.mult, ["a2"], ["a2"])
                self.ACT(a2, a2, AF.Ln, ["a2"], ["a2"], bias=self.oneb)
                self.ACT(a2, a2, AF.Exp, ["a2"], ["a2"], scale=0.5)
                self.STT("dve", ti, ti, 1.0, ua, ALU.add, ALU.mult, ["ti"] + uak, ["ti"])
                self.STT("dve", ti, ti, 0.5, a2, ALU.mult, ALU.mult, ["ti", "a2"], ["ti"])

            def lru3():
                for c in range(4):
                    self.SCAN(tr[:, c, :], aa[:, c, :], ti[:, c, :], hst[:, c:c + 1], ["aa", "ti", "hst", "tr"], ["tr"])
                self.CP("dve", hst, tr[:, :, T - 1], ["tr"], ["hst"])
                self.TT("dve", yb[:, 0:4, :], tr, gg[p], ALU.mult, ["tr", ("gg", p)], [("y", p, c) for c in range(4)])

            for c2 in range(2):
                pcs.append(pair(2 * c2, lambda pj, pk, c2=c2: self.CP("act", xa[p][:, 2 * c2:2 * c2 + 2, 3:3 + T], pj, [pk], [("xa", p)])))
            for c2 in range(2):
                pcs.append(pair(4 + 2 * c2, lambda pj, pk, c2=c2: self.ACT(gg[p][:, 2 * c2:2 * c2 + 2, :], pj, AF.Gelu_apprx_tanh, [pk], [("gg", p)])))
            for c2 in range(2):
                pcs.append(pair(12 + 2 * c2, lambda pj, pk, c2=c2: self.ACT(tz[p][:, 2 * c2:2 * c2 + 2, :], pj, AF.Tanh, [pk], [("tz", p)], scale=0.5)))
            for c2 in range(2):
                pcs.append(pair(8 + 2 * c2, lambda pj, pk, c2=c2: self.ACT(qf[p][:, 2 * c2:2 * c2 + 2, :], pj, AF.Silu, [pk], [("qf", p)])))
            for c2 in range(2):
                pcs.append(pair(20 + 2 * c2, lambda pj, pk, c2=c2: self.ACT(sg[p][:, 2 * c2:2 * c2 + 2, :], pj, AF.Silu, [pk], [("sg", p)])))

            def vsub(sub):
                def f():
                    i = cnt[0] % 2
                    cnt[0] += 1
                    pjv = self.ps_pj[i].rearrange("p a t -> p (a t)")
                    pk = f"pj{i}"
                    for k in range(KC):
                        lo = k * 3072 + 2048
                        self.MM(pjv, self.xn[:, k, sub * 128:(sub + 1) * 128], win[:, k, 2048:2560], k == 0, k == KC - 1,
                                [("xn", k), ("wa", lo // 4096), ("wa", (lo + 511) // 4096)], [pk])
                    self.CP("dve", vt[p][:, sub, :], pjv, [pk], [("vt", p)])
                return f
            pcs.append(vsub(0))
            pcs.append(vsub(1))
            return [pcs, [lru1, lru2, lru3]]

        def B(j, slot):
            p = j % 2
            first = (j % self.tiles_per_seq) == 0
            yb = self.y[p]
            pcs = []

            def hg1():
                if first:
                    self.MS("dve", S, 0.0, Skeys)
                for hd in range(4):
                    self.ACT(logf[:, hd, :], tz[p][:, hd, :], AF.Ln, [("tz", p), "ec"], ["logf"],
                             bias=ec[:, 28 + hd:29 + hd], scale=ec[:, 32 + hd:33 + hd])
                for hd in range(4):
                    self.TS("dve", kk[:, hd, :], tz[p][:, hd, :], ec[:, 36 + hd:37 + hd], ec[:, 32 + hd:33 + hd], ALU.mult, ALU.add,
                            [("tz", p), "ec"], ["kk"])
                flat = lambda a: a.rearrange("p c t -> p (c t)")
                self.SCAN(flat(bc), self.segm4, flat(logf), 0.0, ["segm", "logf"], ["bc"])
                bcv = flat(bc).rearrange("p (c t) -> p c t", t=64)
                self.TT("dve", flat(logf).rearrange("p (c t) -> p c t", t=64), bcv, bcv[:, :, 31:32].to_broadcast([128, 16, 64]),
                        ALU.subtract, ["bc", "logf"], ["logf"])
                self.CP("dve", sc[:, 0, :], bcv[:, :, 31], ["bc"], ["sc"])
                self.CP("dve", sc[:, 1, :], bcv[:, :, 63], ["bc", "sc"], ["sc"])
                self.TT("dve", sc[:, 2, :], sc[:, 1, :], sc[:, 0, :], ALU.subtract, ["sc"], ["sc"])
                self.ACT(e1, logf, AF.Exp, ["logf"], ["e1"])
                self.ACT(logf, logf, AF.Exp, ["logf"], ["logf"], scale=-1.0)
                self.ACT(esc, sc, AF.Exp, ["sc"], ["esc"])
                self.TT("dve", qt, qf[p], e1, ALU.mult, [("qf", p), "e1"], ["qt"])
                self.TT("dve", kt, kk, logf, ALU.mult, ["kk", "logf"], ["kt"])
            pcs.append(hg1)

            def hg2(hp):
                def f():
                    h0 = 2 * hp
                    ktp = self.ps_misc[:, 256:512].bitcast(BF16).rearrange("p (h r t) -> p h r t", h=2, r=2)
                    for hh in range(2):
                        for pr in range(2):
                            self.TR(ktp[:, hh, pr, :], kt[:, h0 + hh, pr * 128:(pr + 1) * 128], ["kt"], ["ps_misc"])
                    self.CP("act", ktT, ktp, ["ps_misc"], ["ktT"])
                    aps = self.ps_ao.rearrange("p a t -> p (a t)").rearrange("p (h r t) -> p h r t", h=2, r=2)
                    for hh in range(2):
                        for pr in range(2):
                            self.MM(aps[:, hh, pr, :], kt[:, h0 + hh, pr * 128:(pr + 1) * 128], qt[:, h0 + hh, pr * 128:(pr + 1) * 128],
                                    True, True, ["kt", "qt"], ["ps_ao"])
                    pm = self.pmask4
                    self.OP("dve", (lambda h, o=Asb.rearrange("p h r t -> p (h r) t"), m_=pm, d_=aps.rearrange("p h r t -> p (h r) t"):
                                 h.copy_predicated(o, m_, d_)), ["ps_ao", "pmask", "Asb"], ["Asb"])
                    ob = self.ps_po[1].rearrange("p (h t) -> p h t", h=2)
                    firstmm = [True]

                    def omm(out, lhsT, rhs, R):
                        st = firstmm[0]
                        firstmm[0] = False
                        self.OP("pe", (lambda h, o=out, l_=lhsT, r_=rhs, st=st: h.matmul(o, l_, r_, start=st, stop=False, skip_group_check=True)),
                                R, ["po1"], 0.035 + self._fd(out) / 2100.0)
                    for ch in range(4):
                        pr, half = ch // 2, ch % 2
                        for hh in range(2):
                            hd = h0 + hh
                            ei = hd * 4 + ch
                            if half == 0:
                                omm(ob[:, hh, pr * 128:(pr + 1) * 128], vt[p][:, pr, hd * 128:(hd + 1) * 128], Asb[:, hh, pr, :],
                                    [("vt", p), "Asb"])
                            self.OP("act", (lambda h, o=Sb[:, hd, :], i_=S[:, hd, :], s_=esc[:, 0, ei:ei + 1]: h.mul(o, i_, s_)),
                                 [("S", hd), "esc"], [("Sb", hd)])
                            omm(ob[:, hh, ch * 64:(ch + 1) * 64], Sb[:, hd, :], qt[:, hd, ch * 64:(ch + 1) * 64], [("Sb", hd), "qt"])
                        for hh in range(2):
                            hd = h0 + hh
                            ei = hd * 4 + ch
                            up = self.ps_u[hh]
                            self.MM(up, ktT[half * 64:(half + 1) * 64, hh, pr, :],
                                    vt[p][half * 64:(half + 1) * 64, pr, hd * 128:(hd + 1) * 128],
                                    True, True, ["ktT", ("vt", p)], ["ps_misc"])
                            self.TS("dve", St[:, hd, :], S[:, hd, :], esc[:, 1, ei:ei + 1], None, ALU.mult, None,
                                    [("S", hd), "esc"], [("St", hd)])
                            self.STT("dve", S[:, hd, :], up, esc[:, 2, ei:ei + 1], St[:, hd, :], ALU.mult, ALU.add,
                                     ["ps_misc", ("St", hd), "esc"], [("S", hd)])
                    self.ACT(osq, ob, AF.Square, ["po1"], ["osq"])
                    self.CP("act", osb[:, h0:h0 + 2, :], ob, ["po1"], [("osb", hp)])
                    osp = self.ps_ao
                    for hh in range(2):
                        self.MM(osp[:, hh, :], self.ones, osq[:, hh, :], True, True, ["osq", "ones"], ["ps_ao"])
                    self.ACT(rs2, osp, AF.Ln, ["ps_ao"], ["rs2"], bias=self.epsb, scale=1.0 / 128.0)
                    self.ACT(rs2, rs2, AF.Exp, ["rs2"], ["rs2"], scale=-0.5)
                    for hh in range(2):
                        hd = h0 + hh
                        self.STT("dve", osb[:, hd, :], osb[:, hd, :], pv[:, pvb + E_HN + hd:pvb + E_HN + hd + 1], rs2[:, hh, :],
                                 ALU.mult, ALU.mult, [("osb", hp), "rs2", "pv"], [("osb", hp)])
                return f
            pcs.append(hg2(0))
            pcs.append(hg2(1))

            def fin():
                self.TT("dve", yb[:, 4:8, :], osb, sg[p], ALU.mult, [("osb", 0), ("osb", 1), ("sg", p)],
                        [("y", p, 4 + h_) for h_ in range(4)])
                self.out_proj(wout, slot, yb, p)
            pcs.append(fin)
            return pcs

        self.pipeline(src, dst, A, B)

    def build(self):
        sb = self.sb
        yf = self.y[0].rearrange("p k t -> p (k t)")
        self.sb_r32a = yf[:, 0:1024].bitcast(F32).rearrange("p (a t) -> p a t", t=T)
        self.sb_r32b = yf[:, 1024:2048].bitcast(F32).rearrange("p (a t) -> p a t", t=T)
        self.sb_st6 = sb("st6", [128, 4, 6])
        self.sb_mv = sb("mv", [128, 4, 2])
        self.sb_l8 = sb("l8", [128, 8])
        self.sb_r8 = sb("r8", [128, 8])
        self.sb_mv8 = sb("mv8", [128, 8, 2])
        self.sb_ec = sb("ec", [128, 40])
        self.sb_hst = sb("hst", [128, 4])
        self.sb_Sb = sb("Sb", [128, 4, 128], BF16)
        self.sb_ktT = sb("ktT", [128, 2, 2, 128], BF16)
        self.sb_Asb = sb("Asb", [128, 2, 2, 128], BF16)
        self.sb_sc = sb("sc", [128, 3, 16])
        self.sb_esc = sb("esc", [128, 3, 16])
        self.sb_S = sb("S", [128, 4, 128])
        self.sb_St = sb("St", [128, 4, 128])
        self.sb_osq = sb("osq", [128, 2, T], BF16)
        self.sb_rs2 = sb("rs2", [128, 2, T])
        self.segm4 = sb("segm4", [128, 4 * T])
        self.sb_bias = sb("biasc", [128, 4])
        self.ps_u = [self.ps_misc[:, 0:128], self.ps_misc[:, 128:256]]
        self.ps_ktT = self.ps_misc[:, 256:384].bitcast(BF16).rearrange("p (r t) -> p r t", t=128)
        self.out_keys = []
        self.w1_prefetched = {}
        self.next_ffn = None
        self.next_mix = None
        self.mix_prefetched = set()
        self.MS("dve", self.sb_bias[:, 0:1], EPS, ["biasc"])
        self.MS("dve", self.sb_bias[:, 1:2], 1.0, ["biasc"])
        self.MS("dve", self.sb_bias[:, 2:3], 1024.0 * EPS, ["biasc"])
        self.epsb = self.sb_bias[:, 0:1]
        self.oneb = self.sb_bias[:, 1:2]
        self.eps1024 = self.sb_bias[:, 2:3]
        self.setup()
        n = len(self.phases)
        for i, (kind, l) in enumerate(self.phases):
            src = self.xT if i == 0 else self.hbuf
            dst = self.outT if i == n - 1 else self.hbuf
            if i > 0:
                self.P.barrier()
                self.P.new_epoch()
            self.next_ffn = self.phases[i + 1][1] if (i + 1 < n and self.phases[i + 1][0] == "ffn" and kind != "ffn") else None
            self.next_mix = self.phases[i + 1][1] if (i + 1 < n and self.phases[i + 1][0] == "mix" and kind == "ffn") else None
            if kind == "ffn":
                self.phase_ffn(l, src, dst, last=(i == n - 1))
            elif l % 2 == 0:
                self.phase_even(l, src, dst)
            else:
                self.phase_odd(l, src, dst)
        self.P.fence("sp", (), self.out_keys)
        with self.nc.allow_low_precision("bf16 matmul operands, fp32 accumulation"):
            self.P.emit()
        return self.nc


ALL_PHASES = [("mix", 0), ("ffn", 0), ("mix", 1), ("ffn", 1), ("mix", 2), ("ffn", 2), ("mix", 3), ("ffn", 3)]


def prep_shared(inp):
    f = lambda a: np.ascontiguousarray(np.asarray(a, np.float32))
    return {
        "w_in_even": f(inp["w_in_even"]), "w_out_even": f(inp["w_out_even"]),
        "w_in_odd": f(inp["w_in_odd"]), "w_out_odd": f(inp["w_out_odd"]),
        "w_ffn_in": f(inp["w_ffn_in"]), "w_ffn_out": f(inp["w_ffn_out"]),
        "wgate": pack_gates(inp),
        "sguwT": f(np.transpose(np.asarray(inp["sgu_w"], np.float32), (0, 1, 3, 2))),
        "sgub": f(np.asarray(inp["sgu_b"], np.float32).reshape(2, 1, 512)),
        "poolw": f(inp["pool_w"]),
        "pvec": pack_pvec(inp),
        "cst": const_table(),
    }


def run_phases(xT_list, shared, phases, final_norm, seqlen):
    ntok = xT_list[0].shape[1]
    b = Builder(ntok, seqlen, phases, final_norm)
    nc = b.build()
    in_maps = []
    for xT in xT_list:
        m = dict(shared)
        m["xT"] = np.ascontiguousarray(xT, dtype=np.float32)
        in_maps.append(m)
    res = run_bass_kernel_spmd(nc, in_maps, core_ids=list(range(len(xT_list))))
    return [r["outT"] for r in res.results]


def kernel(**inputs):
    x = np.asarray(inputs["x"], np.float32)
    B, S, _ = x.shape
    per = B // NCORES
    shared = prep_shared(inputs)
    xT_list = [np.ascontiguousarray(x[c * per:(c + 1) * per].reshape(per * S, D).T) for c in range(NCORES)]
    outs = run_phases(xT_list, shared, ALL_PHASES, True, S)
    out = np.empty((B, S, D), np.float32)
    for c in range(NCORES):
        out[c * per:(c + 1) * per] = outs[c].T.reshape(per, S, D)
    return out
```
